# Optimizing a Trainium2 kernel written in Bass

```python
import math
import jax, jax.numpy as jnp
from jax import lax
import numpy as np

D_MODEL = 1024
BATCH = 1
SEQ = 16384
DEPTH = 4

GRID_W = 64
CTX_LEN = 256
ROPE_DIM = 64
ROPE_BASE = 10000.0
EPS = 1e-6
HG_HEADS = 4
HG_DIM = 128
HG_W = HG_HEADS * HG_DIM
MLA_HEADS = 4
MLA_NOPE = 128
MLA_ROPE = ROPE_DIM
MLA_V = 128
MLA_Q_RANK = 384
MLA_KV_RANK = 256
DA_HEADS = 4
DA_DIM = ROPE_DIM
DA_V = 2 * DA_DIM
RT_HEADS = 4
RT_K = ROPE_DIM
RT_V = 128
MLP_HIDDEN = 4 * D_MODEL
CHUNK = 64
Q_BLOCK = 128
ADA_STD = 0.5
N_EVEN = (DEPTH + 1) // 2
N_ODD = DEPTH // 2
A_SPLITS = (HG_W, HG_W, HG_W, HG_W, HG_W, MLA_Q_RANK, MLA_KV_RANK, MLA_ROPE)
C_SPLITS = (DA_HEADS * 2 * DA_DIM, DA_HEADS * 2 * DA_DIM, DA_HEADS * DA_V,
            RT_HEADS * RT_K, RT_HEADS * RT_K, RT_HEADS * RT_V, RT_HEADS * RT_V)
A_IN = sum(A_SPLITS)
C_IN = sum(C_SPLITS)
MIX_W = HG_W + MLA_HEADS * MLA_V

kernel_name = "hybrid_hgrn2_mla_diffattn_retnet_dit"


def _split(p, sizes):
    return jnp.split(p, np.cumsum(sizes)[:-1].tolist(), axis=-1)


def _rms(x, w):
    xf = x.astype(jnp.float32)
    y = xf * lax.rsqrt(jnp.mean(xf * xf, axis=-1, keepdims=True) + EPS)
    return y.astype(x.dtype) * w


def _heads(a, h):
    return a.reshape(a.shape[0], a.shape[1], h, -1)


def _rope_tables(n, dim):
    rows = n // GRID_W
    row = jnp.repeat(jnp.arange(rows, dtype=jnp.float32), GRID_W)
    col = jnp.tile(jnp.arange(GRID_W, dtype=jnp.float32), rows)
    quarter = dim // 4
    inv_freq = ROPE_BASE ** (-jnp.arange(quarter, dtype=jnp.float32) / quarter)
    ang_r = row[:, None] * inv_freq
    ang_c = col[:, None] * inv_freq
    ang = jnp.concatenate([ang_r, ang_r, ang_c, ang_c], axis=-1)
    return jnp.cos(ang), jnp.sin(ang)


def _apply_rope(x, cos, sin):
    bshape = (1, x.shape[1]) + (1,) * (x.ndim - 3) + (x.shape[-1],)
    cs = cos.reshape(bshape).astype(x.dtype)
    sn = sin.reshape(bshape).astype(x.dtype)
    xr = x.reshape(x.shape[:-1] + (2, 2, x.shape[-1] // 4))
    rot = jnp.stack([-xr[..., 1, :], xr[..., 0, :]], axis=-2).reshape(x.shape)
    return x * cs + rot * sn


def _softmax_attend(q, k, v, scale):
    s = jnp.einsum('bqhd,bkhd->bhqk', q, k).astype(jnp.float32) * scale
    p = jax.nn.softmax(s, axis=-1).astype(v.dtype)
    return jnp.einsum('bhqk,bkhd->bqhd', p, v)


def _diff_attend(q, k, v, lam, scale):
    s = jnp.einsum('bqhcd,bkhcd->bhcqk', q, k).astype(jnp.float32) * scale
    p = jax.nn.softmax(s, axis=-1)
    a = (p[:, :, 0] - lam * p[:, :, 1]).astype(v.dtype)
    return jnp.einsum('bhqk,bkhd->bqhd', a, v)


def _sweep_queries(q, attend):
    b, n = q.shape[:2]
    qb = q.reshape((b, n // Q_BLOCK, Q_BLOCK) + q.shape[2:]).swapaxes(0, 1)
    out = lax.map(attend, qb)
    return out.swapaxes(0, 1).reshape((b, n) + out.shape[3:])


def _chunk_scan(q, k, v, logf, s0):
    b, t, h, _ = q.shape
    nc = t // CHUNK

    def to_chunks(a):
        return a.reshape(b, nc, CHUNK, h, a.shape[-1]).transpose(1, 0, 3, 2, 4)

    qc, kc, vc, gc = to_chunks(q), to_chunks(k), to_chunks(v), to_chunks(logf)
    mask = jnp.tril(jnp.ones((CHUNK, CHUNK), dtype=bool))[:, :, None]

    def step(s, inp):
        qi, ki, vi, gi = inp
        bcum = jnp.cumsum(gi.astype(jnp.float32), axis=2)
        diff = bcum[:, :, :, None, :] - bcum[:, :, None, :, :]
        dec = jnp.exp(jnp.where(mask, diff, -jnp.inf))
        a = jnp.sum(qi[:, :, :, None, :] * dec * ki[:, :, None, :, :], axis=-1)
        o = (jnp.einsum('bhts,bhsv->bhtv', a, vi)
             + jnp.einsum('bhtk,bhkv->bhtv', qi * jnp.exp(bcum), s))
        last = bcum[:, :, -1:, :]
        s_new = (jnp.exp(last[:, :, 0, :])[..., None] * s
                 + jnp.einsum('bhsk,bhsv->bhkv', ki * jnp.exp(last - bcum), vi))
        return s_new, o

    s_fin, oc = lax.scan(step, s0, (qc, kc, vc, gc))
    o = oc.transpose(1, 0, 3, 2, 4).reshape(b, t, h, -1)
    return o, s_fin


def _final_state(k, v, logf):
    bcum = jnp.cumsum(logf.astype(jnp.float32), axis=1)
    w = jnp.exp(bcum[:, -1:] - bcum)
    return jnp.einsum('bthk,bthv->bhkv', k * w, v)


def _bidir_recurrence(q_l, v_l, kg_l, q_c, v_c, kg_c, need_ctx):
    outs_l, outs_c = [], []
    for d in range(2):
        fl = (lambda a: a[:, ::-1]) if d == 1 else (lambda a: a)
        (k_l, g_l), (k_c, g_c) = kg_l[d], kg_c[d]
        if need_ctx:
            s0 = jnp.zeros((v_c.shape[0], v_c.shape[2], k_c.shape[-1], v_c.shape[-1]), jnp.float32)
            o_c, s_ctx = _chunk_scan(fl(q_c), fl(k_c), fl(v_c), fl(g_c), s0)
            outs_c.append(fl(o_c))
        else:
            s_ctx = _final_state(fl(k_c), fl(v_c), fl(g_c))
        o_l, _ = _chunk_scan(fl(q_l), fl(k_l), fl(v_l), fl(g_l), s_ctx)
        outs_l.append(fl(o_l))
    y_l = (outs_l[0] + outs_l[1]).astype(v_l.dtype)
    y_c = (outs_c[0] + outs_c[1]).astype(v_c.dtype) if need_ctx else None
    return y_l, y_c


def _hgrn2_gate(p_f, lb):
    z = p_f.astype(jnp.float32)
    logf = jnp.logaddexp(jnp.log(lb), jnp.log1p(-lb) + jax.nn.log_sigmoid(z))
    logf = _heads(logf, HG_HEADS)
    return (-jnp.expm1(logf)).astype(p_f.dtype), logf


def _even_mixer(h_l, h_c, cos, sin, lb, w_in, hg_norm, q_norm, kv_norm, w_uq, w_ukv,
                qk_q, qk_k, need_ctx):
    b, n, _ = h_l.shape
    m = h_c.shape[1]
    pl = _split(h_l @ w_in, A_SPLITS)
    pc = _split(h_c @ w_in, A_SPLITS)

    kg_l = tuple(_hgrn2_gate(pl[1 + d], lb[d]) for d in range(2))
    kg_c = tuple(_hgrn2_gate(pc[1 + d], lb[d]) for d in range(2))
    hq_l = _heads(jax.nn.silu(pl[0]), HG_HEADS)
    hq_c = _heads(jax.nn.silu(pc[0]), HG_HEADS) if need_ctx else None
    o_l, o_c = _bidir_recurrence(hq_l, _heads(pl[3], HG_HEADS), kg_l,
                                 hq_c, _heads(pc[3], HG_HEADS), kg_c, need_ctx)
    hg_l = _rms(o_l, hg_norm).reshape(b, n, HG_W) * jax.nn.silu(pl[4])

    def mla_kv(p):
        kv = _heads(_rms(p[6], kv_norm) @ w_ukv, MLA_HEADS)
        k_nope, v = kv[..., :MLA_NOPE], kv[..., MLA_NOPE:]
        k_rope = jnp.broadcast_to(p[7][:, :, None, :], k_nope.shape[:3] + (MLA_ROPE,))
        return _rms(jnp.concatenate([k_nope, k_rope], axis=-1), qk_k), v

    def mla_q(p):
        return _rms(_heads(_rms(p[5], q_norm) @ w_uq, MLA_HEADS), qk_q)

    def rope_tail(a):
        return jnp.concatenate([a[..., :MLA_NOPE], _apply_rope(a[..., MLA_NOPE:], cos, sin)], axis=-1)

    scale = (MLA_NOPE + MLA_ROPE) ** -0.5
    k_c, v_c = mla_kv(pc)
    k_l, v_l = mla_kv(pl)
    k_all = jnp.concatenate([k_c, rope_tail(k_l)], axis=1)
    v_all = jnp.concatenate([v_c, v_l], axis=1)
    a_l = _sweep_queries(rope_tail(mla_q(pl)), lambda qb: _softmax_attend(qb, k_all, v_all, scale))
    y_l = jnp.concatenate([hg_l, a_l.reshape(b, n, MLA_HEADS * MLA_V)], axis=-1)

    if need_ctx:
        hg_c = _rms(o_c, hg_norm).reshape(b, m, HG_W) * jax.nn.silu(pc[4])
        a_c = _softmax_attend(mla_q(pc), k_c, v_c, scale)
        y_c = jnp.concatenate([hg_c, a_c.reshape(b, m, MLA_HEADS * MLA_V)], axis=-1)
    else:
        y_c = None
    return y_l, y_c


def _odd_mixer(h_l, h_c, cos, sin, layer, w_in, lam, qk_q, qk_k, subln, rt_decay, rt_norm, need_ctx):
    b, n, _ = h_l.shape
    m = h_c.shape[1]
    pl = _split(h_l @ w_in, C_SPLITS)
    pc = _split(h_c @ w_in, C_SPLITS)

    lam_init = 0.8 - 0.6 * math.exp(-0.3 * layer)
    lf = lam.astype(jnp.float32)
    lam_full = jnp.exp(jnp.sum(lf[0] * lf[1])) - jnp.exp(jnp.sum(lf[2] * lf[3])) + lam_init

    def sub(a):
        return a.reshape(a.shape[0], a.shape[1], DA_HEADS, 2, DA_DIM)

    scale = DA_DIM ** -0.5
    dk_c = _rms(sub(pc[1]), qk_k)
    dv_c = _heads(pc[2], DA_HEADS)
    dk_l = _apply_rope(_rms(sub(pl[1]), qk_k), cos, sin)
    dq_l = _apply_rope(_rms(sub(pl[0]), qk_q), cos, sin)
    k_all = jnp.concatenate([dk_c, dk_l], axis=1)
    v_all = jnp.concatenate([dv_c, _heads(pl[2], DA_HEADS)], axis=1)
    d_l = _sweep_queries(dq_l, lambda qb: _diff_attend(qb, k_all, v_all, lam_full, scale))
    d_l = (_rms(d_l, subln) * (1.0 - lam_init)).reshape(b, n, DA_HEADS * DA_V)

    log_gamma = jax.nn.log_sigmoid(rt_decay.astype(jnp.float32))

    def rt_kg(k):
        shp = k.shape[:3] + (1,)
        return tuple((k, jnp.broadcast_to(log_gamma[d][:, None], shp)) for d in range(2))

    rq_l = _apply_rope(_heads(pl[3], RT_HEADS), cos, sin)
    rk_l = _apply_rope(_heads(pl[4], RT_HEADS) * RT_K ** -0.5, cos, sin)
    rk_c = _heads(pc[4], RT_HEADS) * RT_K ** -0.5
    rq_c = _heads(pc[3], RT_HEADS) if need_ctx else None
    r_l, r_c = _bidir_recurrence(rq_l, _heads(pl[5], RT_HEADS), rt_kg(rk_l),
                                 rq_c, _heads(pc[5], RT_HEADS), rt_kg(rk_c), need_ctx)
    r_l = _rms(r_l, rt_norm).reshape(b, n, RT_HEADS * RT_V) * jax.nn.silu(pl[6])
    y_l = jnp.concatenate([d_l, r_l], axis=-1)

    if need_ctx:
        d_c = _diff_attend(_rms(sub(pc[0]), qk_q), dk_c, dv_c, lam_full, scale)
        d_c = (_rms(d_c, subln) * (1.0 - lam_init)).reshape(b, m, DA_HEADS * DA_V)
        r_c = _rms(r_c, rt_norm).reshape(b, m, RT_HEADS * RT_V) * jax.nn.silu(pc[6])
        y_c = jnp.concatenate([d_c, r_c], axis=-1)
    else:
        y_c = None
    return y_l, y_c


def _mlp(h, w1, w2):
    return jnp.square(jax.nn.relu(h @ w1)) @ w2


def setup_inputs(seed: int = 0) -> dict:
    key = jax.random.key(seed)
    k = jax.random.split(key, 26)
    f32 = jnp.float32

    def nrm(kk, shape, std):
        return jax.random.normal(kk, shape, f32) * std

    def gain(kk, shape):
        return 1.0 + 0.02 * jax.random.normal(kk, shape, f32)

    rt_base = jnp.log(2.0 ** (5.0 + jnp.arange(RT_HEADS, dtype=f32)) - 1.0)
    return {
        "x": nrm(k[0], (BATCH, SEQ, D_MODEL), 1.0),
        "c": nrm(k[1], (BATCH, D_MODEL), 1.0),
        "ctx": nrm(k[2], (BATCH, CTX_LEN, D_MODEL), 1.0),
        "c_ctx": nrm(k[3], (D_MODEL,), 1.0),
        "ada_w": nrm(k[4], (DEPTH, D_MODEL, 6 * D_MODEL), ADA_STD * D_MODEL ** -0.5),
        "ada_b": nrm(k[5], (DEPTH, 6 * D_MODEL), 0.01),
        "norm_w": gain(k[6], (DEPTH, 2, D_MODEL)),
        "w_o": nrm(k[7], (DEPTH, MIX_W, D_MODEL), MIX_W ** -0.5),
        "mlp_w1": nrm(k[8], (DEPTH, D_MODEL, MLP_HIDDEN), D_MODEL ** -0.5),
        "mlp_w2": nrm(k[9], (DEPTH, MLP_HIDDEN, D_MODEL), MLP_HIDDEN ** -0.5),
        "a_w_in": nrm(k[10], (N_EVEN, D_MODEL, A_IN), D_MODEL ** -0.5),
        "hg_lb": nrm(k[11], (N_EVEN, 2, HG_W), 0.5),
        "hg_norm": gain(k[12], (N_EVEN, HG_DIM)),
        "mla_q_norm": gain(k[13], (N_EVEN, MLA_Q_RANK)),
        "mla_kv_norm": gain(k[14], (N_EVEN, MLA_KV_RANK)),
        "mla_w_uq": nrm(k[15], (N_EVEN, MLA_Q_RANK, MLA_HEADS * (MLA_NOPE + MLA_ROPE)), MLA_Q_RANK ** -0.5),
        "mla_w_ukv": nrm(k[16], (N_EVEN, MLA_KV_RANK, MLA_HEADS * (MLA_NOPE + MLA_V)), MLA_KV_RANK ** -0.5),
        "mla_qk_q": gain(k[17], (N_EVEN, MLA_NOPE + MLA_ROPE)),
        "mla_qk_k": gain(k[18], (N_EVEN, MLA_NOPE + MLA_ROPE)),
        "c_w_in": nrm(k[19], (N_ODD, D_MODEL, C_IN), D_MODEL ** -0.5),
        "da_lambda": nrm(k[20], (N_ODD, 4, DA_DIM), 0.1),
        "da_qk_q": gain(k[21], (N_ODD, DA_DIM)),
        "da_qk_k": gain(k[22], (N_ODD, DA_DIM)),
        "da_subln": gain(k[23], (N_ODD, DA_V)),
        "rt_decay": rt_base + nrm(k[24], (N_ODD, 2, RT_HEADS), 0.01),
        "rt_norm": gain(k[25], (N_ODD, RT_V)),
    }


def reference(x, c, ctx, c_ctx, ada_w, ada_b, norm_w, w_o, mlp_w1, mlp_w2,
              a_w_in, hg_lb, hg_norm, mla_q_norm, mla_kv_norm, mla_w_uq, mla_w_ukv, mla_qk_q, mla_qk_k,
              c_w_in, da_lambda, da_qk_q, da_qk_k, da_subln, rt_decay, rt_norm):
    n = x.shape[1]
    cos, sin = _rope_tables(n, ROPE_DIM)
    lb = jnp.cumsum(jax.nn.softmax(hg_lb.astype(jnp.float32), axis=0), axis=0)
    lb = lb - lb[:1]
    xc = ctx
    for l in range(DEPTH):
        last = l == DEPTH - 1
        mod_l = (jax.nn.silu(c) @ ada_w[l] + ada_b[l])[:, None, :]
        mod_c = jax.nn.silu(c_ctx) @ ada_w[l] + ada_b[l]
        sh1, sc1, g1, sh2, sc2, g2 = jnp.split(mod_l, 6, axis=-1)
        csh1, csc1, cg1, csh2, csc2, cg2 = jnp.split(mod_c, 6, axis=-1)
        h_l = _rms(x, norm_w[l, 0]) * (1 + sc1) + sh1
        h_c = _rms(xc, norm_w[l, 0]) * (1 + csc1) + csh1
        j = l // 2
        if l % 2 == 0:
            y_l, y_c = _even_mixer(h_l, h_c, cos, sin, lb[j], a_w_in[j], hg_norm[j], mla_q_norm[j],
                                   mla_kv_norm[j], mla_w_uq[j], mla_w_ukv[j], mla_qk_q[j], mla_qk_k[j],
                                   not last)
        else:
            y_l, y_c = _odd_mixer(h_l, h_c, cos, sin, l, c_w_in[j], da_lambda[j], da_qk_q[j], da_qk_k[j],
                                  da_subln[j], rt_decay[j], rt_norm[j], not last)
        x = x + g1 * (y_l @ w_o[l])
        x = x + g2 * _mlp(_rms(x, norm_w[l, 1]) * (1 + sc2) + sh2, mlp_w1[l], mlp_w2[l])
        if not last:
            xc = xc + cg1 * (y_c @ w_o[l])
            xc = xc + cg2 * _mlp(_rms(xc, norm_w[l, 1]) * (1 + csc2) + csh2, mlp_w1[l], mlp_w2[l])
    return x
```

```python
import contextlib
import math
import numpy as np
import concourse.bass as bass
import concourse.mybir as mybir
from concourse.bass_utils import run_bass_kernel_spmd

F32 = mybir.dt.float32
BF16 = mybir.dt.bfloat16
AF = mybir.ActivationFunctionType
ALU = mybir.AluOpType

NCORES = 8
DBG = 0
D = 1024
NLAT = 16384
NCTX = 256
LPC = NLAT // NCORES
TPC = NCTX + LPC
TALL = NCTX + NLAT
NKB = TALL // 128
EPS = 1e-6
TILES = [(0, 256, 1), (256, 512, 0), (768, 512, 0), (1280, 512, 0), (1792, 512, 0)]


class Buf:
    __slots__ = ("lw", "rd")

    def __init__(self):
        self.lw = None
        self.rd = []


class Eng:
    def __init__(self, eng, sem):
        self.eng, self.sem = eng, sem
        self.count = 0
        self.seen = {}


class K:
    def __init__(self, nc, sems):
        self.nc = nc
        self.E = {"pe": Eng(nc.tensor, sems[0]), "dve": Eng(nc.vector, sems[1]),
                  "act": Eng(nc.scalar, sems[2]), "pool": Eng(nc.gpsimd, sems[3])}
        self.dmaq = Eng(nc.sync, None)
        self.dsems = sems[4:]
        self.dcnt = [0] * len(self.dsems)
        self.dnext = 0
        self.n = 0

    def _deps(self, e, reads, writes):
        deps = {}
        for b in reads:
            if b.lw is not None:
                k, c = b.lw
                if deps.get(k, 0) < c:
                    deps[k] = c
        for b in writes:
            if b.lw is not None:
                k, c = b.lw
                if deps.get(k, 0) < c:
                    deps[k] = c
            for k, c in b.rd:
                if deps.get(k, 0) < c:
                    deps[k] = c
        for k, c in deps.items():
            if e.seen.get(k, 0) >= c:
                continue
            e.seen[k] = c
            e.eng.wait_ge(k, c)

    def _mark(self, key, reads, writes):
        for b in reads:
            b.rd.append(key)
            if len(b.rd) > 24:
                m = {}
                for k, c in b.rd:
                    if m.get(k, 0) < c:
                        m[k] = c
                b.rd = list(m.items())
        for b in writes:
            b.lw = key
            b.rd = []

    def op(self, en, fn, reads=(), writes=()):
        e = self.E[en]
        self._deps(e, reads, writes)
        ins = fn(e.eng)
        e.count += 1
        ins.then_inc(e.sem, 1)
        self._mark((e.sem, e.count), reads, writes)
        self.n += 1

    def dma(self, out, in_, reads=(), writes=()):
        e = self.dmaq
        i = self.dnext
        self.dnext = (i + 1) % len(self.dsems)
        s = self.dsems[i]
        if self.dcnt[i] > 0 and e.seen.get(s, 0) < self.dcnt[i]:
            e.eng.wait_ge(s, self.dcnt[i])
            e.seen[s] = self.dcnt[i]
        self._deps(e, reads, writes)
        ins = e.eng.dma_start(out=out, in_=in_)
        self.dcnt[i] += 16
        ins.then_inc(s, 16)
        self._mark((s, self.dcnt[i]), reads, writes)
        self.n += 1

    def finish(self, bufs):
        self._deps(self.dmaq, bufs, bufs)


class T:
    def __init__(self, t):
        self.t = t
        self._b = {}

    def b(self, key=0):
        if key not in self._b:
            self._b[key] = Buf()
        return self._b[key]

    def __getitem__(self, idx):
        return self.t[idx]


def fm(v):
    v = np.asarray(v, np.float32)
    return np.ascontiguousarray(v.reshape(-1, 128).T)


def pad128(v):
    o = np.zeros(128, np.float32)
    o[: v.shape[0]] = v
    return o[:, None]


class VecPack:
    def __init__(self):
        self.cols = []
        self.off = {}
        self.n = 0

    def add(self, name, arr):
        arr = np.asarray(arr, np.float32)
        assert arr.shape[0] == 128
        self.off[name] = self.n
        self.cols.append(arr)
        self.n += arr.shape[1]

    def pack(self):
        return np.ascontiguousarray(np.concatenate(self.cols, axis=1))


def vec_layout(even):
    off = {}
    n = 0

    def a(name, w):
        nonlocal n
        off[name] = n
        n += w
    a("nw1", 8); a("nw2", 8); a("adab", 48); a("c", 8); a("cctx", 8)
    a("flf", 8); a("nflf", 8); a("flb", 8); a("nflb", 8); a("eps", 1); a("one", 1); a("rmask", 4)
    if even:
        a("lb0", 8); a("lb1", 8); a("hgn", 1); a("qn", 3); a("kvn", 2)
        a("qkq_n", 1); a("qkq_r", 1); a("qkk_n", 1); a("qkk_r", 1); a("lbsel", 1)
    else:
        a("daq", 1); a("dak", 1); a("subln", 1); a("rtn", 1); a("rtd", 4); a("lam", 4)
        a("laminit", 1); a("omlam", 1)
    return off, n


def build_vecs(inp, l, core):
    even = l % 2 == 0
    j = l // 2
    vp = VecPack()
    vp.add("nw1", fm(inp["norm_w"][l, 0])); vp.add("nw2", fm(inp["norm_w"][l, 1]))
    vp.add("adab", fm(inp["ada_b"][l])); vp.add("c", fm(inp["c"][0])); vp.add("cctx", fm(inp["c_ctx"]))
    flf = np.zeros((128, 8), np.float32); flb = np.zeros((128, 8), np.float32)
    flf[:, :core] = 1.0
    flb[:, core + 1:] = 1.0
    vp.add("flf", flf); vp.add("nflf", 1.0 - flf); vp.add("flb", flb); vp.add("nflb", 1.0 - flb)
    vp.add("eps", np.full((128, 1), EPS, np.float32)); vp.add("one", np.ones((128, 1), np.float32))
    rm = np.zeros((128, 4), np.float32)
    for cc in range(4):
        rm[cc * 32:(cc + 1) * 32, cc] = 1.0
    vp.add("rmask", rm)
    if even:
        vp.add("lb0", fm(inp["hg_lb"][0].reshape(-1))); vp.add("lb1", fm(inp["hg_lb"][1].reshape(-1)))
        vp.add("hgn", fm(inp["hg_norm"][j])); vp.add("qn", fm(inp["mla_q_norm"][j])); vp.add("kvn", fm(inp["mla_kv_norm"][j]))
        vp.add("qkq_n", fm(inp["mla_qk_q"][j][:128])); vp.add("qkq_r", pad128(inp["mla_qk_q"][j][128:]))
        vp.add("qkk_n", fm(inp["mla_qk_k"][j][:128])); vp.add("qkk_r", pad128(inp["mla_qk_k"][j][128:]))
        vp.add("lbsel", np.full((128, 1), float(j), np.float32))
    else:
        vp.add("daq", fm(np.tile(inp["da_qk_q"][j], 2))); vp.add("dak", fm(np.tile(inp["da_qk_k"][j], 2)))
        vp.add("subln", fm(inp["da_subln"][j])); vp.add("rtn", fm(inp["rt_norm"][j]))
        rtd = np.zeros((128, 4), np.float32)
        for d in range(2):
            for c in range(2):
                rtd[:64, d * 2 + c] = inp["rt_decay"][j, d, 2 * c]
                rtd[64:, d * 2 + c] = inp["rt_decay"][j, d, 2 * c + 1]
        vp.add("rtd", rtd)
        lam = np.zeros((128, 4), np.float32)
        lam[:64, :] = inp["da_lambda"][j].T
        vp.add("lam", lam)
        li = 0.8 - 0.6 * math.exp(-0.3 * l)
        vp.add("laminit", np.full((128, 1), li, np.float32)); vp.add("omlam", np.full((128, 1), 1.0 - li, np.float32))
    off, n = vec_layout(even)
    assert off == vp.off and n == vp.n
    return vp.pack()


def const_tables(core):
    pos = np.arange(LPC) + core * LPC
    row = (pos // 64).astype(np.float32)
    col = (pos % 64).astype(np.float32)
    inv = (10000.0 ** (-np.arange(16, dtype=np.float32) / 16)).astype(np.float32)
    ang = np.concatenate([row[:, None] * inv, row[:, None] * inv, col[:, None] * inv, col[:, None] * inv], axis=1)
    cos = np.ones((128, TPC), np.float32); sin = np.zeros((128, TPC), np.float32)
    cos[:64, NCTX:] = np.cos(ang).T.astype(np.float32); cos[64:, NCTX:] = cos[:64, NCTX:]
    sin[:64, NCTX:] = np.sin(ang).T.astype(np.float32); sin[64:, NCTX:] = sin[:64, NCTX:]
    R = np.zeros((128, 128), np.float32)
    for base in (0, 32, 64, 96):
        for i in range(16):
            R[base + i + 16, base + i] = -1.0
            R[base + i, base + i + 16] = 1.0
    s = np.arange(128)[:, None]; t = np.arange(128)[None, :]
    same = (s // 32) == (t // 32)
    mf = (same & (s <= t)).astype(np.float32)
    mb = (same & (s >= t)).astype(np.float32)
    ones = np.ones((128, 128), np.float32)
    bd = ((s // 64) == (t // 64)).astype(np.float32)
    ident = np.eye(128, dtype=np.float32)
    return {"cos": cos, "sin": sin, "cmat": np.ascontiguousarray(np.stack([R, mf, mb, ones, bd, ident], 0))}


class Prog:
    def __init__(self, even, phase):
        self.even, self.phase = even, phase
        self.nc = bass.Bass("TRN2", target_bir_lowering=False)
        self.st = contextlib.ExitStack()
        self.off, self.nv = vec_layout(even)

    def din(self, name, shape, dt=F32):
        return self.nc.dram_tensor(name, list(shape), dt, kind="ExternalInput").ap()

    def dout(self, name, shape, dt=F32):
        return self.nc.dram_tensor(name, list(shape), dt, kind="ExternalOutput").ap()

    def dscr(self, name, shape, dt=BF16):
        return self.nc.dram_tensor(name, list(shape), dt, kind="Internal").ap()

    def sb(self, name, shape, dt=F32):
        return T(self.st.enter_context(self.nc.sbuf_tensor(name, list(shape), dt)))

    def ps(self, name, shape, dt=F32):
        return T(self.st.enter_context(self.nc.psum_tensor(name, list(shape), dt)))

    def V(self, name, i=0, n=1, rows=128):
        o = self.off[name] + i
        return self.vec[0:rows, o:o + n]


def build(even, phase):
    P = Prog(even, phase)
    nc = P.nc
    k = None
    WIN = 3264 if even else 3072
    xT_in = P.din("xT", [D, TPC])
    vec_in = P.din("vecs", [128, P.nv])
    cos_in = P.din("cos", [128, TPC]); sin_in = P.din("sin", [128, TPC]); cm_in = P.din("cmat", [6, 128, 128])
    ada_in = P.din("ada_w", [D, 6 * D])
    win_in = P.din("w_in", [D, WIN])
    if even:
        wukv_in = P.din("w_ukv", [256, 1024])
    if phase == "A":
        if even:
            kn_out = P.dout("kn", [4, 128, TPC], BF16); kr_out = P.dout("kr", [4, 64, TPC], BF16)
        else:
            kn_out = P.dout("kn", [4, 128, TPC], BF16)
        v_out = P.dout("v", [TPC // 128, 128, 512], BF16)
        sums_out = P.dout("sumS", [2, 4, 128, 128]); sumd_out = P.dout("sumD", [128, 8])
    else:
        xT_out = P.dout("xTo", [D, TPC])
        kn_in = P.din("kn_all", [4, 128, TALL], BF16)
        if even:
            kr_in = P.din("kr_all", [4, 64, TALL], BF16)
            wuq_in = P.din("w_uq", [384, 768])
        v_in = P.din("v_all", [4, 128, NKB, 128], BF16)
        sums_in = P.din("sumS_all", [NCORES, 2, 4, 128, 128]); sumd_in = P.din("sumD_all", [128, NCORES * 8])
        wo_in = P.din("w_o", [D, D]); w1_in = P.din("w1", [D, 4 * D]); w2_in = P.din("w2", [4 * D, D])
        wo_bf = P.dscr("wo_bf", [D, D]); w1_bf = P.dscr("w1_bf", [D, 4 * D]); w2_bf = P.dscr("w2_bf", [4 * D, D])
        if even:
            wuq_bf = P.dscr("wuq_bf", [384, 768])
    win_bf = P.dscr("win_bf", [D, WIN])
    if even:
        wukv_bf = P.dscr("wukv_bf", [256, 1024])

    st = P.st
    with st:
        sems = [st.enter_context(nc.semaphore(f"s{i}")) for i in range(4 + 10)]
        k = K(nc, sems)
        op, dma = k.op, k.dma
        P.vec = None
        vec = P.sb("vec", [128, P.nv]); P.vec = vec.t
        cosT = P.sb("cosT", [128, 512]); sinT = P.sb("sinT", [128, 512])
        cmf = P.sb("cmf", [128, 6, 128]); cmb = P.sb("cmb", [128, 6, 128], BF16)
        dma(vec[:], vec_in, writes=[vec.b()])
        dma(cmf[:], cm_in.rearrange("c p d -> p c d"), writes=[cmf.b()])
        op("dve", lambda e: e.tensor_copy(out=cmb[:], in_=cmf[:]), [cmf.b()], [cmb.b()])
        RMf = cmf[:, 0, :]
        MASK = {0: cmf[:, 1, :], 1: cmf[:, 2, :]}
        ONESb = cmb[:, 3, :]; BDb = cmb[:, 4, :]; IDb = cmb[:, 5, :]; ONESf = cmf[:, 3, :]
        CB = [vec.b(), cmf.b(), cmb.b(), cosT.b(), sinT.b()]

        PS = [P.ps(f"ps{i}", [128, 512]) for i in range(7)]
        PSB = P.ps("psb", [128, 1024], BF16)

        stg = [P.sb(f"stg{i}", [128, 1024]) for i in range(2)]
        stgb = [P.sb(f"stgb{i}", [128, 1024], BF16) for i in range(2)]
        prep_n = [0]
        WB = {}

        def prep(src, dst, rows, cols, name, col_ranges=None):
            WB[name] = Buf()
            for r0 in range(0, rows, 128):
                rr = min(128, rows - r0)
                pieces = []
                for (a0, an) in (col_ranges or [(0, cols)]):
                    for c in range(a0, a0 + an, 1024):
                        pieces.append((c, min(1024, a0 + an - c)))
                for (c0, cn) in pieces:
                    i = prep_n[0] % 2
                    prep_n[0] += 1
                    dma(stg[i][0:rr, 0:cn], src[r0:r0 + rr, c0:c0 + cn], writes=[stg[i].b()])
                    op("pool" if i else "dve", lambda e, i=i, rr=rr, cn=cn: e.tensor_copy(out=stgb[i][0:rr, 0:cn], in_=stg[i][0:rr, 0:cn]),
                       [stg[i].b()], [stgb[i].b()])
                    dma(dst[r0:r0 + rr, c0:c0 + cn], stgb[i][0:rr, 0:cn], reads=[stgb[i].b()], writes=[WB[name]])

        if phase == "A":
            if even:
                prep(win_in, win_bf, D, WIN, "win", [(512, 1536), (2944, 320)])
                prep(wukv_in, wukv_bf, 256, 1024, "wukv")
            else:
                prep(win_in, win_bf, D, WIN, "win", [(512, 1024), (1792, 768)])
        else:
            if even:
                prep(win_in, win_bf, D, WIN, "win", [(0, 2048), (2048, 896)])
                prep(wuq_in, wuq_bf, 384, 768, "wuq")
            else:
                prep(win_in, win_bf, D, WIN, "win", [(0, 512), (1536, 1536)])
            prep(wo_in, wo_bf, D, D, "wo"); prep(w1_in, w1_bf, D, 4 * D, "w1"); prep(w2_in, w2_bf, 4 * D, D, "w2")

        scT = P.sb("scT", [128, 8, 2]); MOD = P.sb("MOD", [128, 48, 2])
        for kc in range(8):
            op("act", lambda e, kc=kc: e.activation(out=scT[:, kc, 0:1], in_=P.V("c", kc), func=AF.Silu), [vec.b()], [scT.b()])
            op("act", lambda e, kc=kc: e.activation(out=scT[:, kc, 1:2], in_=P.V("cctx", kc), func=AF.Silu), [vec.b()], [scT.b()])
        nblk = 16 if phase == "A" else 48
        for blk in range(nblk):
            ab = stg[blk % 2]
            abv = ab.t[:, 0:1024].rearrange("p (kc n) -> p kc n", kc=8)
            dma(abv, ada_in[:, blk * 128:(blk + 1) * 128].rearrange("(kc p) n -> p kc n", p=128), writes=[ab.b()])
            for oc in range(1):
                ch = blk
                for kc in range(8):
                    op("pe", lambda e, kc=kc, abv=abv: e.matmul(PS[6][:, 0:2], lhsT=abv[:, kc, :], rhs=scT[:, kc, :],
                                                                  start=(kc == 0), stop=(kc == 7)), [ab.b(), scT.b()], [PS[6].b()])
                op("dve", lambda e, ch=ch: e.tensor_scalar(out=MOD[:, ch, :], in0=PS[6][:, 0:2], scalar1=P.V("adab", ch), scalar2=None, op0=ALU.add),
                   [PS[6].b(), vec.b()], [MOD.b()])
        A1 = P.sb("A1", [128, 8, 2]); A2 = P.sb("A2", [128, 8, 2])
        for kc in range(8):
            op("dve", lambda e, kc=kc: e.tensor_scalar(out=A1[:, kc, :], in0=MOD[:, 8 + kc, :], scalar1=1.0, scalar2=P.V("nw1", kc), op0=ALU.add, op1=ALU.mult),
               [MOD.b(), vec.b()], [A1.b()])
            if phase == "B":
                op("dve", lambda e, kc=kc: e.tensor_scalar(out=A2[:, kc, :], in0=MOD[:, 32 + kc, :], scalar1=1.0, scalar2=P.V("nw2", kc), op0=ALU.add, op1=ALU.mult),
                   [MOD.b(), vec.b()], [A2.b()])
        CB += [MOD.b(), A1.b(), A2.b()]

        XT = [P.sb("XT0", [128, 8, 512])] * 2
        HT = P.sb("HT", [128, 8, 512], BF16)
        SQ = P.sb("SQ", [128, 8, 512], BF16)
        RSTD = P.sb("RSTD", [128, 512]); TMP = [P.sb(f"TMP{i}", [128, 512]) for i in range(4)]
        WBUF = [P.sb(f"WBUF{i}", [128, 4096], BF16) for i in range(3)]
        wn = [0]

        def load_w(wbf, name, r0, rows, c0, cols):
            w = WBUF[wn[0] % 3]
            wn[0] += 1
            kcn = max(1, rows // 128)
            pr = min(rows, 128)
            view = w.t[0:pr, 0:kcn * cols].rearrange("p (kc n) -> p kc n", kc=kcn)
            if rows >= 128:
                src = wbf[r0:r0 + rows, c0:c0 + cols].rearrange("(kc p) n -> p kc n", p=128)
            else:
                src = wbf[r0:r0 + rows, c0:c0 + cols].rearrange("(kc p) n -> p kc n", kc=1)
            dma(view, src, reads=[WB[name]], writes=[w.b()])
            return w, view

        def load_x(ti, xt):
            c0, n, _ = TILES[ti]
            dma(xt[:, :, 0:n], xT_in[:, c0:c0 + n].rearrange("(kc p) t -> p kc t", p=128), writes=[xt.b()])
            dma(cosT[:, 0:n], cos_in[:, c0:c0 + n], writes=[cosT.b()])
            dma(sinT[:, 0:n], sin_in[:, c0:c0 + n], writes=[sinT.b()])

        def rstd_from(ps_t, n, scale, out_t, rows=128):
            op("act", lambda e: e.activation(out=out_t[0:rows, 0:n], in_=ps_t[0:rows, 0:n], func=AF.Sqrt, bias=P.V("eps", rows=rows), scale=scale),
               [ps_t.b(), vec.b()], [out_t.b()])
            op("dve", lambda e: e.reciprocal(out=out_t[0:rows, 0:n], in_=out_t[0:rows, 0:n]), [out_t.b()], [out_t.b()])

        def norm_mod(xt, n, s, A, shc, ht):
            for kc in range(8):
                op("act", lambda e, kc=kc: e.activation(out=SQ[:, kc, 0:n], in_=xt[:, kc, 0:n], func=AF.Square), [xt.b()], [SQ.b()])
            for kc in range(8):
                op("pe", lambda e, kc=kc: e.matmul(PS[6][:, 0:n], lhsT=ONESb, rhs=SQ[:, kc, 0:n], start=(kc == 0), stop=(kc == 7)),
                   [SQ.b(), cmb.b()], [PS[6].b()])
            rstd_from(PS[6], n, 1.0 / D, RSTD)
            for kc in range(8):
                tm = TMP[kc % 2]
                op("dve", lambda e, kc=kc, tm=tm: e.scalar_tensor_tensor(out=tm[:, 0:n], in0=xt[:, kc, 0:n], scalar=A[:, kc, s:s + 1], in1=RSTD[:, 0:n],
                                                                       op0=ALU.mult, op1=ALU.mult), [xt.b(), RSTD.b(), A.b()], [tm.b()])
                op("act", lambda e, kc=kc, tm=tm: e.activation(out=ht[:, kc, 0:n], in_=tm[:, 0:n], func=AF.Identity, bias=MOD[:, shc + kc, s:s + 1]),
                   [tm.b(), MOD.b()], [ht.b()])

        def lin_fm(pst, n, wview, c0, m, rhs_t, rhs_view, kcn, m0=0, extra_reads=()):
            for kc in range(kcn):
                op("pe", lambda e, kc=kc: e.matmul(pst[m0:m0 + m, 0:n], lhsT=wview[:, kc, c0:c0 + m], rhs=rhs_view[:, kc, 0:n],
                                                  start=(kc == 0), stop=(kc == kcn - 1)), [rhs_t.b()] + list(extra_reads), [pst.b()])

        def rope(x_t, x_view, rows, c0, n, out_view, out_t, pst):
            op("pe", lambda e: e.matmul(pst[0:rows, 0:n], lhsT=RMf[0:rows, 0:rows], rhs=x_view, start=True, stop=True), [x_t.b(), cmf.b()], [pst.b()])
            t1, t2 = TMP[2], TMP[3]
            op("pool", lambda e: e.tensor_tensor(out=t1[0:rows, 0:n], in0=x_view, in1=cosT[0:rows, 0:n], op=ALU.mult), [x_t.b(), cosT.b()], [t1.b()])
            op("dve", lambda e: e.tensor_tensor(out=t2[0:rows, 0:n], in0=pst[0:rows, 0:n], in1=sinT[0:rows, 0:n], op=ALU.mult), [pst.b(), sinT.b()], [t2.b()])
            op("dve", lambda e: e.tensor_tensor(out=out_view, in0=t1[0:rows, 0:n], in1=t2[0:rows, 0:n], op=ALU.add), [t1.b(), t2.b()], [out_t.b()])

        NH = 4
        KD = 128 if even else 64
        S = [[P.sb(f"S{h}_{i}", [128, 128]) for i in range(2)] for h in range(NH)]
        scur = [0] * NH
        QF = P.sb("QF", [128, 512]); KF = P.sb("KF", [128, 512]); LF = P.sb("LF", [128, 512])
        Bc = P.sb("Bc", [128, 512]); Pc = P.sb("Pc", [128, 512]); Dk = P.sb("Dk", [128, 512]); Db = P.sb("Db", [128, 512])
        Ek = P.sb("Ek", [128, 512]); Eq = P.sb("Eq", [128, 512]); Eb = P.sb("Eb", [128, 512])
        KH = P.sb("KH", [128, 512], BF16); QH = P.sb("QH", [128, 512], BF16); QTL = P.sb("QTL", [128, 512])
        KHT = P.sb("KHT", [128, 4, 128], BF16); ATM = P.sb("ATM", [128, 128], BF16)
        VT = P.sb("VT", [128, 4, 512], BF16)
        OACC = P.sb("OACC", [128, 4, TPC], BF16) if phase == "B" else None
        LTOT = P.sb("LTOT", [128, 8])

        def decay_factors(n, d):
            nch = n // 32
            op("dve", lambda e: e.tensor_tensor_scan(out=Bc[:, 0:n], data0=LF[:, 0:n], data1=LF[:, 0:n], initial=0.0, op0=ALU.add, op1=ALU.bypass),
               [LF.b()], [Bc.b()])
            op("pool", lambda e: e.tensor_tensor(out=Pc[:, 0:n], in0=Bc[:, 0:n], in1=LF[:, 0:n], op=ALU.subtract), [Bc.b(), LF.b()], [Pc.b()])
            v3 = lambda t: t[:, 0:n].rearrange("p (c t) -> p c t", t=32)
            bend = v3(Bc)[:, :, 31:32].to_broadcast([128, nch, 32])
            pst = v3(Pc)[:, :, 0:1].to_broadcast([128, nch, 32])
            if d == 0:
                op("dve", lambda e: e.tensor_tensor(out=v3(Dk), in0=bend, in1=v3(Bc), op=ALU.subtract), [Bc.b()], [Dk.b()])
                op("dve", lambda e: e.tensor_tensor(out=v3(Db), in0=v3(Bc), in1=pst, op=ALU.subtract), [Bc.b(), Pc.b()], [Db.b()])
            else:
                op("dve", lambda e: e.tensor_tensor(out=v3(Dk), in0=v3(Pc), in1=pst, op=ALU.subtract), [Pc.b()], [Dk.b()])
                op("dve", lambda e: e.tensor_tensor(out=v3(Db), in0=bend, in1=v3(Pc), op=ALU.subtract), [Bc.b(), Pc.b()], [Db.b()])
            op("pool", lambda e: e.tensor_scalar(out=Dk[:, 0:n], in0=Dk[:, 0:n], scalar1=-80.0, scalar2=None, op0=ALU.max), [Dk.b()], [Dk.b()])
            op("act", lambda e: e.activation(out=Ek[:, 0:n], in_=Dk[:, 0:n], func=AF.Exp), [Dk.b()], [Ek.b()])
            op("act", lambda e: e.activation(out=Eq[:, 0:n], in_=Dk[:, 0:n], func=AF.Exp, scale=-1.0), [Dk.b()], [Eq.b()])
            op("act", lambda e: e.activation(out=Eb[:, 0:n], in_=Db[:, 0:n], func=AF.Exp), [Db.b()], [Eb.b()])

        def rec_chunk_tile(heads, c0, n, d, want_out):
            op("pool", lambda e: e.tensor_tensor(out=KH[:, 0:n], in0=KF[:, 0:n], in1=Ek[:, 0:n], op=ALU.mult), [KF.b(), Ek.b()], [KH.b()])
            if want_out:
                op("pool", lambda e: e.tensor_tensor(out=QH[:, 0:n], in0=QF[:, 0:n], in1=Eq[:, 0:n], op=ALU.mult), [QF.b(), Eq.b()], [QH.b()])
                op("dve", lambda e: e.tensor_tensor(out=QTL[:, 0:n], in0=QF[:, 0:n], in1=Eb[:, 0:n], op=ALU.mult), [QF.b(), Eb.b()], [QTL.b()])
            nst = n // 128
            order = range(nst) if d == 0 else range(nst - 1, -1, -1)
            for sti in order:
                t0 = sti * 128
                op("pe", lambda e, t0=t0: e.transpose(PSB[:, 0:128], KH[:, t0:t0 + 128], IDb), [KH.b(), cmb.b()], [PSB.b()])
                for cm in range(4):
                    op("act", lambda e, cm=cm: e.activation(out=KHT[:, cm, :], in_=PSB[:, 0:128], func=AF.Copy, scale=P.V("rmask", cm)), [PSB.b(), vec.b()], [KHT.b()])
                for (h, pb) in heads:
                    if DBG == 95: break
                    rows = slice(pb, pb + KD)
                    if want_out:
                        op("pe", lambda e, t0=t0, rows=rows: e.matmul(PS[4][:, 0:128], lhsT=KH[rows, t0:t0 + 128], rhs=QH[rows, t0:t0 + 128], start=True, stop=True),
                           [KH.b(), QH.b()], [PS[4].b()])
                        op("dve", lambda e: e.tensor_tensor(out=ATM[:], in0=PS[4][:, 0:128], in1=MASK[d], op=ALU.mult), [PS[4].b(), cmf.b()], [ATM.b()])
                        op("pe", lambda e, h=h, sti=sti: e.matmul(PS[5][:, 0:128], lhsT=VT[:, sti, h * 128:(h + 1) * 128], rhs=ATM[:], start=True, stop=False),
                           [VT.b(), ATM.b()], [PS[5].b()])
                    corder = range(4) if d == 0 else range(3, -1, -1)
                    for ci, c in enumerate(corder):
                        sp = S[h][scur[h]]; sn = S[h][1 - scur[h]]
                        cc = t0 + c * 32
                        if want_out:
                            op("pe", lambda e, sp=sp, rows=rows, cc=cc, c=c, ci=ci: e.matmul(PS[5][:, c * 32:(c + 1) * 32], lhsT=sp[rows, :], rhs=QTL[rows, cc:cc + 32],
                                                                                    start=False, stop=(ci == 3)), [sp.b(), QTL.b()], [PS[5].b()])
                        op("pe", lambda e, c=c, h=h, sti=sti: e.matmul(PS[3][:, 0:128], lhsT=KHT[:, c, :], rhs=VT[:, sti, h * 128:(h + 1) * 128],
                                                                       start=True, stop=True), [KHT.b(), VT.b()], [PS[3].b()])
                        ecol = cc + 31 if d == 0 else cc
                        if DBG == 96: continue
                        op("dve", lambda e, sp=sp, sn=sn, rows=rows, ecol=ecol: e.scalar_tensor_tensor(out=sn[rows, :], in0=sp[rows, :], scalar=Eb[rows, ecol:ecol + 1],
                                                                                                     in1=PS[3][rows, 0:128], op0=ALU.mult, op1=ALU.add),
                           [sp.b(), Eb.b(), PS[3].b()], [sn.b()])
                        scur[h] = 1 - scur[h]
                    if want_out:
                        oc = OACC[:, h, c0 + t0:c0 + t0 + 128]
                        if d == 0:
                            op("act", lambda e, oc=oc: e.copy(out=oc, in_=PS[5][:, 0:128]), [PS[5].b()], [OACC.b((h, c0))])
                        else:
                            op("dve", lambda e, oc=oc: e.tensor_tensor(out=oc, in0=PS[5][:, 0:128], in1=oc, op=ALU.add), [PS[5].b()], [OACC.b((h, c0))])

        def zero_states():
            for h in range(NH):
                op("pool", lambda e, h=h: e.memset(S[h][scur[h]][:], 0.0), [], [S[h][scur[h]].b()])

        def v_token_major(ht, n, wname_cols):
            c0w, = wname_cols
            w, wv = load_w(win_bf, "win", 0, D, c0w, 512)
            for sti in range(n // 128):
                for kc in range(8):
                    op("pe", lambda e, kc=kc, sti=sti: e.matmul(PS[2][:, 0:512], lhsT=ht[:, kc, sti * 128:(sti + 1) * 128], rhs=wv[:, kc, :],
                                                               start=(kc == 0), stop=(kc == 7)), [ht.b(), w.b()], [PS[2].b()])
                op("act", lambda e, sti=sti: e.copy(out=VT[:, sti, :], in_=PS[2][:, 0:512]), [PS[2].b()], [VT.b()])

        LB = P.sb("LB", [128, 8]); OMLB = P.sb("OMLB", [128, 8]); LG = P.sb("LG", [128, 4])
        if even:
            op("dve", lambda e: e.tensor_tensor(out=LB[:], in0=P.V("lb1", 0, 8), in1=P.V("lb0", 0, 8), op=ALU.subtract), [vec.b()], [LB.b()])
            op("act", lambda e: e.activation(out=LB[:], in_=LB[:], func=AF.Sigmoid), [LB.b()], [LB.b()])
            op("dve", lambda e: e.tensor_scalar(out=LB[:], in0=LB[:], scalar1=P.V("lbsel"), scalar2=None, op0=ALU.mult), [LB.b(), vec.b()], [LB.b()])
            op("dve", lambda e: e.tensor_scalar(out=OMLB[:], in0=LB[:], scalar1=-1.0, scalar2=1.0, op0=ALU.mult, op1=ALU.add), [LB.b()], [OMLB.b()])
        else:
            op("act", lambda e: e.activation(out=LG[:], in_=P.V("rtd", 0, 4), func=AF.Sigmoid), [vec.b()], [LG.b()])
            op("act", lambda e: e.activation(out=LG[:], in_=LG[:], func=AF.Ln), [LG.b()], [LG.b()])
        CB += [LB.b(), OMLB.b(), LG.b()]

        def rec_feature_chunk(ht, n, c0, fc, d, want_q, wk, wq):
            if even:
                w, wv = wk
                lin_fm(PS[0], n, wv, fc * 128, 128, ht, ht.t, 8, extra_reads=[w.b()])
                sg = TMP[0]
                op("act", lambda e: e.activation(out=sg[:, 0:n], in_=PS[0][:, 0:n], func=AF.Sigmoid), [PS[0].b()], [sg.b()])
                col = d * 4 + fc
                op("dve", lambda e: e.tensor_scalar(out=sg[:, 0:n], in0=sg[:, 0:n], scalar1=OMLB[:, col:col + 1], scalar2=LB[:, col:col + 1], op0=ALU.mult, op1=ALU.add),
                   [sg.b(), LB.b(), OMLB.b()], [sg.b()])
                op("pool", lambda e: e.tensor_scalar(out=KF[:, 0:n], in0=sg[:, 0:n], scalar1=-1.0, scalar2=1.0, op0=ALU.mult, op1=ALU.add), [sg.b()], [KF.b()])
                op("act", lambda e: e.activation(out=LF[:, 0:n], in_=sg[:, 0:n], func=AF.Ln), [sg.b()], [LF.b()])
                if want_q:
                    w, wv = wq
                    lin_fm(PS[1], n, wv, fc * 128, 128, ht, ht.t, 8, extra_reads=[w.b()])
                    op("act", lambda e: e.activation(out=QF[:, 0:n], in_=PS[1][:, 0:n], func=AF.Silu), [PS[1].b()], [QF.b()])
            else:
                w, wv = wk
                lin_fm(PS[0], n, wv, fc * 128, 128, ht, ht.t, 8, extra_reads=[w.b()])
                kx = TMP[0]
                op("act", lambda e: e.activation(out=kx[:, 0:n], in_=PS[0][:, 0:n], func=AF.Copy, scale=0.125), [PS[0].b()], [kx.b()])
                rope(kx, kx[:, 0:n], 128, c0, n, KF[:, 0:n], KF, PS[1])
                op("pool", lambda e: e.memset(LF[:, 0:n], 1.0), [], [LF.b()])
                col = d * 2 + fc
                op("dve", lambda e: e.tensor_scalar(out=LF[:, 0:n], in0=LF[:, 0:n], scalar1=LG[:, col:col + 1], scalar2=None, op0=ALU.mult), [LF.b(), LG.b()], [LF.b()])
                if want_q:
                    w, wv = wq
                    lin_fm(PS[0], n, wv, fc * 128, 128, ht, ht.t, 8, extra_reads=[w.b()])
                    qx = TMP[1]
                    op("act", lambda e: e.copy(out=qx[:, 0:n], in_=PS[0][:, 0:n]), [PS[0].b()], [qx.b()])
                    rope(qx, qx[:, 0:n], 128, c0, n, QF[:, 0:n], QF, PS[1])

        NFC = 4 if even else 2
        VCOL = 1536 if even else 2048

        def heads_of(fc):
            return [(fc, 0)] if even else [(2 * fc, 0), (2 * fc + 1, 64)]

        def rec_tile(ht, ti, d, want_out):
            c0, n, _ = TILES[ti]
            v_token_major(ht, n, (VCOL,))
            if even:
                wk = load_w(win_bf, "win", 0, D, 512 + d * 512, 512)
                wq = load_w(win_bf, "win", 0, D, 0, 512) if want_out else None
            else:
                wk = load_w(win_bf, "win", 0, D, 1792, 256)
                wq = load_w(win_bf, "win", 0, D, 1536, 256) if want_out else None
            for fc in range(NFC):
                if DBG == 91: break
                rec_feature_chunk(ht, n, c0, fc, d, want_out, wk, wq)
                if DBG == 94: continue
                decay_factors(n, d)
                if DBG == 92: continue
                if phase == "A":
                    col = d * 4 + fc
                    tot = Bc[:, n - 1:n]
                    op("dve", lambda e, col=col, tot=tot: e.tensor_tensor(out=LTOT[:, col:col + 1], in0=LTOT[:, col:col + 1], in1=tot, op=ALU.add),
                       [Bc.b(), LTOT.b()], [LTOT.b()])
                rec_chunk_tile(heads_of(fc), c0, n, d, want_out)

        if phase == "A":
            KNs = P.sb("KNs", [128, 4, 512], BF16); KRs = P.sb("KRs", [128, 4, 512], BF16)
            CKV = P.sb("CKV", [128, 2, 512]); CKVN = P.sb("CKVN", [128, 2, 512], BF16)
            KRP = P.sb("KRP", [128, 512]); KRR = P.sb("KRR", [128, 512]); KNF = P.sb("KNF", [128, 512])
            VS = P.sb("VS", [128, 4, 512], BF16)
            op("pool", lambda e: e.memset(LTOT[:], 0.0), [], [LTOT.b()])

            def kv_even(ht, ti):
                c0, n, _ = TILES[ti]
                w, wv = load_w(win_bf, "win", 0, D, 2944, 320)
                for cc in range(2):
                    lin_fm(PS[0], n, wv, cc * 128, 128, ht, ht.t, 8, extra_reads=[w.b()])
                    op("act", lambda e, cc=cc: e.copy(out=CKV[:, cc, 0:n], in_=PS[0][:, 0:n]), [PS[0].b()], [CKV.b()])
                    op("pool", lambda e, cc=cc: e.tensor_tensor(out=SQ[:, cc, 0:n], in0=CKV[:, cc, 0:n], in1=CKV[:, cc, 0:n], op=ALU.mult), [CKV.b()], [SQ.b()])
                if DBG == 31: return
                lin_fm(PS[1], n, wv, 256, 64, ht, ht.t, 8, extra_reads=[w.b()])
                if DBG == 311: return
                raw = TMP[1]
                op("act", lambda e: e.copy(out=raw[0:64, 0:n], in_=PS[1][0:64, 0:n]), [PS[1].b()], [raw.b()])
                op("dve", lambda e: e.tensor_scalar(out=KRP[0:64, 0:n], in0=raw[0:64, 0:n], scalar1=P.V("qkk_r", rows=64), scalar2=None, op0=ALU.mult),
                   [raw.b(), vec.b()], [KRP.b()])
                if DBG == 312: return
                op("pool", lambda e: e.tensor_tensor(out=SQ[0:64, 2, 0:n], in0=raw[0:64, 0:n], in1=raw[0:64, 0:n], op=ALU.mult), [raw.b()], [SQ.b()])
                if DBG == 32: return
                for cc in range(2):
                    op("pe", lambda e, cc=cc: e.matmul(PS[6][:, 0:n], lhsT=ONESb, rhs=SQ[:, cc, 0:n], start=(cc == 0), stop=(cc == 1)), [SQ.b(), cmb.b()], [PS[6].b()])
                rstd_from(PS[6], n, 1.0 / 256, RSTD)
                for cc in range(2):
                    op("dve", lambda e, cc=cc: e.scalar_tensor_tensor(out=CKVN[:, cc, 0:n], in0=CKV[:, cc, 0:n], scalar=P.V("kvn", cc), in1=RSTD[:, 0:n], op0=ALU.mult, op1=ALU.mult),
                       [CKV.b(), RSTD.b(), vec.b()], [CKVN.b()])
                if DBG == 33: return
                rope(KRP, KRP[0:64, 0:n], 64, c0, n, KRR[0:64, 0:n], KRR, PS[1])
                if DBG == 34: return
                w2, w2v = load_w(wukv_bf, "wukv", 0, 256, 0, 1024)
                for sti in range(n // 128):
                    for hh in range(4):
                        for kc in range(2):
                            op("pe", lambda e, kc=kc, sti=sti, hh=hh: e.matmul(PS[2][:, hh * 128:(hh + 1) * 128], lhsT=CKVN[:, kc, sti * 128:(sti + 1) * 128],
                                                                       rhs=w2v[:, kc, hh * 256 + 128:hh * 256 + 256],
                                                                       start=(kc == 0), stop=(kc == 1)), [CKVN.b(), w2.b()], [PS[2].b()])
                    op("act", lambda e, sti=sti: e.copy(out=VS[:, sti, :], in_=PS[2][:, 0:512]), [PS[2].b()], [VS.b()])
                    tb = (c0 + sti * 128) // 128
                    dma(v_out[tb], VS[:, sti, :], reads=[VS.b()], writes=[v_outb])
                if DBG == 35: return
                for h in range(4):
                    lin_fm(PS[0], n, w2v, h * 256, 128, CKVN, CKVN.t, 2, extra_reads=[w2.b()])
                    op("act", lambda e: e.copy(out=KNF[:, 0:n], in_=PS[0][:, 0:n]), [PS[0].b()], [KNF.b()])
                    op("pool", lambda e: e.tensor_tensor(out=SQ[:, 3, 0:n], in0=KNF[:, 0:n], in1=KNF[:, 0:n], op=ALU.mult), [KNF.b()], [SQ.b()])
                    op("pe", lambda e: e.matmul(PS[6][:, 0:n], lhsT=ONESb, rhs=SQ[:, 3, 0:n], start=True, stop=False), [SQ.b(), cmb.b()], [PS[6].b()])
                    op("pe", lambda e: e.matmul(PS[6][:, 0:n], lhsT=ONESb[0:64, :], rhs=SQ[0:64, 2, 0:n], start=False, stop=True), [SQ.b(), cmb.b()], [PS[6].b()])
                    rstd_from(PS[6], n, 1.0 / 192, RSTD)
                    op("dve", lambda e, h=h: e.scalar_tensor_tensor(out=KNs[:, h, 0:n], in0=KNF[:, 0:n], scalar=P.V("qkk_n"), in1=RSTD[:, 0:n], op0=ALU.mult, op1=ALU.mult),
                       [KNF.b(), RSTD.b(), vec.b()], [KNs.b()])
                    op("dve", lambda e, h=h: e.tensor_tensor(out=KRs[0:64, h, 0:n], in0=KRR[0:64, 0:n], in1=RSTD[0:64, 0:n], op=ALU.mult), [KRR.b(), RSTD.b()], [KRs.b()])
                for h in range(4):
                    dma(kn_out[h, :, c0:c0 + n], KNs[:, h, 0:n], reads=[KNs.b()], writes=[v_outb])
                    dma(kr_out[h, :, c0:c0 + n], KRs[0:64, h, 0:n], reads=[KRs.b()], writes=[v_outb])

            def kv_odd(ht, ti):
                c0, n, _ = TILES[ti]
                w, wv = load_w(win_bf, "win", 0, D, 512, 512)
                for h in range(4):
                    lin_fm(PS[0], n, wv, h * 128, 128, ht, ht.t, 8, extra_reads=[w.b()])
                    op("act", lambda e: e.copy(out=KNF[:, 0:n], in_=PS[0][:, 0:n]), [PS[0].b()], [KNF.b()])
                    op("pool", lambda e: e.tensor_tensor(out=SQ[:, 3, 0:n], in0=KNF[:, 0:n], in1=KNF[:, 0:n], op=ALU.mult), [KNF.b()], [SQ.b()])
                    op("pe", lambda e: e.matmul(PS[6][:, 0:n], lhsT=BDb, rhs=SQ[:, 3, 0:n], start=True, stop=True), [SQ.b(), cmb.b()], [PS[6].b()])
                    rstd_from(PS[6], n, 1.0 / 64, RSTD)
                    op("dve", lambda e: e.scalar_tensor_tensor(out=KRP[:, 0:n], in0=KNF[:, 0:n], scalar=P.V("dak"), in1=RSTD[:, 0:n], op0=ALU.mult, op1=ALU.mult),
                       [KNF.b(), RSTD.b(), vec.b()], [KRP.b()])
                    rope(KRP, KRP[:, 0:n], 128, c0, n, KNs[:, h, 0:n], KNs, PS[1])
                for h in range(4):
                    dma(kn_out[h, :, c0:c0 + n], KNs[:, h, 0:n], reads=[KNs.b()], writes=[v_outb])
                w, wv = load_w(win_bf, "win", 0, D, 1024, 512)
                for sti in range(n // 128):
                    for kc in range(8):
                        op("pe", lambda e, kc=kc, sti=sti: e.matmul(PS[2][:, 0:512], lhsT=ht[:, kc, sti * 128:(sti + 1) * 128], rhs=wv[:, kc, :],
                                                                   start=(kc == 0), stop=(kc == 7)), [ht.b(), w.b()], [PS[2].b()])
                    op("act", lambda e, sti=sti: e.copy(out=VS[:, sti, :], in_=PS[2][:, 0:512]), [PS[2].b()], [VS.b()])
                    tb = (c0 + sti * 128) // 128
                    dma(v_out[tb], VS[:, sti, :], reads=[VS.b()], writes=[v_outb])

            v_outb = Buf()
            zero_states()
            for ti in range(5 if DBG != 1 else 0):
                xt = XT[ti % 2]
                load_x(ti, xt)
                c0, n, isc = TILES[ti]
                norm_mod(xt, n, isc, A1, 0, HT)
                if DBG != 2:
                    (kv_even if even else kv_odd)(HT, ti)
                if not isc and DBG not in (2, 3, 31, 32, 33, 34, 35, 311, 312):
                    rec_tile(HT, ti, 0, False)
            for h in range(4):
                sp = S[h][scur[h]]
                dma(sums_out[0, h], sp[:], reads=[sp.b()], writes=[v_outb])
            zero_states()
            for ti in ((4, 3, 2, 1) if DBG in (0, 9, 91, 92, 93, 94, 95, 96) else ()):
                xt = XT[ti % 2]
                load_x(ti, xt)
                c0, n, isc = TILES[ti]
                norm_mod(xt, n, isc, A1, 0, HT)
                rec_tile(HT, ti, 1, False)
            for h in range(4):
                sp = S[h][scur[h]]
                dma(sums_out[1, h], sp[:], reads=[sp.b()], writes=[v_outb])
            op("act", lambda e: e.activation(out=LTOT[:], in_=LTOT[:], func=AF.Exp), [LTOT.b()], [LTOT.b()])
            dma(sumd_out, LTOT[:], reads=[LTOT.b()], writes=[v_outb])
            k.finish([v_outb])
            P.n_instr = k.n
            return P

        xob = Buf()
        SJ = [P.sb(f"SJ{i}", [128, 128]) for i in range(2)]
        SD = P.sb("SD", [128, NCORES * 8]); MJ = P.sb("MJ", [128, 1])
        dma(SD[:], sumd_in, writes=[SD.b()])
        GT = P.sb("GT", [128, 4, 512], BF16)
        YT = P.sb("YT", [128, 8, 512], BF16)
        QN = P.sb("QN", [128, 4, 512], BF16); QR = P.sb("QR", [128, 4, 512], BF16)
        KNb = [P.sb(f"KNb{i}", [128, 1280], BF16) for i in range(2)]
        KRb = [P.sb(f"KRb{i}", [128, 1280], BF16) for i in range(2)]
        Vb = [P.sb(f"Vb{i}", [128, 10, 128], BF16) for i in range(2)]
        PT = [P.sb(f"PT{i}", [128, 512], BF16) for i in range(2)]
        AO = [P.sb(f"AO{i}", [128, 512]) for i in range(2)]
        HID = [P.sb("HID0", [128, 4, 512], BF16)] * 2
        CQ = P.sb("CQ", [128, 3, 512]); CQN = P.sb("CQN", [128, 3, 512], BF16)
        QX = P.sb("QX", [128, 512]); QY = P.sb("QY", [128, 512])
        NLAM = P.sb("NLAM", [128, 1])
        sbn = [0]

        def fold(d):
            js = range(NCORES) if d == 0 else range(NCORES - 1, -1, -1)
            fl, nfl = ("flf", "nflf") if d == 0 else ("flb", "nflb")
            for j in js:
                for h in range(4):
                    sj = SJ[sbn[0] % 2]
                    sbn[0] += 1
                    dma(sj[:], sums_in[j, d, h], writes=[sj.b()])
                    fc, pb = (h, 0) if even else (h // 2, 64 * (h % 2))
                    rows = slice(pb, pb + KD)
                    col = j * 8 + d * 4 + fc
                    op("dve", lambda e, col=col, j=j: e.tensor_scalar(out=MJ[:], in0=SD[:, col:col + 1], scalar1=P.V(fl, j), scalar2=P.V(nfl, j), op0=ALU.mult, op1=ALU.add),
                       [SD.b(), vec.b()], [MJ.b()])
                    op("pool", lambda e, sj=sj, j=j: e.tensor_scalar(out=sj[:], in0=sj[:], scalar1=P.V(fl, j), scalar2=None, op0=ALU.mult), [sj.b(), vec.b()], [sj.b()])
                    sp = S[h][scur[h]]; sn = S[h][1 - scur[h]]
                    op("dve", lambda e, sp=sp, sn=sn, sj=sj, rows=rows: e.scalar_tensor_tensor(out=sn[rows, :], in0=sp[rows, :], scalar=MJ[rows, 0:1], in1=sj[rows, :],
                                                                                          op0=ALU.mult, op1=ALU.add), [sp.b(), sj.b(), MJ.b()], [sn.b()])
                    scur[h] = 1 - scur[h]

        def attention(parts, h, n, out_t, kbs, scale):
            nsb = (kbs + 9) // 10
            first = True
            for sbi in range(nsb):
                kb0 = sbi * 10
                nk = min(10, kbs - kb0)
                bi = sbn[0] % 2
                sbn[0] += 1
                for (qv, ksrc, r0, nr, kbuf) in parts:
                    dma(kbuf[bi][r0:r0 + nr, 0:nk * 128], ksrc[:, kb0 * 128:(kb0 + nk) * 128], writes=[kbuf[bi].b()])
                dma(Vb[bi][:, 0:nk, :], v_in[h, :, kb0:kb0 + nk, :], writes=[Vb[bi].b()])
                for kb in range(nk):
                    pss = PS[kb % 2]
                    for pi, (qv, ksrc, r0, nr, kbuf) in enumerate(parts):
                        op("pe", lambda e, qv=qv, r0=r0, nr=nr, kbuf=kbuf, kb=kb, pi=pi, pss=pss: e.matmul(pss[:, 0:n], lhsT=kbuf[bi][r0:r0 + nr, kb * 128:(kb + 1) * 128],
                                                                                                   rhs=qv, start=(pi == 0), stop=(pi == len(parts) - 1)),
                           [kbuf[bi].b(), QN.b(), QR.b()], [pss.b()])
                    pt = PT[kb % 2]
                    op("act", lambda e, pss=pss, pt=pt: e.activation(out=pt[:, 0:n], in_=pss[:, 0:n], func=AF.Exp, scale=scale), [pss.b()], [pt.b()])
                    last = (sbi == nsb - 1 and kb == nk - 1)
                    op("pe", lambda e, kb=kb, pt=pt, last=last, first=first: e.matmul(PS[2][:, 0:n], lhsT=Vb[bi][:, kb, :], rhs=pt[:, 0:n], start=first, stop=last),
                       [Vb[bi].b(), pt.b()], [PS[2].b()])
                    op("pe", lambda e, pt=pt, last=last, first=first: e.matmul(PS[3][:, 0:n], lhsT=ONESb, rhs=pt[:, 0:n], start=first, stop=last),
                       [pt.b(), cmb.b()], [PS[3].b()])
                    first = False
            rc = TMP[0]
            op("dve", lambda e: e.reciprocal(out=rc[:, 0:n], in_=PS[3][:, 0:n]), [PS[3].b()], [rc.b()])
            op("dve", lambda e: e.tensor_tensor(out=out_t[:, 0:n], in0=PS[2][:, 0:n], in1=rc[:, 0:n], op=ALU.mult), [PS[2].b(), rc.b()], [out_t.b()])

        def head_rms_gate(src_view, n, gain_ap, gate_view, out_view, src_b, out_b, lhs_ones, inv_dim, extra_scale=None):
            sqv = SQ[:, 4, 0:n]
            op("pool", lambda e: e.tensor_tensor(out=sqv, in0=src_view, in1=src_view, op=ALU.mult), [src_b], [SQ.b()])
            op("pe", lambda e: e.matmul(PS[6][:, 0:n], lhsT=lhs_ones, rhs=sqv, start=True, stop=True), [SQ.b(), cmb.b()], [PS[6].b()])
            rstd_from(PS[6], n, inv_dim, RSTD)
            t = TMP[1]
            op("dve", lambda e: e.scalar_tensor_tensor(out=t[:, 0:n], in0=src_view, scalar=gain_ap, in1=RSTD[:, 0:n], op0=ALU.mult, op1=ALU.mult),
               [src_b, RSTD.b(), vec.b()], [t.b()])
            if gate_view is not None:
                op("dve", lambda e: e.tensor_tensor(out=out_view, in0=t[:, 0:n], in1=gate_view, op=ALU.mult), [t.b(), GT.b()], [out_b])
            else:
                op("dve", lambda e: e.tensor_scalar(out=out_view, in0=t[:, 0:n], scalar1=extra_scale, scalar2=None, op0=ALU.mult), [t.b(), vec.b()], [out_b])

        if not even:
            lp = P.sb("lp", [128, 2])
            op("dve", lambda e: e.tensor_tensor(out=lp[:, 0:1], in0=P.V("lam", 0), in1=P.V("lam", 1), op=ALU.mult), [vec.b()], [lp.b()])
            op("dve", lambda e: e.tensor_tensor(out=lp[:, 1:2], in0=P.V("lam", 2), in1=P.V("lam", 3), op=ALU.mult), [vec.b()], [lp.b()])
            op("pe", lambda e: e.matmul(PS[6][:, 0:2], lhsT=ONESf, rhs=lp[:], start=True, stop=True), [lp.b(), cmf.b()], [PS[6].b()])
            op("act", lambda e: e.activation(out=lp[:], in_=PS[6][:, 0:2], func=AF.Exp), [PS[6].b()], [lp.b()])
            op("dve", lambda e: e.tensor_tensor(out=NLAM[:], in0=lp[:, 1:2], in1=lp[:, 0:1], op=ALU.subtract), [lp.b()], [NLAM.b()])
            op("dve", lambda e: e.tensor_tensor(out=NLAM[:], in0=NLAM[:], in1=P.V("laminit"), op=ALU.subtract), [NLAM.b(), vec.b()], [NLAM.b()])

        def mixer_attn_even(ht, ti):
            c0, n, isc = TILES[ti]
            kbs = 2 if isc else NKB
            w, wv = load_w(win_bf, "win", 0, D, 2560, 384)
            for cc in range(3):
                lin_fm(PS[0], n, wv, cc * 128, 128, ht, ht.t, 8, extra_reads=[w.b()])
                op("act", lambda e, cc=cc: e.copy(out=CQ[:, cc, 0:n], in_=PS[0][:, 0:n]), [PS[0].b()], [CQ.b()])
                op("pool", lambda e, cc=cc: e.tensor_tensor(out=SQ[:, cc, 0:n], in0=CQ[:, cc, 0:n], in1=CQ[:, cc, 0:n], op=ALU.mult), [CQ.b()], [SQ.b()])
            for cc in range(3):
                op("pe", lambda e, cc=cc: e.matmul(PS[6][:, 0:n], lhsT=ONESb, rhs=SQ[:, cc, 0:n], start=(cc == 0), stop=(cc == 2)), [SQ.b(), cmb.b()], [PS[6].b()])
            rstd_from(PS[6], n, 1.0 / 384, RSTD)
            for cc in range(3):
                op("dve", lambda e, cc=cc: e.scalar_tensor_tensor(out=CQN[:, cc, 0:n], in0=CQ[:, cc, 0:n], scalar=P.V("qn", cc), in1=RSTD[:, 0:n], op0=ALU.mult, op1=ALU.mult),
                   [CQ.b(), RSTD.b(), vec.b()], [CQN.b()])
            w2, w2v = load_w(wuq_bf, "wuq", 0, 384, 0, 768)
            for h in range(4):
                lin_fm(PS[0], n, w2v, h * 192, 128, CQN, CQN.t, 3, extra_reads=[w2.b()])
                lin_fm(PS[1], n, w2v, h * 192 + 128, 64, CQN, CQN.t, 3, extra_reads=[w2.b()])
                op("act", lambda e: e.copy(out=QX[:, 0:n], in_=PS[0][:, 0:n]), [PS[0].b()], [QX.b()])
                op("act", lambda e: e.copy(out=QY[0:64, 0:n], in_=PS[1][0:64, 0:n]), [PS[1].b()], [QY.b()])
                op("pool", lambda e: e.tensor_tensor(out=SQ[:, 3, 0:n], in0=QX[:, 0:n], in1=QX[:, 0:n], op=ALU.mult), [QX.b()], [SQ.b()])
                op("pool", lambda e: e.tensor_tensor(out=SQ[0:64, 2, 0:n], in0=QY[0:64, 0:n], in1=QY[0:64, 0:n], op=ALU.mult), [QY.b()], [SQ.b()])
                op("pe", lambda e: e.matmul(PS[6][:, 0:n], lhsT=ONESb, rhs=SQ[:, 3, 0:n], start=True, stop=False), [SQ.b(), cmb.b()], [PS[6].b()])
                op("pe", lambda e: e.matmul(PS[6][:, 0:n], lhsT=ONESb[0:64, :], rhs=SQ[0:64, 2, 0:n], start=False, stop=True), [SQ.b(), cmb.b()], [PS[6].b()])
                rstd_from(PS[6], n, 1.0 / 192, RSTD)
                op("dve", lambda e, h=h: e.scalar_tensor_tensor(out=QN[:, h, 0:n], in0=QX[:, 0:n], scalar=P.V("qkq_n"), in1=RSTD[:, 0:n], op0=ALU.mult, op1=ALU.mult),
                   [QX.b(), RSTD.b(), vec.b()], [QN.b()])
                op("dve", lambda e: e.scalar_tensor_tensor(out=QY[0:64, 0:n], in0=QY[0:64, 0:n], scalar=P.V("qkq_r", rows=64), in1=RSTD[0:64, 0:n], op0=ALU.mult, op1=ALU.mult),
                   [QY.b(), RSTD.b(), vec.b()], [QY.b()])
                rope(QY, QY[0:64, 0:n], 64, c0, n, QR[0:64, h, 0:n], QR, PS[1])
            for h in range(4):
                ao = AO[h % 2]
                attention([(QN[:, h, 0:n], kn_in[h], 0, 128, KNb), (QR[0:64, h, 0:n], kr_in[h], 0, 64, KRb)], h, n, ao, kbs, 192 ** -0.5)
                op("act", lambda e, h=h, ao=ao: e.copy(out=YT[:, 4 + h, 0:n], in_=ao[:, 0:n]), [ao.b()], [YT.b()])

        def mixer_attn_odd(ht, ti):
            c0, n, isc = TILES[ti]
            kbs = 2 if isc else NKB
            w, wv = load_w(win_bf, "win", 0, D, 0, 512)
            for h in range(4):
                lin_fm(PS[0], n, wv, h * 128, 128, ht, ht.t, 8, extra_reads=[w.b()])
                op("act", lambda e: e.copy(out=QX[:, 0:n], in_=PS[0][:, 0:n]), [PS[0].b()], [QX.b()])
                op("pool", lambda e: e.tensor_tensor(out=SQ[:, 3, 0:n], in0=QX[:, 0:n], in1=QX[:, 0:n], op=ALU.mult), [QX.b()], [SQ.b()])
                op("pe", lambda e: e.matmul(PS[6][:, 0:n], lhsT=BDb, rhs=SQ[:, 3, 0:n], start=True, stop=True), [SQ.b(), cmb.b()], [PS[6].b()])
                rstd_from(PS[6], n, 1.0 / 64, RSTD)
                op("dve", lambda e: e.scalar_tensor_tensor(out=QY[:, 0:n], in0=QX[:, 0:n], scalar=P.V("daq"), in1=RSTD[:, 0:n], op0=ALU.mult, op1=ALU.mult),
                   [QX.b(), RSTD.b(), vec.b()], [QY.b()])
                rope(QY, QY[:, 0:n], 128, c0, n, QN[:, h, 0:n], QN, PS[1])
            for h in range(4):
                attention([(QN[0:64, h, 0:n], kn_in[h, 0:64], 0, 64, KNb)], h, n, AO[0], kbs, 0.125)
                attention([(QN[64:128, h, 0:n], kn_in[h, 64:128], 64, 64, KNb)], h, n, AO[1], kbs, 0.125)
                op("dve", lambda e: e.scalar_tensor_tensor(out=AO[0][:, 0:n], in0=AO[1][:, 0:n], scalar=NLAM[:, 0:1], in1=AO[0][:, 0:n], op0=ALU.mult, op1=ALU.add),
                   [AO[0].b(), AO[1].b(), NLAM.b()], [AO[0].b()])
                head_rms_gate(AO[0][:, 0:n], n, P.V("subln"), None, YT[:, h, 0:n], AO[0].b(), YT.b(), ONESb, 1.0 / 128, extra_scale=P.V("omlam"))

        def gate(ht, n):
            gc0 = 2048 if even else 2560
            w, wv = load_w(win_bf, "win", 0, D, gc0, 512)
            for cc in range(4):
                lin_fm(PS[0], n, wv, cc * 128, 128, ht, ht.t, 8, extra_reads=[w.b()])
                op("act", lambda e, cc=cc: e.activation(out=GT[:, cc, 0:n], in_=PS[0][:, 0:n], func=AF.Silu), [PS[0].b()], [GT.b()])

        def finish_tile(xt, ti):
            c0, n, isc = TILES[ti]
            yoff = 0 if even else 4
            for h in range(4):
                src = OACC[:, h, c0:c0 + n]
                op("act", lambda e, src=src: e.copy(out=QX[:, 0:n], in_=src), [OACC.b((h, c0))], [QX.b()])
                head_rms_gate(QX[:, 0:n], n, P.V("hgn" if even else "rtn"), GT[:, h, 0:n], YT[:, yoff + h, 0:n], QX.b(), YT.b(), ONESb, 1.0 / 128)
            for oc in range(8):
                if oc % 4 == 0:
                    w, wv = load_w(wo_bf, "wo", 0, D, oc * 128, 512)
                lin_fm(PS[oc % 2], n, wv, (oc % 4) * 128, 128, YT, YT.t, 8, extra_reads=[w.b()])
                op("dve", lambda e, oc=oc: e.scalar_tensor_tensor(out=xt[:, oc, 0:n], in0=PS[oc % 2][:, 0:n], scalar=MOD[:, 16 + oc, isc:isc + 1], in1=xt[:, oc, 0:n],
                                                                 op0=ALU.mult, op1=ALU.add), [PS[oc % 2].b(), MOD.b(), xt.b()], [xt.b()])
            norm_mod(xt, n, isc, A2, 24, HT)
            for hb in range(8):
                w1, w1v = load_w(w1_bf, "w1", 0, D, hb * 512, 512)
                w2, w2v = load_w(w2_bf, "w2", hb * 512, 512, 0, 1024)
                hid = HID[hb % 2]
                for hc in range(4):
                    lin_fm(PS[hc % 2], n, w1v, hc * 128, 128, HT, HT.t, 8, extra_reads=[w1.b()])
                    r = TMP[hc % 2]
                    op("act", lambda e, hc=hc, r=r: e.activation(out=r[:, 0:n], in_=PS[hc % 2][:, 0:n], func=AF.Relu), [PS[hc % 2].b()], [r.b()])
                    op("dve", lambda e, hc=hc, r=r: e.tensor_tensor(out=hid[:, hc, 0:n], in0=PS[hc % 2][:, 0:n], in1=r[:, 0:n], op=ALU.mult), [PS[hc % 2].b(), r.b()], [hid.b()])
                for oc in range(8):
                    pst = PS[2 + oc % 2]
                    lin_fm(pst, n, w2v, oc * 128, 128, hid, hid.t, 4, extra_reads=[w2.b()])
                    op("dve", lambda e, oc=oc, pst=pst: e.scalar_tensor_tensor(out=xt[:, oc, 0:n], in0=pst[:, 0:n], scalar=MOD[:, 40 + oc, isc:isc + 1], in1=xt[:, oc, 0:n],
                                                                             op0=ALU.mult, op1=ALU.add), [pst.b(), MOD.b(), xt.b()], [xt.b()])
            dma(xT_out[:, c0:c0 + n].rearrange("(kc p) t -> p kc t", p=128), xt[:, :, 0:n], reads=[xt.b()], writes=[xob])

        zero_states()
        for ti in range(5):
            xt = XT[ti % 2]
            load_x(ti, xt)
            c0, n, isc = TILES[ti]
            norm_mod(xt, n, isc, A1, 0, HT)
            if ti == 1:
                fold(0)
            rec_tile(HT, ti, 0, True)
        zero_states()
        for ti in (0, 4, 3, 2, 1):
            xt = XT[ti % 2]
            load_x(ti, xt)
            c0, n, isc = TILES[ti]
            norm_mod(xt, n, isc, A1, 0, HT)
            if ti == 4:
                fold(1)
            rec_tile(HT, ti, 1, True)
            gate(HT, n)
            (mixer_attn_even if even else mixer_attn_odd)(HT, ti)
            finish_tile(xt, ti)
        k.finish([xob])
        P.n_instr = k.n
    return P


_PROGS = {}


def get_prog(even, phase):
    key = (even, phase)
    if key not in _PROGS:
        _PROGS[key] = build(even, phase)
    return _PROGS[key]


def run_layer(inp, l, xT_sh, consts):
    even = l % 2 == 0
    j = l // 2
    f32 = lambda a: np.ascontiguousarray(np.asarray(a, np.float32))
    w_in = f32(inp["a_w_in"][j] if even else inp["c_w_in"][j])
    ada = f32(inp["ada_w"][l])
    vecs = [build_vecs(inp, l, c) for c in range(NCORES)]
    base = []
    for c in range(NCORES):
        m = {"xT": xT_sh[c], "vecs": vecs[c], "cos": consts[c]["cos"], "sin": consts[c]["sin"], "cmat": consts[c]["cmat"],
             "ada_w": ada, "w_in": w_in}
        if even:
            m["w_ukv"] = f32(inp["mla_w_ukv"][j])
        base.append(m)
    pa = get_prog(even, "A")
    ra = run_bass_kernel_spmd(pa.nc, base, core_ids=list(range(NCORES))).results
    cat = lambda name, ax, sl_ctx, sl_lat: np.ascontiguousarray(np.concatenate([sl_ctx(ra[0][name])] + [sl_lat(ra[c][name]) for c in range(NCORES)], axis=ax))
    kn_all = cat("kn", 2, lambda a: a[:, :, :NCTX], lambda a: a[:, :, NCTX:])
    vtm = np.concatenate([ra[0]["v"][:NCTX // 128]] + [ra[c]["v"][NCTX // 128:] for c in range(NCORES)], axis=0)
    v_all = np.ascontiguousarray(vtm.reshape(NKB, 128, 4, 128).transpose(2, 1, 0, 3))
    sumS = np.ascontiguousarray(np.stack([ra[c]["sumS"] for c in range(NCORES)], 0))
    sumD = np.ascontiguousarray(np.concatenate([ra[c]["sumD"] for c in range(NCORES)], 1))
    pb = get_prog(even, "B")
    maps = []
    for c in range(NCORES):
        m = dict(base[c])
        m.update({"kn_all": kn_all, "v_all": v_all, "sumS_all": sumS, "sumD_all": sumD,
                  "w_o": f32(inp["w_o"][l]), "w1": f32(inp["mlp_w1"][l]), "w2": f32(inp["mlp_w2"][l])})
        if even:
            m["kr_all"] = cat("kr", 2, lambda a: a[:, :, :NCTX], lambda a: a[:, :, NCTX:])
            m["w_uq"] = f32(inp["mla_w_uq"][j])
        maps.append(m)
    rb = run_bass_kernel_spmd(pb.nc, maps, core_ids=list(range(NCORES))).results
    return [np.ascontiguousarray(rb[c]["xTo"]) for c in range(NCORES)]


def kernel(**inp):
    x = np.asarray(inp["x"], np.float32)[0]
    ctx = np.asarray(inp["ctx"], np.float32)[0]
    consts = [const_tables(c) for c in range(NCORES)]
    xT_sh = [np.ascontiguousarray(np.concatenate([ctx.T, x[c * LPC:(c + 1) * LPC].T], axis=1)) for c in range(NCORES)]
    for l in range(4):
        xT_sh = run_layer(inp, l, xT_sh, consts)
    out = np.concatenate([xT_sh[c][:, NCTX:].T for c in range(NCORES)], axis=0)
    return np.ascontiguousarray(out[None].astype(np.float32))
```

```python
import contextlib
import math
import numpy as np
import concourse.bass as bass
import concourse.mybir as mybir
from concourse.bass_utils import run_bass_kernel_spmd

F32 = mybir.dt.float32
BF16 = mybir.dt.bfloat16
AF = mybir.ActivationFunctionType
ALU = mybir.AluOpType

NCORES = 8
DBG = 0
D = 1024
NLAT = 16384
NCTX = 256
LPC = NLAT // NCORES
TPC = NCTX + LPC
TALL = NCTX + NLAT
NKB = TALL // 128
EPS = 1e-6
TILES = [(0, 256, 1), (256, 512, 0), (768, 512, 0), (1280, 512, 0), (1792, 512, 0)]


class Buf:
    __slots__ = ("lw", "rd")

    def __init__(self):
        self.lw = None
        self.rd = []


class Eng:
    def __init__(self, eng, sem):
        self.eng, self.sem = eng, sem
        self.count = 0
        self.seen = {}


class K:
    def __init__(self, nc, sems):
        self.nc = nc
        self.E = {"pe": Eng(nc.tensor, sems[0]), "dve": Eng(nc.vector, sems[1]),
                  "act": Eng(nc.scalar, sems[2]), "pool": Eng(nc.gpsimd, sems[3])}
        self.dmaq = Eng(nc.sync, None)
        self.dsems = sems[4:]
        self.dcnt = [0] * len(self.dsems)
        self.dnext = 0
        self.n = 0

    def _deps(self, e, reads, writes):
        deps = {}
        for b in reads:
            if b.lw is not None:
                k, c = b.lw
                if deps.get(k, 0) < c:
                    deps[k] = c
        for b in writes:
            if b.lw is not None:
                k, c = b.lw
                if deps.get(k, 0) < c:
                    deps[k] = c
            for k, c in b.rd:
                if deps.get(k, 0) < c:
                    deps[k] = c
        for k, c in deps.items():
            if e.seen.get(k, 0) >= c:
                continue
            if k is e.sem and e is self.E["pe"]:
                continue
            e.seen[k] = c
            e.eng.wait_ge(k, c)

    def _mark(self, key, reads, writes):
        for b in reads:
            b.rd.append(key)
            if len(b.rd) > 24:
                m = {}
                for k, c in b.rd:
                    if m.get(k, 0) < c:
                        m[k] = c
                b.rd = list(m.items())
        for b in writes:
            b.lw = key
            b.rd = []

    def op(self, en, fn, reads=(), writes=()):
        e = self.E[en]
        self._deps(e, reads, writes)
        ins = fn(e.eng)
        e.count += 1
        ins.then_inc(e.sem, 1)
        self._mark((e.sem, e.count), reads, writes)
        self.n += 1

    def dma(self, out, in_, reads=(), writes=()):
        e = self.dmaq
        i = self.dnext
        self.dnext = (i + 1) % len(self.dsems)
        s = self.dsems[i]
        if self.dcnt[i] > 0 and e.seen.get(s, 0) < self.dcnt[i]:
            e.eng.wait_ge(s, self.dcnt[i])
            e.seen[s] = self.dcnt[i]
        self._deps(e, reads, writes)
        ins = e.eng.dma_start(out=out, in_=in_)
        self.dcnt[i] += 16
        ins.then_inc(s, 16)
        self._mark((s, self.dcnt[i]), reads, writes)
        self.n += 1

    def finish(self, bufs):
        self._deps(self.dmaq, bufs, bufs)


class T:
    def __init__(self, t):
        self.t = t
        self._b = {}

    def b(self, key=0):
        if key not in self._b:
            self._b[key] = Buf()
        return self._b[key]

    def __getitem__(self, idx):
        return self.t[idx]


def fm(v):
    v = np.asarray(v, np.float32)
    return np.ascontiguousarray(v.reshape(-1, 128).T)


def pad128(v):
    o = np.zeros(128, np.float32)
    o[: v.shape[0]] = v
    return o[:, None]


class VecPack:
    def __init__(self):
        self.cols = []
        self.off = {}
        self.n = 0

    def add(self, name, arr):
        arr = np.asarray(arr, np.float32)
        assert arr.shape[0] == 128
        self.off[name] = self.n
        self.cols.append(arr)
        self.n += arr.shape[1]

    def pack(self):
        return np.ascontiguousarray(np.concatenate(self.cols, axis=1))


def vec_layout(even):
    off = {}
    n = 0

    def a(name, w):
        nonlocal n
        off[name] = n
        n += w
    a("nw1", 8); a("nw2", 8); a("adab", 48); a("c", 8); a("cctx", 8)
    a("flf", 8); a("nflf", 8); a("flb", 8); a("nflb", 8); a("eps", 1); a("one", 1); a("rmask", 4)
    if even:
        a("lb0", 8); a("lb1", 8); a("hgn", 1); a("qn", 3); a("kvn", 2)
        a("qkq_n", 1); a("qkq_r", 1); a("qkk_n", 1); a("qkk_r", 1); a("lbsel", 1)
    else:
        a("daq", 1); a("dak", 1); a("subln", 1); a("rtn", 1); a("rtd", 4); a("lam", 4)
        a("laminit", 1); a("omlam", 1)
    return off, n


def build_vecs(inp, l, core):
    even = l % 2 == 0
    j = l // 2
    vp = VecPack()
    vp.add("nw1", fm(inp["norm_w"][l, 0])); vp.add("nw2", fm(inp["norm_w"][l, 1]))
    vp.add("adab", fm(inp["ada_b"][l])); vp.add("c", fm(inp["c"][0])); vp.add("cctx", fm(inp["c_ctx"]))
    flf = np.zeros((128, 8), np.float32); flb = np.zeros((128, 8), np.float32)
    flf[:, :core] = 1.0
    flb[:, core + 1:] = 1.0
    vp.add("flf", flf); vp.add("nflf", 1.0 - flf); vp.add("flb", flb); vp.add("nflb", 1.0 - flb)
    vp.add("eps", np.full((128, 1), EPS, np.float32)); vp.add("one", np.ones((128, 1), np.float32))
    rm = np.zeros((128, 4), np.float32)
    for cc in range(4):
        rm[cc * 32:(cc + 1) * 32, cc] = 1.0
    vp.add("rmask", rm)
    if even:
        vp.add("lb0", fm(inp["hg_lb"][0].reshape(-1))); vp.add("lb1", fm(inp["hg_lb"][1].reshape(-1)))
        vp.add("hgn", fm(inp["hg_norm"][j])); vp.add("qn", fm(inp["mla_q_norm"][j])); vp.add("kvn", fm(inp["mla_kv_norm"][j]))
        vp.add("qkq_n", fm(inp["mla_qk_q"][j][:128])); vp.add("qkq_r", pad128(inp["mla_qk_q"][j][128:]))
        vp.add("qkk_n", fm(inp["mla_qk_k"][j][:128])); vp.add("qkk_r", pad128(inp["mla_qk_k"][j][128:]))
        vp.add("lbsel", np.full((128, 1), float(j), np.float32))
    else:
        vp.add("daq", fm(np.tile(inp["da_qk_q"][j], 2))); vp.add("dak", fm(np.tile(inp["da_qk_k"][j], 2)))
        vp.add("subln", fm(inp["da_subln"][j])); vp.add("rtn", fm(inp["rt_norm"][j]))
        rtd = np.zeros((128, 4), np.float32)
        for d in range(2):
            for c in range(2):
                rtd[:64, d * 2 + c] = inp["rt_decay"][j, d, 2 * c]
                rtd[64:, d * 2 + c] = inp["rt_decay"][j, d, 2 * c + 1]
        vp.add("rtd", rtd)
        lam = np.zeros((128, 4), np.float32)
        lam[:64, :] = inp["da_lambda"][j].T
        vp.add("lam", lam)
        li = 0.8 - 0.6 * math.exp(-0.3 * l)
        vp.add("laminit", np.full((128, 1), li, np.float32)); vp.add("omlam", np.full((128, 1), 1.0 - li, np.float32))
    off, n = vec_layout(even)
    assert off == vp.off and n == vp.n
    return vp.pack()


def const_tables(core):
    pos = np.arange(LPC) + core * LPC
    row = (pos // 64).astype(np.float32)
    col = (pos % 64).astype(np.float32)
    inv = (10000.0 ** (-np.arange(16, dtype=np.float32) / 16)).astype(np.float32)
    ang = np.concatenate([row[:, None] * inv, row[:, None] * inv, col[:, None] * inv, col[:, None] * inv], axis=1)
    cos = np.ones((128, TPC), np.float32); sin = np.zeros((128, TPC), np.float32)
    cos[:64, NCTX:] = np.cos(ang).T.astype(np.float32); cos[64:, NCTX:] = cos[:64, NCTX:]
    sin[:64, NCTX:] = np.sin(ang).T.astype(np.float32); sin[64:, NCTX:] = sin[:64, NCTX:]
    R = np.zeros((128, 128), np.float32)
    for base in (0, 32, 64, 96):
        for i in range(16):
            R[base + i + 16, base + i] = -1.0
            R[base + i, base + i + 16] = 1.0
    s = np.arange(128)[:, None]; t = np.arange(128)[None, :]
    same = (s // 32) == (t // 32)
    mf = (same & (s <= t)).astype(np.float32)
    mb = (same & (s >= t)).astype(np.float32)
    ones = np.ones((128, 128), np.float32)
    bd = ((s // 64) == (t // 64)).astype(np.float32)
    ident = np.eye(128, dtype=np.float32)
    return {"cos": cos, "sin": sin, "cmat": np.ascontiguousarray(np.stack([R, mf, mb, ones, bd, ident], 0))}


class Prog:
    def __init__(self, even, phase):
        self.even, self.phase = even, phase
        self.nc = bass.Bass("TRN2", target_bir_lowering=False)
        self.st = contextlib.ExitStack()
        self.off, self.nv = vec_layout(even)

    def din(self, name, shape, dt=F32):
        return self.nc.dram_tensor(name, list(shape), dt, kind="ExternalInput").ap()

    def dout(self, name, shape, dt=F32):
        return self.nc.dram_tensor(name, list(shape), dt, kind="ExternalOutput").ap()

    def dscr(self, name, shape, dt=BF16):
        return self.nc.dram_tensor(name, list(shape), dt, kind="Internal").ap()

    def sb(self, name, shape, dt=F32):
        return T(self.st.enter_context(self.nc.sbuf_tensor(name, list(shape), dt)))

    def ps(self, name, shape, dt=F32):
        return T(self.st.enter_context(self.nc.psum_tensor(name, list(shape), dt)))

    def V(self, name, i=0, n=1, rows=128):
        o = self.off[name] + i
        return self.vec[0:rows, o:o + n]


def build(even, phase):
    P = Prog(even, phase)
    nc = P.nc
    k = None
    WIN = 3264 if even else 3072
    xT_in = P.din("xT", [D, TPC])
    vec_in = P.din("vecs", [128, P.nv])
    cos_in = P.din("cos", [128, TPC]); sin_in = P.din("sin", [128, TPC]); cm_in = P.din("cmat", [6, 128, 128])
    ada_in = P.din("ada_w", [D, 6 * D])
    win_in = P.din("w_in", [D, WIN])
    if even:
        wukv_in = P.din("w_ukv", [256, 1024])
    if phase == "A":
        if even:
            kn_out = P.dout("kn", [4, 128, TPC], BF16); kr_out = P.dout("kr", [4, 64, TPC], BF16)
        else:
            kn_out = P.dout("kn", [4, 128, TPC], BF16)
        v_out = P.dout("v", [TPC // 128, 128, 512], BF16)
        sums_out = P.dout("sumS", [2, 4, 128, 128]); sumd_out = P.dout("sumD", [128, 8])
    else:
        xT_out = P.dout("xTo", [D, TPC])
        kn_in = P.din("kn_all", [4, 128, TALL], BF16)
        if even:
            kr_in = P.din("kr_all", [4, 64, TALL], BF16)
            wuq_in = P.din("w_uq", [384, 768])
        v_in = P.din("v_all", [4, 128, NKB, 128], BF16)
        sums_in = P.din("sumS_all", [NCORES, 2, 4, 128, 128]); sumd_in = P.din("sumD_all", [128, NCORES * 8])
        wo_in = P.din("w_o", [D, D]); w1_in = P.din("w1", [D, 4 * D]); w2_in = P.din("w2", [4 * D, D])
        wo_bf = P.dscr("wo_bf", [D, D]); w1_bf = P.dscr("w1_bf", [D, 4 * D]); w2_bf = P.dscr("w2_bf", [4 * D, D])
        if even:
            wuq_bf = P.dscr("wuq_bf", [384, 768])
    win_bf = P.dscr("win_bf", [D, WIN])
    if even:
        wukv_bf = P.dscr("wukv_bf", [256, 1024])

    st = P.st
    with st:
        sems = [st.enter_context(nc.semaphore(f"s{i}")) for i in range(4 + 10)]
        k = K(nc, sems)
        op, dma = k.op, k.dma
        P.vec = None
        vec = P.sb("vec", [128, P.nv]); P.vec = vec.t
        cosT = P.sb("cosT", [128, 512]); sinT = P.sb("sinT", [128, 512])
        cmf = P.sb("cmf", [128, 6, 128]); cmb = P.sb("cmb", [128, 6, 128], BF16)
        dma(vec[:], vec_in, writes=[vec.b()])
        dma(cmf[:], cm_in.rearrange("c p d -> p c d"), writes=[cmf.b()])
        op("dve", lambda e: e.tensor_copy(out=cmb[:], in_=cmf[:]), [cmf.b()], [cmb.b()])
        RMf = cmf[:, 0, :]
        MASK = {0: cmf[:, 1, :], 1: cmf[:, 2, :]}
        ONESb = cmb[:, 3, :]; BDb = cmb[:, 4, :]; IDb = cmb[:, 5, :]; ONESf = cmf[:, 3, :]
        CB = [vec.b(), cmf.b(), cmb.b(), cosT.b(), sinT.b()]

        PS = [P.ps(f"ps{i}", [128, 512]) for i in range(7)]
        PSB = P.ps("psb", [128, 1024], BF16)

        stg = [P.sb(f"stg{i}", [128, 1024]) for i in range(2)]
        stgb = [P.sb(f"stgb{i}", [128, 1024], BF16) for i in range(2)]
        prep_n = [0]
        WB = {}

        def prep(src, dst, rows, cols, name, col_ranges=None):
            WB[name] = Buf()
            for r0 in range(0, rows, 128):
                rr = min(128, rows - r0)
                pieces = []
                for (a0, an) in (col_ranges or [(0, cols)]):
                    for c in range(a0, a0 + an, 1024):
                        pieces.append((c, min(1024, a0 + an - c)))
                for (c0, cn) in pieces:
                    i = prep_n[0] % 2
                    prep_n[0] += 1
                    dma(stg[i][0:rr, 0:cn], src[r0:r0 + rr, c0:c0 + cn], writes=[stg[i].b()])
                    op("pool" if i else "dve", lambda e, i=i, rr=rr, cn=cn: e.tensor_copy(out=stgb[i][0:rr, 0:cn], in_=stg[i][0:rr, 0:cn]),
                       [stg[i].b()], [stgb[i].b()])
                    dma(dst[r0:r0 + rr, c0:c0 + cn], stgb[i][0:rr, 0:cn], reads=[stgb[i].b()], writes=[WB[name]])

        if phase == "A":
            if even:
                prep(win_in, win_bf, D, WIN, "win", [(512, 1536), (2944, 320)])
                prep(wukv_in, wukv_bf, 256, 1024, "wukv")
            else:
                prep(win_in, win_bf, D, WIN, "win", [(512, 1024), (1792, 768)])
        else:
            if even:
                prep(win_in, win_bf, D, WIN, "win", [(0, 2048), (2048, 896)])
                prep(wuq_in, wuq_bf, 384, 768, "wuq")
            else:
                prep(win_in, win_bf, D, WIN, "win", [(0, 512), (1536, 1536)])
            prep(wo_in, wo_bf, D, D, "wo"); prep(w1_in, w1_bf, D, 4 * D, "w1"); prep(w2_in, w2_bf, 4 * D, D, "w2")

        scT = P.sb("scT", [128, 8, 2]); MOD = P.sb("MOD", [128, 48, 2])
        for kc in range(8):
            op("act", lambda e, kc=kc: e.activation(out=scT[:, kc, 0:1], in_=P.V("c", kc), func=AF.Silu), [vec.b()], [scT.b()])
            op("act", lambda e, kc=kc: e.activation(out=scT[:, kc, 1:2], in_=P.V("cctx", kc), func=AF.Silu), [vec.b()], [scT.b()])
        ncb = 2 if phase == "A" else 6
        IDf = cmf[:, 5, :]
        mrow = [P.sb(f"mrow{i}", [128, 512]) for i in range(2)]
        an = 0
        for cb in range(ncb):
            for kc in range(8):
                ab = stg[an % 2]
                an += 1
                dma(ab[:, 0:1024], ada_in[kc * 128:(kc + 1) * 128, cb * 1024:(cb + 1) * 1024], writes=[ab.b()])
                for half in range(2):
                    op("pe", lambda e, kc=kc, ab=ab, half=half: e.matmul(PS[half][0:2, 0:512], lhsT=scT[:, kc, :], rhs=ab[:, half * 512:(half + 1) * 512],
                                                                        start=(kc == 0), stop=(kc == 7)), [ab.b(), scT.b()], [PS[half].b()])
            for half in range(2):
                op("act", lambda e, half=half: e.copy(out=mrow[half][0:2, 0:512], in_=PS[half][0:2, 0:512]), [PS[half].b()], [mrow[half].b()])
            for j in range(8):
                ch = cb * 8 + j
                half, jj = j // 4, j % 4
                op("pe", lambda e, half=half, jj=jj: e.matmul(PS[6][:, 0:2], lhsT=mrow[half][0:2, jj * 128:(jj + 1) * 128], rhs=IDf[0:2, 0:2], start=True, stop=True),
                   [mrow[half].b(), cmf.b()], [PS[6].b()])
                op("dve", lambda e, ch=ch: e.tensor_scalar(out=MOD[:, ch, :], in0=PS[6][:, 0:2], scalar1=P.V("adab", ch), scalar2=None, op0=ALU.add),
                   [PS[6].b(), vec.b()], [MOD.b()])
        A1 = P.sb("A1", [128, 8, 2]); A2 = P.sb("A2", [128, 8, 2])
        for kc in range(8):
            op("dve", lambda e, kc=kc: e.tensor_scalar(out=A1[:, kc, :], in0=MOD[:, 8 + kc, :], scalar1=1.0, scalar2=P.V("nw1", kc), op0=ALU.add, op1=ALU.mult),
               [MOD.b(), vec.b()], [A1.b()])
            if phase == "B":
                op("dve", lambda e, kc=kc: e.tensor_scalar(out=A2[:, kc, :], in0=MOD[:, 32 + kc, :], scalar1=1.0, scalar2=P.V("nw2", kc), op0=ALU.add, op1=ALU.mult),
                   [MOD.b(), vec.b()], [A2.b()])
        CB += [MOD.b(), A1.b(), A2.b()]

        XT = [P.sb("XT0", [128, 8, 512])] * 2
        HT = P.sb("HT", [128, 8, 512], BF16)
        SQ = P.sb("SQ", [128, 8, 512], BF16)
        RSTD = P.sb("RSTD", [128, 512]); TMP = [P.sb(f"TMP{i}", [128, 512]) for i in range(4)]
        WBUF = [P.sb(f"WBUF{i}", [128, 4096], BF16) for i in range(3)]
        wn = [0]

        def load_w(wbf, name, r0, rows, c0, cols):
            w = WBUF[wn[0] % 3]
            wn[0] += 1
            kcn = max(1, rows // 128)
            pr = min(rows, 128)
            view = w.t[0:pr, 0:kcn * cols].rearrange("p (kc n) -> p kc n", kc=kcn)
            if rows >= 128:
                src = wbf[r0:r0 + rows, c0:c0 + cols].rearrange("(kc p) n -> p kc n", p=128)
            else:
                src = wbf[r0:r0 + rows, c0:c0 + cols].rearrange("(kc p) n -> p kc n", kc=1)
            dma(view, src, reads=[WB[name]], writes=[w.b()])
            return w, view

        def load_x(ti, xt):
            c0, n, _ = TILES[ti]
            dma(xt[:, :, 0:n], xT_in[:, c0:c0 + n].rearrange("(kc p) t -> p kc t", p=128), writes=[xt.b()])
            dma(cosT[:, 0:n], cos_in[:, c0:c0 + n], writes=[cosT.b()])
            dma(sinT[:, 0:n], sin_in[:, c0:c0 + n], writes=[sinT.b()])

        def rstd_from(ps_t, n, scale, out_t, rows=128):
            op("act", lambda e: e.activation(out=out_t[0:rows, 0:n], in_=ps_t[0:rows, 0:n], func=AF.Sqrt, bias=P.V("eps", rows=rows), scale=scale),
               [ps_t.b(), vec.b()], [out_t.b()])
            op("dve", lambda e: e.reciprocal(out=out_t[0:rows, 0:n], in_=out_t[0:rows, 0:n]), [out_t.b()], [out_t.b()])

        def norm_mod(xt, n, s, A, shc, ht):
            for kc in range(8):
                op("act", lambda e, kc=kc: e.activation(out=SQ[:, kc, 0:n], in_=xt[:, kc, 0:n], func=AF.Square), [xt.b()], [SQ.b()])
            for kc in range(8):
                op("pe", lambda e, kc=kc: e.matmul(PS[6][:, 0:n], lhsT=ONESb, rhs=SQ[:, kc, 0:n], start=(kc == 0), stop=(kc == 7)),
                   [SQ.b(), cmb.b()], [PS[6].b()])
            rstd_from(PS[6], n, 1.0 / D, RSTD)
            for kc in range(8):
                tm = TMP[kc % 2]
                op("dve", lambda e, kc=kc, tm=tm: e.scalar_tensor_tensor(out=tm[:, 0:n], in0=xt[:, kc, 0:n], scalar=A[:, kc, s:s + 1], in1=RSTD[:, 0:n],
                                                                       op0=ALU.mult, op1=ALU.mult), [xt.b(), RSTD.b(), A.b()], [tm.b()])
                op("act", lambda e, kc=kc, tm=tm: e.activation(out=ht[:, kc, 0:n], in_=tm[:, 0:n], func=AF.Identity, bias=MOD[:, shc + kc, s:s + 1]),
                   [tm.b(), MOD.b()], [ht.b()])

        def lin_fm(pst, n, wview, c0, m, rhs_t, rhs_view, kcn, m0=0, extra_reads=()):
            for kc in range(kcn):
                op("pe", lambda e, kc=kc: e.matmul(pst[m0:m0 + m, 0:n], lhsT=wview[:, kc, c0:c0 + m], rhs=rhs_view[:, kc, 0:n],
                                                  start=(kc == 0), stop=(kc == kcn - 1)), [rhs_t.b()] + list(extra_reads), [pst.b()])

        def rope(x_t, x_view, rows, c0, n, out_view, out_t, pst):
            op("pe", lambda e: e.matmul(pst[0:rows, 0:n], lhsT=RMf[0:rows, 0:rows], rhs=x_view, start=True, stop=True), [x_t.b(), cmf.b()], [pst.b()])
            t1, t2 = TMP[2], TMP[3]
            op("pool", lambda e: e.tensor_tensor(out=t1[0:rows, 0:n], in0=x_view, in1=cosT[0:rows, 0:n], op=ALU.mult), [x_t.b(), cosT.b()], [t1.b()])
            op("dve", lambda e: e.tensor_tensor(out=t2[0:rows, 0:n], in0=pst[0:rows, 0:n], in1=sinT[0:rows, 0:n], op=ALU.mult), [pst.b(), sinT.b()], [t2.b()])
            op("dve", lambda e: e.tensor_tensor(out=out_view, in0=t1[0:rows, 0:n], in1=t2[0:rows, 0:n], op=ALU.add), [t1.b(), t2.b()], [out_t.b()])

        NH = 4
        KD = 128 if even else 64
        S = [[P.sb(f"S{h}_{i}", [128, 128]) for i in range(2)] for h in range(NH)]
        scur = [0] * NH
        QF = P.sb("QF", [128, 512]); KF = P.sb("KF", [128, 512]); LF = P.sb("LF", [128, 512])
        Bc = P.sb("Bc", [128, 512]); Pc = P.sb("Pc", [128, 512]); Dk = P.sb("Dk", [128, 512]); Db = P.sb("Db", [128, 512])
        Ek = P.sb("Ek", [128, 512]); Eq = P.sb("Eq", [128, 512]); Eb = P.sb("Eb", [128, 512])
        KH = P.sb("KH", [128, 512], BF16); QH = P.sb("QH", [128, 512], BF16); QTL = P.sb("QTL", [128, 512])
        KHT = P.sb("KHT", [128, 4, 128], BF16); ATM = P.sb("ATM", [128, 128], BF16)
        VT = P.sb("VT", [128, 4, 512], BF16)
        OACC = P.sb("OACC", [128, 4, TPC], BF16) if phase == "B" else None
        LTOT = P.sb("LTOT", [128, 8])

        def decay_factors(n, d):
            nch = n // 32
            op("dve", lambda e: e.tensor_tensor_scan(out=Bc[:, 0:n], data0=LF[:, 0:n], data1=LF[:, 0:n], initial=0.0, op0=ALU.add, op1=ALU.bypass),
               [LF.b()], [Bc.b()])
            op("pool", lambda e: e.tensor_tensor(out=Pc[:, 0:n], in0=Bc[:, 0:n], in1=LF[:, 0:n], op=ALU.subtract), [Bc.b(), LF.b()], [Pc.b()])
            v3 = lambda t: t[:, 0:n].rearrange("p (c t) -> p c t", t=32)
            bend = v3(Bc)[:, :, 31:32].to_broadcast([128, nch, 32])
            pst = v3(Pc)[:, :, 0:1].to_broadcast([128, nch, 32])
            if d == 0:
                op("dve", lambda e: e.tensor_tensor(out=v3(Dk), in0=bend, in1=v3(Bc), op=ALU.subtract), [Bc.b()], [Dk.b()])
                op("dve", lambda e: e.tensor_tensor(out=v3(Db), in0=v3(Bc), in1=pst, op=ALU.subtract), [Bc.b(), Pc.b()], [Db.b()])
            else:
                op("dve", lambda e: e.tensor_tensor(out=v3(Dk), in0=v3(Pc), in1=pst, op=ALU.subtract), [Pc.b()], [Dk.b()])
                op("dve", lambda e: e.tensor_tensor(out=v3(Db), in0=bend, in1=v3(Pc), op=ALU.subtract), [Bc.b(), Pc.b()], [Db.b()])
            op("pool", lambda e: e.tensor_scalar(out=Dk[:, 0:n], in0=Dk[:, 0:n], scalar1=-80.0, scalar2=None, op0=ALU.max), [Dk.b()], [Dk.b()])
            op("act", lambda e: e.activation(out=Ek[:, 0:n], in_=Dk[:, 0:n], func=AF.Exp), [Dk.b()], [Ek.b()])
            op("act", lambda e: e.activation(out=Eq[:, 0:n], in_=Dk[:, 0:n], func=AF.Exp, scale=-1.0), [Dk.b()], [Eq.b()])
            op("act", lambda e: e.activation(out=Eb[:, 0:n], in_=Db[:, 0:n], func=AF.Exp), [Db.b()], [Eb.b()])

        def rec_chunk_tile(heads, c0, n, d, want_out):
            op("pool", lambda e: e.tensor_tensor(out=KH[:, 0:n], in0=KF[:, 0:n], in1=Ek[:, 0:n], op=ALU.mult), [KF.b(), Ek.b()], [KH.b()])
            if want_out:
                op("pool", lambda e: e.tensor_tensor(out=QH[:, 0:n], in0=QF[:, 0:n], in1=Eq[:, 0:n], op=ALU.mult), [QF.b(), Eq.b()], [QH.b()])
                op("dve", lambda e: e.tensor_tensor(out=QTL[:, 0:n], in0=QF[:, 0:n], in1=Eb[:, 0:n], op=ALU.mult), [QF.b(), Eb.b()], [QTL.b()])
            nst = n // 128
            order = range(nst) if d == 0 else range(nst - 1, -1, -1)
            for sti in order:
                t0 = sti * 128
                op("pe", lambda e, t0=t0: e.transpose(PSB[:, 0:128], KH[:, t0:t0 + 128], IDb), [KH.b(), cmb.b()], [PSB.b()])
                for cm in range(4):
                    op("act", lambda e, cm=cm: e.activation(out=KHT[:, cm, :], in_=PSB[:, 0:128], func=AF.Copy, scale=P.V("rmask", cm)), [PSB.b(), vec.b()], [KHT.b()])
                for (h, pb) in heads:
                    if DBG == 95: break
                    rows = slice(pb, pb + KD)
                    if want_out:
                        op("pe", lambda e, t0=t0, rows=rows: e.matmul(PS[4][:, 0:128], lhsT=KH[rows, t0:t0 + 128], rhs=QH[rows, t0:t0 + 128], start=True, stop=True),
                           [KH.b(), QH.b()], [PS[4].b()])
                        op("dve", lambda e: e.tensor_tensor(out=ATM[:], in0=PS[4][:, 0:128], in1=MASK[d], op=ALU.mult), [PS[4].b(), cmf.b()], [ATM.b()])
                        op("pe", lambda e, h=h, sti=sti: e.matmul(PS[5][:, 0:128], lhsT=VT[:, sti, h * 128:(h + 1) * 128], rhs=ATM[:], start=True, stop=False),
                           [VT.b(), ATM.b()], [PS[5].b()])
                    corder = range(4) if d == 0 else range(3, -1, -1)
                    for ci, c in enumerate(corder):
                        sp = S[h][scur[h]]; sn = S[h][1 - scur[h]]
                        cc = t0 + c * 32
                        if want_out:
                            op("pe", lambda e, sp=sp, rows=rows, cc=cc, c=c, ci=ci: e.matmul(PS[5][:, c * 32:(c + 1) * 32], lhsT=sp[rows, :], rhs=QTL[rows, cc:cc + 32],
                                                                                    start=False, stop=(ci == 3)), [sp.b(), QTL.b()], [PS[5].b()])
                        op("pe", lambda e, c=c, h=h, sti=sti: e.matmul(PS[3][:, 0:128], lhsT=KHT[:, c, :], rhs=VT[:, sti, h * 128:(h + 1) * 128],
                                                                       start=True, stop=True), [KHT.b(), VT.b()], [PS[3].b()])
                        ecol = cc + 31 if d == 0 else cc
                        if DBG == 96: continue
                        op("dve", lambda e, sp=sp, sn=sn, rows=rows, ecol=ecol: e.scalar_tensor_tensor(out=sn[rows, :], in0=sp[rows, :], scalar=Eb[rows, ecol:ecol + 1],
                                                                                                     in1=PS[3][rows, 0:128], op0=ALU.mult, op1=ALU.add),
                           [sp.b(), Eb.b(), PS[3].b()], [sn.b()])
                        scur[h] = 1 - scur[h]
                    if want_out:
                        oc = OACC[:, h, c0 + t0:c0 + t0 + 128]
                        if d == 0:
                            op("act", lambda e, oc=oc: e.copy(out=oc, in_=PS[5][:, 0:128]), [PS[5].b()], [OACC.b((h, c0))])
                        else:
                            op("dve", lambda e, oc=oc: e.tensor_tensor(out=oc, in0=PS[5][:, 0:128], in1=oc, op=ALU.add), [PS[5].b()], [OACC.b((h, c0))])

        def zero_states():
            for h in range(NH):
                op("pool", lambda e, h=h: e.memset(S[h][scur[h]][:], 0.0), [], [S[h][scur[h]].b()])

        def v_token_major(ht, n, wname_cols):
            c0w, = wname_cols
            w, wv = load_w(win_bf, "win", 0, D, c0w, 512)
            for sti in range(n // 128):
                for kc in range(8):
                    op("pe", lambda e, kc=kc, sti=sti: e.matmul(PS[2][:, 0:512], lhsT=ht[:, kc, sti * 128:(sti + 1) * 128], rhs=wv[:, kc, :],
                                                               start=(kc == 0), stop=(kc == 7)), [ht.b(), w.b()], [PS[2].b()])
                op("act", lambda e, sti=sti: e.copy(out=VT[:, sti, :], in_=PS[2][:, 0:512]), [PS[2].b()], [VT.b()])

        LB = P.sb("LB", [128, 8]); OMLB = P.sb("OMLB", [128, 8]); LG = P.sb("LG", [128, 4])
        if even:
            op("dve", lambda e: e.tensor_tensor(out=LB[:], in0=P.V("lb1", 0, 8), in1=P.V("lb0", 0, 8), op=ALU.subtract), [vec.b()], [LB.b()])
            op("act", lambda e: e.activation(out=LB[:], in_=LB[:], func=AF.Sigmoid), [LB.b()], [LB.b()])
            op("dve", lambda e: e.tensor_scalar(out=LB[:], in0=LB[:], scalar1=P.V("lbsel"), scalar2=None, op0=ALU.mult), [LB.b(), vec.b()], [LB.b()])
            op("dve", lambda e: e.tensor_scalar(out=OMLB[:], in0=LB[:], scalar1=-1.0, scalar2=1.0, op0=ALU.mult, op1=ALU.add), [LB.b()], [OMLB.b()])
        else:
            op("act", lambda e: e.activation(out=LG[:], in_=P.V("rtd", 0, 4), func=AF.Sigmoid), [vec.b()], [LG.b()])
            op("act", lambda e: e.activation(out=LG[:], in_=LG[:], func=AF.Ln), [LG.b()], [LG.b()])
        CB += [LB.b(), OMLB.b(), LG.b()]

        def rec_feature_chunk(ht, n, c0, fc, d, want_q, wk, wq):
            if even:
                w, wv = wk
                lin_fm(PS[0], n, wv, fc * 128, 128, ht, ht.t, 8, extra_reads=[w.b()])
                sg = TMP[0]
                op("act", lambda e: e.activation(out=sg[:, 0:n], in_=PS[0][:, 0:n], func=AF.Sigmoid), [PS[0].b()], [sg.b()])
                col = d * 4 + fc
                op("dve", lambda e: e.tensor_scalar(out=sg[:, 0:n], in0=sg[:, 0:n], scalar1=OMLB[:, col:col + 1], scalar2=LB[:, col:col + 1], op0=ALU.mult, op1=ALU.add),
                   [sg.b(), LB.b(), OMLB.b()], [sg.b()])
                op("pool", lambda e: e.tensor_scalar(out=KF[:, 0:n], in0=sg[:, 0:n], scalar1=-1.0, scalar2=1.0, op0=ALU.mult, op1=ALU.add), [sg.b()], [KF.b()])
                op("act", lambda e: e.activation(out=LF[:, 0:n], in_=sg[:, 0:n], func=AF.Ln), [sg.b()], [LF.b()])
                if want_q:
                    w, wv = wq
                    lin_fm(PS[1], n, wv, fc * 128, 128, ht, ht.t, 8, extra_reads=[w.b()])
                    op("act", lambda e: e.activation(out=QF[:, 0:n], in_=PS[1][:, 0:n], func=AF.Silu), [PS[1].b()], [QF.b()])
            else:
                w, wv = wk
                lin_fm(PS[0], n, wv, fc * 128, 128, ht, ht.t, 8, extra_reads=[w.b()])
                kx = TMP[0]
                op("act", lambda e: e.activation(out=kx[:, 0:n], in_=PS[0][:, 0:n], func=AF.Copy, scale=0.125), [PS[0].b()], [kx.b()])
                rope(kx, kx[:, 0:n], 128, c0, n, KF[:, 0:n], KF, PS[1])
                op("pool", lambda e: e.memset(LF[:, 0:n], 1.0), [], [LF.b()])
                col = d * 2 + fc
                op("dve", lambda e: e.tensor_scalar(out=LF[:, 0:n], in0=LF[:, 0:n], scalar1=LG[:, col:col + 1], scalar2=None, op0=ALU.mult), [LF.b(), LG.b()], [LF.b()])
                if want_q:
                    w, wv = wq
                    lin_fm(PS[0], n, wv, fc * 128, 128, ht, ht.t, 8, extra_reads=[w.b()])
                    qx = TMP[1]
                    op("act", lambda e: e.copy(out=qx[:, 0:n], in_=PS[0][:, 0:n]), [PS[0].b()], [qx.b()])
                    rope(qx, qx[:, 0:n], 128, c0, n, QF[:, 0:n], QF, PS[1])

        NFC = 4 if even else 2
        VCOL = 1536 if even else 2048

        def heads_of(fc):
            return [(fc, 0)] if even else [(2 * fc, 0), (2 * fc + 1, 64)]

        def rec_tile(ht, ti, d, want_out):
            c0, n, _ = TILES[ti]
            v_token_major(ht, n, (VCOL,))
            if even:
                wk = load_w(win_bf, "win", 0, D, 512 + d * 512, 512)
                wq = load_w(win_bf, "win", 0, D, 0, 512) if want_out else None
            else:
                wk = load_w(win_bf, "win", 0, D, 1792, 256)
                wq = load_w(win_bf, "win", 0, D, 1536, 256) if want_out else None
            for fc in range(NFC):
                if DBG == 91: break
                rec_feature_chunk(ht, n, c0, fc, d, want_out, wk, wq)
                if DBG == 94: continue
                decay_factors(n, d)
                if DBG == 92: continue
                if phase == "A":
                    col = d * 4 + fc
                    tot = Bc[:, n - 1:n]
                    op("dve", lambda e, col=col, tot=tot: e.tensor_tensor(out=LTOT[:, col:col + 1], in0=LTOT[:, col:col + 1], in1=tot, op=ALU.add),
                       [Bc.b(), LTOT.b()], [LTOT.b()])
                rec_chunk_tile(heads_of(fc), c0, n, d, want_out)

        if phase == "A":
            KNs = P.sb("KNs", [128, 4, 512], BF16); KRs = P.sb("KRs", [128, 4, 512], BF16)
            CKV = P.sb("CKV", [128, 2, 512]); CKVN = P.sb("CKVN", [128, 2, 512], BF16)
            KRP = P.sb("KRP", [128, 512]); KRR = P.sb("KRR", [128, 512]); KNF = P.sb("KNF", [128, 512])
            VS = P.sb("VS", [128, 4, 512], BF16)
            op("pool", lambda e: e.memset(LTOT[:], 0.0), [], [LTOT.b()])

            def kv_even(ht, ti):
                c0, n, _ = TILES[ti]
                w, wv = load_w(win_bf, "win", 0, D, 2944, 320)
                for cc in range(2):
                    lin_fm(PS[0], n, wv, cc * 128, 128, ht, ht.t, 8, extra_reads=[w.b()])
                    op("act", lambda e, cc=cc: e.copy(out=CKV[:, cc, 0:n], in_=PS[0][:, 0:n]), [PS[0].b()], [CKV.b()])
                    op("pool", lambda e, cc=cc: e.tensor_tensor(out=SQ[:, cc, 0:n], in0=CKV[:, cc, 0:n], in1=CKV[:, cc, 0:n], op=ALU.mult), [CKV.b()], [SQ.b()])
                if DBG == 31: return
                lin_fm(PS[1], n, wv, 256, 64, ht, ht.t, 8, extra_reads=[w.b()])
                if DBG == 311: return
                raw = TMP[1]
                op("act", lambda e: e.copy(out=raw[0:64, 0:n], in_=PS[1][0:64, 0:n]), [PS[1].b()], [raw.b()])
                op("dve", lambda e: e.tensor_scalar(out=KRP[0:64, 0:n], in0=raw[0:64, 0:n], scalar1=P.V("qkk_r", rows=64), scalar2=None, op0=ALU.mult),
                   [raw.b(), vec.b()], [KRP.b()])
                if DBG == 312: return
                op("pool", lambda e: e.tensor_tensor(out=SQ[0:64, 2, 0:n], in0=raw[0:64, 0:n], in1=raw[0:64, 0:n], op=ALU.mult), [raw.b()], [SQ.b()])
                if DBG == 32: return
                for cc in range(2):
                    op("pe", lambda e, cc=cc: e.matmul(PS[6][:, 0:n], lhsT=ONESb, rhs=SQ[:, cc, 0:n], start=(cc == 0), stop=(cc == 1)), [SQ.b(), cmb.b()], [PS[6].b()])
                rstd_from(PS[6], n, 1.0 / 256, RSTD)
                for cc in range(2):
                    op("dve", lambda e, cc=cc: e.scalar_tensor_tensor(out=CKVN[:, cc, 0:n], in0=CKV[:, cc, 0:n], scalar=P.V("kvn", cc), in1=RSTD[:, 0:n], op0=ALU.mult, op1=ALU.mult),
                       [CKV.b(), RSTD.b(), vec.b()], [CKVN.b()])
                if DBG == 33: return
                rope(KRP, KRP[0:64, 0:n], 64, c0, n, KRR[0:64, 0:n], KRR, PS[1])
                if DBG == 34: return
                w2, w2v = load_w(wukv_bf, "wukv", 0, 256, 0, 1024)
                for sti in range(n // 128):
                    for hh in range(4):
                        for kc in range(2):
                            op("pe", lambda e, kc=kc, sti=sti, hh=hh: e.matmul(PS[2][:, hh * 128:(hh + 1) * 128], lhsT=CKVN[:, kc, sti * 128:(sti + 1) * 128],
                                                                       rhs=w2v[:, kc, hh * 256 + 128:hh * 256 + 256],
                                                                       start=(kc == 0), stop=(kc == 1)), [CKVN.b(), w2.b()], [PS[2].b()])
                    op("act", lambda e, sti=sti: e.copy(out=VS[:, sti, :], in_=PS[2][:, 0:512]), [PS[2].b()], [VS.b()])
                    tb = (c0 + sti * 128) // 128
                    dma(v_out[tb], VS[:, sti, :], reads=[VS.b()], writes=[v_outb])
                if DBG == 35: return
                for h in range(4):
                    lin_fm(PS[0], n, w2v, h * 256, 128, CKVN, CKVN.t, 2, extra_reads=[w2.b()])
                    op("act", lambda e: e.copy(out=KNF[:, 0:n], in_=PS[0][:, 0:n]), [PS[0].b()], [KNF.b()])
                    op("pool", lambda e: e.tensor_tensor(out=SQ[:, 3, 0:n], in0=KNF[:, 0:n], in1=KNF[:, 0:n], op=ALU.mult), [KNF.b()], [SQ.b()])
                    op("pe", lambda e: e.matmul(PS[6][:, 0:n], lhsT=ONESb, rhs=SQ[:, 3, 0:n], start=True, stop=False), [SQ.b(), cmb.b()], [PS[6].b()])
                    op("pe", lambda e: e.matmul(PS[6][:, 0:n], lhsT=ONESb[0:64, :], rhs=SQ[0:64, 2, 0:n], start=False, stop=True), [SQ.b(), cmb.b()], [PS[6].b()])
                    rstd_from(PS[6], n, 1.0 / 192, RSTD)
                    op("dve", lambda e, h=h: e.scalar_tensor_tensor(out=KNs[:, h, 0:n], in0=KNF[:, 0:n], scalar=P.V("qkk_n"), in1=RSTD[:, 0:n], op0=ALU.mult, op1=ALU.mult),
                       [KNF.b(), RSTD.b(), vec.b()], [KNs.b()])
                    op("dve", lambda e, h=h: e.tensor_tensor(out=KRs[0:64, h, 0:n], in0=KRR[0:64, 0:n], in1=RSTD[0:64, 0:n], op=ALU.mult), [KRR.b(), RSTD.b()], [KRs.b()])
                for h in range(4):
                    dma(kn_out[h, :, c0:c0 + n], KNs[:, h, 0:n], reads=[KNs.b()], writes=[v_outb])
                    dma(kr_out[h, :, c0:c0 + n], KRs[0:64, h, 0:n], reads=[KRs.b()], writes=[v_outb])

            def kv_odd(ht, ti):
                c0, n, _ = TILES[ti]
                w, wv = load_w(win_bf, "win", 0, D, 512, 512)
                for h in range(4):
                    lin_fm(PS[0], n, wv, h * 128, 128, ht, ht.t, 8, extra_reads=[w.b()])
                    op("act", lambda e: e.copy(out=KNF[:, 0:n], in_=PS[0][:, 0:n]), [PS[0].b()], [KNF.b()])
                    op("pool", lambda e: e.tensor_tensor(out=SQ[:, 3, 0:n], in0=KNF[:, 0:n], in1=KNF[:, 0:n], op=ALU.mult), [KNF.b()], [SQ.b()])
                    op("pe", lambda e: e.matmul(PS[6][:, 0:n], lhsT=BDb, rhs=SQ[:, 3, 0:n], start=True, stop=True), [SQ.b(), cmb.b()], [PS[6].b()])
                    rstd_from(PS[6], n, 1.0 / 64, RSTD)
                    op("dve", lambda e: e.scalar_tensor_tensor(out=KRP[:, 0:n], in0=KNF[:, 0:n], scalar=P.V("dak"), in1=RSTD[:, 0:n], op0=ALU.mult, op1=ALU.mult),
                       [KNF.b(), RSTD.b(), vec.b()], [KRP.b()])
                    rope(KRP, KRP[:, 0:n], 128, c0, n, KNs[:, h, 0:n], KNs, PS[1])
                for h in range(4):
                    dma(kn_out[h, :, c0:c0 + n], KNs[:, h, 0:n], reads=[KNs.b()], writes=[v_outb])
                w, wv = load_w(win_bf, "win", 0, D, 1024, 512)
                for sti in range(n // 128):
                    for kc in range(8):
                        op("pe", lambda e, kc=kc, sti=sti: e.matmul(PS[2][:, 0:512], lhsT=ht[:, kc, sti * 128:(sti + 1) * 128], rhs=wv[:, kc, :],
                                                                   start=(kc == 0), stop=(kc == 7)), [ht.b(), w.b()], [PS[2].b()])
                    op("act", lambda e, sti=sti: e.copy(out=VS[:, sti, :], in_=PS[2][:, 0:512]), [PS[2].b()], [VS.b()])
                    tb = (c0 + sti * 128) // 128
                    dma(v_out[tb], VS[:, sti, :], reads=[VS.b()], writes=[v_outb])

            v_outb = Buf()
            zero_states()
            for ti in range(5 if DBG != 1 else 0):
                xt = XT[ti % 2]
                load_x(ti, xt)
                c0, n, isc = TILES[ti]
                norm_mod(xt, n, isc, A1, 0, HT)
                if DBG != 2:
                    (kv_even if even else kv_odd)(HT, ti)
                if not isc and DBG not in (2, 3, 31, 32, 33, 34, 35, 311, 312):
                    rec_tile(HT, ti, 0, False)
            for h in range(4):
                sp = S[h][scur[h]]
                dma(sums_out[0, h], sp[:], reads=[sp.b()], writes=[v_outb])
            zero_states()
            for ti in ((4, 3, 2, 1) if DBG in (0, 9, 91, 92, 93, 94, 95, 96) else ()):
                xt = XT[ti % 2]
                load_x(ti, xt)
                c0, n, isc = TILES[ti]
                norm_mod(xt, n, isc, A1, 0, HT)
                rec_tile(HT, ti, 1, False)
            for h in range(4):
                sp = S[h][scur[h]]
                dma(sums_out[1, h], sp[:], reads=[sp.b()], writes=[v_outb])
            op("act", lambda e: e.activation(out=LTOT[:], in_=LTOT[:], func=AF.Exp), [LTOT.b()], [LTOT.b()])
            dma(sumd_out, LTOT[:], reads=[LTOT.b()], writes=[v_outb])
            k.finish([v_outb])
            P.n_instr = k.n
            return P

        xob = Buf()
        SJ = [P.sb(f"SJ{i}", [128, 128]) for i in range(2)]
        SD = P.sb("SD", [128, NCORES * 8]); MJ = P.sb("MJ", [128, 1])
        dma(SD[:], sumd_in, writes=[SD.b()])
        GT = P.sb("GT", [128, 4, 512], BF16)
        YT = P.sb("YT", [128, 8, 512], BF16)
        QN = P.sb("QN", [128, 4, 512], BF16); QR = P.sb("QR", [128, 4, 512], BF16)
        KNb = [P.sb(f"KNb{i}", [128, 1280], BF16) for i in range(2)]
        KRb = [P.sb(f"KRb{i}", [128, 1280], BF16) for i in range(2)]
        Vb = [P.sb(f"Vb{i}", [128, 10, 128], BF16) for i in range(2)]
        PT = [P.sb(f"PT{i}", [128, 512], BF16) for i in range(2)]
        AO = [P.sb(f"AO{i}", [128, 512]) for i in range(2)]
        HID = [P.sb("HID0", [128, 4, 512], BF16)] * 2
        CQ = P.sb("CQ", [128, 3, 512]); CQN = P.sb("CQN", [128, 3, 512], BF16)
        QX = P.sb("QX", [128, 512]); QY = P.sb("QY", [128, 512])
        NLAM = P.sb("NLAM", [128, 1])
        sbn = [0]

        def fold(d):
            js = range(NCORES) if d == 0 else range(NCORES - 1, -1, -1)
            fl, nfl = ("flf", "nflf") if d == 0 else ("flb", "nflb")
            for j in js:
                for h in range(4):
                    sj = SJ[sbn[0] % 2]
                    sbn[0] += 1
                    dma(sj[:], sums_in[j, d, h], writes=[sj.b()])
                    fc, pb = (h, 0) if even else (h // 2, 64 * (h % 2))
                    rows = slice(pb, pb + KD)
                    col = j * 8 + d * 4 + fc
                    op("dve", lambda e, col=col, j=j: e.tensor_scalar(out=MJ[:], in0=SD[:, col:col + 1], scalar1=P.V(fl, j), scalar2=P.V(nfl, j), op0=ALU.mult, op1=ALU.add),
                       [SD.b(), vec.b()], [MJ.b()])
                    op("pool", lambda e, sj=sj, j=j: e.tensor_scalar(out=sj[:], in0=sj[:], scalar1=P.V(fl, j), scalar2=None, op0=ALU.mult), [sj.b(), vec.b()], [sj.b()])
                    sp = S[h][scur[h]]; sn = S[h][1 - scur[h]]
                    op("dve", lambda e, sp=sp, sn=sn, sj=sj, rows=rows: e.scalar_tensor_tensor(out=sn[rows, :], in0=sp[rows, :], scalar=MJ[rows, 0:1], in1=sj[rows, :],
                                                                                          op0=ALU.mult, op1=ALU.add), [sp.b(), sj.b(), MJ.b()], [sn.b()])
                    scur[h] = 1 - scur[h]

        def attention(parts, h, n, out_t, kbs, scale):
            nsb = (kbs + 9) // 10
            sbinfo = []

            def load_sb(sbi):
                kb0 = sbi * 10
                nk = min(10, kbs - kb0)
                bi = sbn[0] % 2
                sbn[0] += 1
                for (qv, ksrc, r0, nr, kbuf) in parts:
                    dma(kbuf[bi][r0:r0 + nr, 0:nk * 128], ksrc[:, kb0 * 128:(kb0 + nk) * 128], writes=[kbuf[bi].b()])
                dma(Vb[bi][:, 0:nk, :], v_in[h, :, kb0:kb0 + nk, :], writes=[Vb[bi].b()])
                sbinfo.append(bi)

            blocks = [(sbi, kb) for sbi in range(nsb) for kb in range(min(10, kbs - sbi * 10))]
            nb = len(blocks)

            def qk(i):
                sbi, kb = blocks[i]
                bi = sbinfo[sbi]
                pss = PS[i % 2]
                for pi, (qv, ksrc, r0, nr, kbuf) in enumerate(parts):
                    op("pe", lambda e, qv=qv, r0=r0, nr=nr, kbuf=kbuf, kb=kb, pi=pi, pss=pss, bi=bi: e.matmul(
                        pss[:, 0:n], lhsT=kbuf[bi][r0:r0 + nr, kb * 128:(kb + 1) * 128], rhs=qv, start=(pi == 0), stop=(pi == len(parts) - 1)),
                       [kbuf[bi].b(), QN.b(), QR.b()], [pss.b()])

            load_sb(0)
            if nsb > 1:
                load_sb(1)
            qk(0)
            for i, (sbi, kb) in enumerate(blocks):
                if kb == 0 and sbi >= 1 and sbi + 1 < nsb:
                    load_sb(sbi + 1)
                if i + 1 < nb:
                    qk(i + 1)
                bi = sbinfo[sbi]
                pss = PS[i % 2]
                pt = PT[i % 2]
                op("act", lambda e, pss=pss, pt=pt: e.activation(out=pt[:, 0:n], in_=pss[:, 0:n], func=AF.Exp, scale=scale), [pss.b()], [pt.b()])
                first = (i == 0)
                last = (i == nb - 1)
                op("pe", lambda e, kb=kb, pt=pt, last=last, first=first, bi=bi: e.matmul(PS[2][:, 0:n], lhsT=Vb[bi][:, kb, :], rhs=pt[:, 0:n], start=first, stop=last),
                   [Vb[bi].b(), pt.b()], [PS[2].b()])
                op("pe", lambda e, pt=pt, last=last, first=first: e.matmul(PS[3][:, 0:n], lhsT=ONESb, rhs=pt[:, 0:n], start=first, stop=last),
                   [pt.b(), cmb.b()], [PS[3].b()])
            rc = TMP[0]
            op("dve", lambda e: e.reciprocal(out=rc[:, 0:n], in_=PS[3][:, 0:n]), [PS[3].b()], [rc.b()])
            op("dve", lambda e: e.tensor_tensor(out=out_t[:, 0:n], in0=PS[2][:, 0:n], in1=rc[:, 0:n], op=ALU.mult), [PS[2].b(), rc.b()], [out_t.b()])

        def head_rms_gate(src_view, n, gain_ap, gate_view, out_view, src_b, out_b, lhs_ones, inv_dim, extra_scale=None):
            sqv = SQ[:, 4, 0:n]
            op("pool", lambda e: e.tensor_tensor(out=sqv, in0=src_view, in1=src_view, op=ALU.mult), [src_b], [SQ.b()])
            op("pe", lambda e: e.matmul(PS[6][:, 0:n], lhsT=lhs_ones, rhs=sqv, start=True, stop=True), [SQ.b(), cmb.b()], [PS[6].b()])
            rstd_from(PS[6], n, inv_dim, RSTD)
            t = TMP[1]
            op("dve", lambda e: e.scalar_tensor_tensor(out=t[:, 0:n], in0=src_view, scalar=gain_ap, in1=RSTD[:, 0:n], op0=ALU.mult, op1=ALU.mult),
               [src_b, RSTD.b(), vec.b()], [t.b()])
            if gate_view is not None:
                op("dve", lambda e: e.tensor_tensor(out=out_view, in0=t[:, 0:n], in1=gate_view, op=ALU.mult), [t.b(), GT.b()], [out_b])
            else:
                op("dve", lambda e: e.tensor_scalar(out=out_view, in0=t[:, 0:n], scalar1=extra_scale, scalar2=None, op0=ALU.mult), [t.b(), vec.b()], [out_b])

        if not even:
            lp = P.sb("lp", [128, 2])
            op("dve", lambda e: e.tensor_tensor(out=lp[:, 0:1], in0=P.V("lam", 0), in1=P.V("lam", 1), op=ALU.mult), [vec.b()], [lp.b()])
            op("dve", lambda e: e.tensor_tensor(out=lp[:, 1:2], in0=P.V("lam", 2), in1=P.V("lam", 3), op=ALU.mult), [vec.b()], [lp.b()])
            op("pe", lambda e: e.matmul(PS[6][:, 0:2], lhsT=ONESf, rhs=lp[:], start=True, stop=True), [lp.b(), cmf.b()], [PS[6].b()])
            op("act", lambda e: e.activation(out=lp[:], in_=PS[6][:, 0:2], func=AF.Exp), [PS[6].b()], [lp.b()])
            op("dve", lambda e: e.tensor_tensor(out=NLAM[:], in0=lp[:, 1:2], in1=lp[:, 0:1], op=ALU.subtract), [lp.b()], [NLAM.b()])
            op("dve", lambda e: e.tensor_tensor(out=NLAM[:], in0=NLAM[:], in1=P.V("laminit"), op=ALU.subtract), [NLAM.b(), vec.b()], [NLAM.b()])

        def mixer_attn_even(ht, ti):
            c0, n, isc = TILES[ti]
            kbs = 2 if isc else NKB
            w, wv = load_w(win_bf, "win", 0, D, 2560, 384)
            for cc in range(3):
                lin_fm(PS[0], n, wv, cc * 128, 128, ht, ht.t, 8, extra_reads=[w.b()])
                op("act", lambda e, cc=cc: e.copy(out=CQ[:, cc, 0:n], in_=PS[0][:, 0:n]), [PS[0].b()], [CQ.b()])
                op("pool", lambda e, cc=cc: e.tensor_tensor(out=SQ[:, cc, 0:n], in0=CQ[:, cc, 0:n], in1=CQ[:, cc, 0:n], op=ALU.mult), [CQ.b()], [SQ.b()])
            for cc in range(3):
                op("pe", lambda e, cc=cc: e.matmul(PS[6][:, 0:n], lhsT=ONESb, rhs=SQ[:, cc, 0:n], start=(cc == 0), stop=(cc == 2)), [SQ.b(), cmb.b()], [PS[6].b()])
            rstd_from(PS[6], n, 1.0 / 384, RSTD)
            for cc in range(3):
                op("dve", lambda e, cc=cc: e.scalar_tensor_tensor(out=CQN[:, cc, 0:n], in0=CQ[:, cc, 0:n], scalar=P.V("qn", cc), in1=RSTD[:, 0:n], op0=ALU.mult, op1=ALU.mult),
                   [CQ.b(), RSTD.b(), vec.b()], [CQN.b()])
            w2, w2v = load_w(wuq_bf, "wuq", 0, 384, 0, 768)
            for h in range(4):
                lin_fm(PS[0], n, w2v, h * 192, 128, CQN, CQN.t, 3, extra_reads=[w2.b()])
                lin_fm(PS[1], n, w2v, h * 192 + 128, 64, CQN, CQN.t, 3, extra_reads=[w2.b()])
                op("act", lambda e: e.copy(out=QX[:, 0:n], in_=PS[0][:, 0:n]), [PS[0].b()], [QX.b()])
                op("act", lambda e: e.copy(out=QY[0:64, 0:n], in_=PS[1][0:64, 0:n]), [PS[1].b()], [QY.b()])
                op("pool", lambda e: e.tensor_tensor(out=SQ[:, 3, 0:n], in0=QX[:, 0:n], in1=QX[:, 0:n], op=ALU.mult), [QX.b()], [SQ.b()])
                op("pool", lambda e: e.tensor_tensor(out=SQ[0:64, 2, 0:n], in0=QY[0:64, 0:n], in1=QY[0:64, 0:n], op=ALU.mult), [QY.b()], [SQ.b()])
                op("pe", lambda e: e.matmul(PS[6][:, 0:n], lhsT=ONESb, rhs=SQ[:, 3, 0:n], start=True, stop=False), [SQ.b(), cmb.b()], [PS[6].b()])
                op("pe", lambda e: e.matmul(PS[6][:, 0:n], lhsT=ONESb[0:64, :], rhs=SQ[0:64, 2, 0:n], start=False, stop=True), [SQ.b(), cmb.b()], [PS[6].b()])
                rstd_from(PS[6], n, 1.0 / 192, RSTD)
                op("dve", lambda e, h=h: e.scalar_tensor_tensor(out=QN[:, h, 0:n], in0=QX[:, 0:n], scalar=P.V("qkq_n"), in1=RSTD[:, 0:n], op0=ALU.mult, op1=ALU.mult),
                   [QX.b(), RSTD.b(), vec.b()], [QN.b()])
                op("dve", lambda e: e.scalar_tensor_tensor(out=QY[0:64, 0:n], in0=QY[0:64, 0:n], scalar=P.V("qkq_r", rows=64), in1=RSTD[0:64, 0:n], op0=ALU.mult, op1=ALU.mult),
                   [QY.b(), RSTD.b(), vec.b()], [QY.b()])
                rope(QY, QY[0:64, 0:n], 64, c0, n, QR[0:64, h, 0:n], QR, PS[1])
            for h in range(4):
                ao = AO[h % 2]
                attention([(QN[:, h, 0:n], kn_in[h], 0, 128, KNb), (QR[0:64, h, 0:n], kr_in[h], 0, 64, KRb)], h, n, ao, kbs, 192 ** -0.5)
                op("act", lambda e, h=h, ao=ao: e.copy(out=YT[:, 4 + h, 0:n], in_=ao[:, 0:n]), [ao.b()], [YT.b()])

        def mixer_attn_odd(ht, ti):
            c0, n, isc = TILES[ti]
            kbs = 2 if isc else NKB
            w, wv = load_w(win_bf, "win", 0, D, 0, 512)
            for h in range(4):
                lin_fm(PS[0], n, wv, h * 128, 128, ht, ht.t, 8, extra_reads=[w.b()])
                op("act", lambda e: e.copy(out=QX[:, 0:n], in_=PS[0][:, 0:n]), [PS[0].b()], [QX.b()])
                op("pool", lambda e: e.tensor_tensor(out=SQ[:, 3, 0:n], in0=QX[:, 0:n], in1=QX[:, 0:n], op=ALU.mult), [QX.b()], [SQ.b()])
                op("pe", lambda e: e.matmul(PS[6][:, 0:n], lhsT=BDb, rhs=SQ[:, 3, 0:n], start=True, stop=True), [SQ.b(), cmb.b()], [PS[6].b()])
                rstd_from(PS[6], n, 1.0 / 64, RSTD)
                op("dve", lambda e: e.scalar_tensor_tensor(out=QY[:, 0:n], in0=QX[:, 0:n], scalar=P.V("daq"), in1=RSTD[:, 0:n], op0=ALU.mult, op1=ALU.mult),
                   [QX.b(), RSTD.b(), vec.b()], [QY.b()])
                rope(QY, QY[:, 0:n], 128, c0, n, QN[:, h, 0:n], QN, PS[1])
            for h in range(4):
                attention([(QN[0:64, h, 0:n], kn_in[h, 0:64], 0, 64, KNb)], h, n, AO[0], kbs, 0.125)
                attention([(QN[64:128, h, 0:n], kn_in[h, 64:128], 64, 64, KNb)], h, n, AO[1], kbs, 0.125)
                op("dve", lambda e: e.scalar_tensor_tensor(out=AO[0][:, 0:n], in0=AO[1][:, 0:n], scalar=NLAM[:, 0:1], in1=AO[0][:, 0:n], op0=ALU.mult, op1=ALU.add),
                   [AO[0].b(), AO[1].b(), NLAM.b()], [AO[0].b()])
                head_rms_gate(AO[0][:, 0:n], n, P.V("subln"), None, YT[:, h, 0:n], AO[0].b(), YT.b(), ONESb, 1.0 / 128, extra_scale=P.V("omlam"))

        def gate(ht, n):
            gc0 = 2048 if even else 2560
            w, wv = load_w(win_bf, "win", 0, D, gc0, 512)
            for cc in range(4):
                lin_fm(PS[0], n, wv, cc * 128, 128, ht, ht.t, 8, extra_reads=[w.b()])
                op("act", lambda e, cc=cc: e.activation(out=GT[:, cc, 0:n], in_=PS[0][:, 0:n], func=AF.Silu), [PS[0].b()], [GT.b()])

        def finish_tile(xt, ti):
            c0, n, isc = TILES[ti]
            yoff = 0 if even else 4
            for h in range(4):
                src = OACC[:, h, c0:c0 + n]
                op("act", lambda e, src=src: e.copy(out=QX[:, 0:n], in_=src), [OACC.b((h, c0))], [QX.b()])
                head_rms_gate(QX[:, 0:n], n, P.V("hgn" if even else "rtn"), GT[:, h, 0:n], YT[:, yoff + h, 0:n], QX.b(), YT.b(), ONESb, 1.0 / 128)
            for oc in range(8):
                if oc % 4 == 0:
                    w, wv = load_w(wo_bf, "wo", 0, D, oc * 128, 512)
                lin_fm(PS[oc % 2], n, wv, (oc % 4) * 128, 128, YT, YT.t, 8, extra_reads=[w.b()])
                op("dve", lambda e, oc=oc: e.scalar_tensor_tensor(out=xt[:, oc, 0:n], in0=PS[oc % 2][:, 0:n], scalar=MOD[:, 16 + oc, isc:isc + 1], in1=xt[:, oc, 0:n],
                                                                 op0=ALU.mult, op1=ALU.add), [PS[oc % 2].b(), MOD.b(), xt.b()], [xt.b()])
            norm_mod(xt, n, isc, A2, 24, HT)
            for hb in range(8):
                w1, w1v = load_w(w1_bf, "w1", 0, D, hb * 512, 512)
                w2, w2v = load_w(w2_bf, "w2", hb * 512, 512, 0, 1024)
                hid = HID[hb % 2]
                for hc in range(4):
                    lin_fm(PS[hc % 2], n, w1v, hc * 128, 128, HT, HT.t, 8, extra_reads=[w1.b()])
                    r = TMP[hc % 2]
                    op("act", lambda e, hc=hc, r=r: e.activation(out=r[:, 0:n], in_=PS[hc % 2][:, 0:n], func=AF.Relu), [PS[hc % 2].b()], [r.b()])
                    op("dve", lambda e, hc=hc, r=r: e.tensor_tensor(out=hid[:, hc, 0:n], in0=PS[hc % 2][:, 0:n], in1=r[:, 0:n], op=ALU.mult), [PS[hc % 2].b(), r.b()], [hid.b()])
                for oc in range(8):
                    pst = PS[2 + oc % 2]
                    lin_fm(pst, n, w2v, oc * 128, 128, hid, hid.t, 4, extra_reads=[w2.b()])
                    op("dve", lambda e, oc=oc, pst=pst: e.scalar_tensor_tensor(out=xt[:, oc, 0:n], in0=pst[:, 0:n], scalar=MOD[:, 40 + oc, isc:isc + 1], in1=xt[:, oc, 0:n],
                                                                             op0=ALU.mult, op1=ALU.add), [pst.b(), MOD.b(), xt.b()], [xt.b()])
            dma(xT_out[:, c0:c0 + n].rearrange("(kc p) t -> p kc t", p=128), xt[:, :, 0:n], reads=[xt.b()], writes=[xob])

        zero_states()
        for ti in range(5):
            xt = XT[ti % 2]
            load_x(ti, xt)
            c0, n, isc = TILES[ti]
            norm_mod(xt, n, isc, A1, 0, HT)
            if ti == 1:
                fold(0)
            rec_tile(HT, ti, 0, True)
        zero_states()
        for ti in (0, 4, 3, 2, 1):
            xt = XT[ti % 2]
            load_x(ti, xt)
            c0, n, isc = TILES[ti]
            norm_mod(xt, n, isc, A1, 0, HT)
            if ti == 4:
                fold(1)
            rec_tile(HT, ti, 1, True)
            gate(HT, n)
            (mixer_attn_even if even else mixer_attn_odd)(HT, ti)
            finish_tile(xt, ti)
        k.finish([xob])
        P.n_instr = k.n
    return P


_PROGS = {}


def get_prog(even, phase):
    key = (even, phase)
    if key not in _PROGS:
        _PROGS[key] = build(even, phase)
    return _PROGS[key]


def run_layer(inp, l, xT_sh, consts):
    even = l % 2 == 0
    j = l // 2
    f32 = lambda a: np.ascontiguousarray(np.asarray(a, np.float32))
    w_in = f32(inp["a_w_in"][j] if even else inp["c_w_in"][j])
    ada = f32(inp["ada_w"][l])
    vecs = [build_vecs(inp, l, c) for c in range(NCORES)]
    base = []
    for c in range(NCORES):
        m = {"xT": xT_sh[c], "vecs": vecs[c], "cos": consts[c]["cos"], "sin": consts[c]["sin"], "cmat": consts[c]["cmat"],
             "ada_w": ada, "w_in": w_in}
        if even:
            m["w_ukv"] = f32(inp["mla_w_ukv"][j])
        base.append(m)
    pa = get_prog(even, "A")
    ra = run_bass_kernel_spmd(pa.nc, base, core_ids=list(range(NCORES))).results
    cat = lambda name, ax, sl_ctx, sl_lat: np.ascontiguousarray(np.concatenate([sl_ctx(ra[0][name])] + [sl_lat(ra[c][name]) for c in range(NCORES)], axis=ax))
    kn_all = cat("kn", 2, lambda a: a[:, :, :NCTX], lambda a: a[:, :, NCTX:])
    vtm = np.concatenate([ra[0]["v"][:NCTX // 128]] + [ra[c]["v"][NCTX // 128:] for c in range(NCORES)], axis=0)
    v_all = np.ascontiguousarray(vtm.reshape(NKB, 128, 4, 128).transpose(2, 1, 0, 3))
    sumS = np.ascontiguousarray(np.stack([ra[c]["sumS"] for c in range(NCORES)], 0))
    sumD = np.ascontiguousarray(np.concatenate([ra[c]["sumD"] for c in range(NCORES)], 1))
    pb = get_prog(even, "B")
    maps = []
    for c in range(NCORES):
        m = dict(base[c])
        m.update({"kn_all": kn_all, "v_all": v_all, "sumS_all": sumS, "sumD_all": sumD,
                  "w_o": f32(inp["w_o"][l]), "w1": f32(inp["mlp_w1"][l]), "w2": f32(inp["mlp_w2"][l])})
        if even:
            m["kr_all"] = cat("kr", 2, lambda a: a[:, :, :NCTX], lambda a: a[:, :, NCTX:])
            m["w_uq"] = f32(inp["mla_w_uq"][j])
        maps.append(m)
    rb = run_bass_kernel_spmd(pb.nc, maps, core_ids=list(range(NCORES))).results
    return [np.ascontiguousarray(rb[c]["xTo"]) for c in range(NCORES)]


def kernel(**inp):
    x = np.asarray(inp["x"], np.float32)[0]
    ctx = np.asarray(inp["ctx"], np.float32)[0]
    consts = [const_tables(c) for c in range(NCORES)]
    xT_sh = [np.ascontiguousarray(np.concatenate([ctx.T, x[c * LPC:(c + 1) * LPC].T], axis=1)) for c in range(NCORES)]
    for l in range(4):
        xT_sh = run_layer(inp, l, xT_sh, consts)
    out = np.concatenate([xT_sh[c][:, NCTX:].T for c in range(NCORES)], axis=0)
    return np.ascontiguousarray(out[None].astype(np.float32))
```

```python
import contextlib
import math
import numpy as np
import concourse.bass as bass
import concourse.mybir as mybir
from concourse.bass_utils import run_bass_kernel_spmd

F32 = mybir.dt.float32
BF16 = mybir.dt.bfloat16
AF = mybir.ActivationFunctionType
ALU = mybir.AluOpType

NCORES = 8
DBG = 0
D = 1024
NLAT = 16384
NCTX = 256
LPC = NLAT // NCORES
TPC = NCTX + LPC
TALL = NCTX + NLAT
NKB = TALL // 128
EPS = 1e-6
TILES = [(0, 256, 1), (256, 512, 0), (768, 512, 0), (1280, 512, 0), (1792, 512, 0)]


class Buf:
    __slots__ = ("lw", "rd")

    def __init__(self):
        self.lw = None
        self.rd = []


class Eng:
    def __init__(self, eng, sem):
        self.eng, self.sem = eng, sem
        self.count = 0
        self.seen = {}


class K:
    def __init__(self, nc, sems):
        self.nc = nc
        self.E = {"pe": Eng(nc.tensor, sems[0]), "dve": Eng(nc.vector, sems[1]),
                  "act": Eng(nc.scalar, sems[2]), "pool": Eng(nc.gpsimd, sems[3])}
        self.dmaq = Eng(nc.sync, None)
        self.dsems = sems[4:]
        self.dcnt = [0] * len(self.dsems)
        self.dnext = 0
        self.n = 0

    def _deps(self, e, reads, writes):
        deps = {}
        for b in reads:
            if b.lw is not None:
                k, c = b.lw
                if deps.get(k, 0) < c:
                    deps[k] = c
        for b in writes:
            if b.lw is not None:
                k, c = b.lw
                if deps.get(k, 0) < c:
                    deps[k] = c
            for k, c in b.rd:
                if deps.get(k, 0) < c:
                    deps[k] = c
        for k, c in deps.items():
            if e.seen.get(k, 0) >= c:
                continue
            if k is e.sem and e is self.E["pe"]:
                continue
            e.seen[k] = c
            e.eng.wait_ge(k, c)

    def _mark(self, key, reads, writes):
        for b in reads:
            b.rd.append(key)
            if len(b.rd) > 24:
                m = {}
                for k, c in b.rd:
                    if m.get(k, 0) < c:
                        m[k] = c
                b.rd = list(m.items())
        for b in writes:
            b.lw = key
            b.rd = []

    def op(self, en, fn, reads=(), writes=()):
        e = self.E[en]
        self._deps(e, reads, writes)
        ins = fn(e.eng)
        e.count += 1
        ins.then_inc(e.sem, 1)
        self._mark((e.sem, e.count), reads, writes)
        self.n += 1

    def dma(self, out, in_, reads=(), writes=()):
        e = self.dmaq
        i = self.dnext
        self.dnext = (i + 1) % len(self.dsems)
        s = self.dsems[i]
        if self.dcnt[i] > 0 and e.seen.get(s, 0) < self.dcnt[i]:
            e.eng.wait_ge(s, self.dcnt[i])
            e.seen[s] = self.dcnt[i]
        self._deps(e, reads, writes)
        ins = e.eng.dma_start(out=out, in_=in_)
        self.dcnt[i] += 16
        ins.then_inc(s, 16)
        self._mark((s, self.dcnt[i]), reads, writes)
        self.n += 1

    def finish(self, bufs):
        self._deps(self.dmaq, bufs, bufs)


class T:
    def __init__(self, t):
        self.t = t
        self._b = {}

    def b(self, key=0):
        if key not in self._b:
            self._b[key] = Buf()
        return self._b[key]

    def __getitem__(self, idx):
        return self.t[idx]


def fm(v):
    v = np.asarray(v, np.float32)
    return np.ascontiguousarray(v.reshape(-1, 128).T)


def pad128(v):
    o = np.zeros(128, np.float32)
    o[: v.shape[0]] = v
    return o[:, None]


class VecPack:
    def __init__(self):
        self.cols = []
        self.off = {}
        self.n = 0

    def add(self, name, arr):
        arr = np.asarray(arr, np.float32)
        assert arr.shape[0] == 128
        self.off[name] = self.n
        self.cols.append(arr)
        self.n += arr.shape[1]

    def pack(self):
        return np.ascontiguousarray(np.concatenate(self.cols, axis=1))


def vec_layout(even):
    off = {}
    n = 0

    def a(name, w):
        nonlocal n
        off[name] = n
        n += w
    a("nw1", 8); a("nw2", 8); a("adab", 48); a("c", 8); a("cctx", 8)
    a("flf", 8); a("nflf", 8); a("flb", 8); a("nflb", 8); a("eps", 1); a("one", 1); a("rmask", 4)
    if even:
        a("lb0", 8); a("lb1", 8); a("hgn", 1); a("qn", 3); a("kvn", 2)
        a("qkq_n", 1); a("qkq_r", 1); a("qkk_n", 1); a("qkk_r", 1); a("lbsel", 1)
    else:
        a("daq", 1); a("dak", 1); a("subln", 1); a("rtn", 1); a("rtd", 4); a("lam", 4)
        a("laminit", 1); a("omlam", 1)
    return off, n


def build_vecs(inp, l, core):
    even = l % 2 == 0
    j = l // 2
    vp = VecPack()
    vp.add("nw1", fm(inp["norm_w"][l, 0])); vp.add("nw2", fm(inp["norm_w"][l, 1]))
    vp.add("adab", fm(inp["ada_b"][l])); vp.add("c", fm(inp["c"][0])); vp.add("cctx", fm(inp["c_ctx"]))
    flf = np.zeros((128, 8), np.float32); flb = np.zeros((128, 8), np.float32)
    flf[:, :core] = 1.0
    flb[:, core + 1:] = 1.0
    vp.add("flf", flf); vp.add("nflf", 1.0 - flf); vp.add("flb", flb); vp.add("nflb", 1.0 - flb)
    vp.add("eps", np.full((128, 1), EPS, np.float32)); vp.add("one", np.ones((128, 1), np.float32))
    rm = np.zeros((128, 4), np.float32)
    for cc in range(4):
        rm[cc * 32:(cc + 1) * 32, cc] = 1.0
    vp.add("rmask", rm)
    if even:
        vp.add("lb0", fm(inp["hg_lb"][0].reshape(-1))); vp.add("lb1", fm(inp["hg_lb"][1].reshape(-1)))
        vp.add("hgn", fm(inp["hg_norm"][j])); vp.add("qn", fm(inp["mla_q_norm"][j])); vp.add("kvn", fm(inp["mla_kv_norm"][j]))
        vp.add("qkq_n", fm(inp["mla_qk_q"][j][:128])); vp.add("qkq_r", pad128(inp["mla_qk_q"][j][128:]))
        vp.add("qkk_n", fm(inp["mla_qk_k"][j][:128])); vp.add("qkk_r", pad128(inp["mla_qk_k"][j][128:]))
        vp.add("lbsel", np.full((128, 1), float(j), np.float32))
    else:
        vp.add("daq", fm(np.tile(inp["da_qk_q"][j], 2))); vp.add("dak", fm(np.tile(inp["da_qk_k"][j], 2)))
        vp.add("subln", fm(inp["da_subln"][j])); vp.add("rtn", fm(inp["rt_norm"][j]))
        rtd = np.zeros((128, 4), np.float32)
        for d in range(2):
            for c in range(2):
                rtd[:64, d * 2 + c] = inp["rt_decay"][j, d, 2 * c]
                rtd[64:, d * 2 + c] = inp["rt_decay"][j, d, 2 * c + 1]
        vp.add("rtd", rtd)
        lam = np.zeros((128, 4), np.float32)
        lam[:64, :] = inp["da_lambda"][j].T
        vp.add("lam", lam)
        li = 0.8 - 0.6 * math.exp(-0.3 * l)
        vp.add("laminit", np.full((128, 1), li, np.float32)); vp.add("omlam", np.full((128, 1), 1.0 - li, np.float32))
    off, n = vec_layout(even)
    assert off == vp.off and n == vp.n
    return vp.pack()


def const_tables(core):
    pos = np.arange(LPC) + core * LPC
    row = (pos // 64).astype(np.float32)
    col = (pos % 64).astype(np.float32)
    inv = (10000.0 ** (-np.arange(16, dtype=np.float32) / 16)).astype(np.float32)
    ang = np.concatenate([row[:, None] * inv, row[:, None] * inv, col[:, None] * inv, col[:, None] * inv], axis=1)
    cos = np.ones((128, TPC), np.float32); sin = np.zeros((128, TPC), np.float32)
    cos[:64, NCTX:] = np.cos(ang).T.astype(np.float32); cos[64:, NCTX:] = cos[:64, NCTX:]
    sin[:64, NCTX:] = np.sin(ang).T.astype(np.float32); sin[64:, NCTX:] = sin[:64, NCTX:]
    R = np.zeros((128, 128), np.float32)
    for base in (0, 32, 64, 96):
        for i in range(16):
            R[base + i + 16, base + i] = -1.0
            R[base + i, base + i + 16] = 1.0
    s = np.arange(128)[:, None]; t = np.arange(128)[None, :]
    same = (s // 32) == (t // 32)
    mf = (same & (s <= t)).astype(np.float32)
    mb = (same & (s >= t)).astype(np.float32)
    ones = np.ones((128, 128), np.float32)
    bd = ((s // 64) == (t // 64)).astype(np.float32)
    ident = np.eye(128, dtype=np.float32)
    return {"cos": cos, "sin": sin, "cmat": np.ascontiguousarray(np.stack([R, mf, mb, ones, bd, ident], 0))}


class Prog:
    def __init__(self, even, phase):
        self.even, self.phase = even, phase
        self.nc = bass.Bass("TRN2", target_bir_lowering=False)
        self.st = contextlib.ExitStack()
        self.off, self.nv = vec_layout(even)

    def din(self, name, shape, dt=F32):
        return self.nc.dram_tensor(name, list(shape), dt, kind="ExternalInput").ap()

    def dout(self, name, shape, dt=F32):
        return self.nc.dram_tensor(name, list(shape), dt, kind="ExternalOutput").ap()

    def dscr(self, name, shape, dt=BF16):
        return self.nc.dram_tensor(name, list(shape), dt, kind="Internal").ap()

    def sb(self, name, shape, dt=F32):
        return T(self.st.enter_context(self.nc.sbuf_tensor(name, list(shape), dt)))

    def ps(self, name, shape, dt=F32):
        return T(self.st.enter_context(self.nc.psum_tensor(name, list(shape), dt)))

    def V(self, name, i=0, n=1, rows=128):
        o = self.off[name] + i
        return self.vec[0:rows, o:o + n]


def build(even, phase):
    P = Prog(even, phase)
    nc = P.nc
    k = None
    WIN = 3264 if even else 3072
    xT_in = P.din("xT", [D, TPC])
    vec_in = P.din("vecs", [128, P.nv])
    cos_in = P.din("cos", [128, TPC]); sin_in = P.din("sin", [128, TPC]); cm_in = P.din("cmat", [6, 128, 128])
    ada_in = P.din("ada_w", [D, 6 * D])
    win_in = P.din("w_in", [D, WIN])
    if even:
        wukv_in = P.din("w_ukv", [256, 1024])
    if phase == "A":
        if even:
            kn_out = P.dout("kn", [4, 128, TPC], BF16); kr_out = P.dout("kr", [4, 64, TPC], BF16)
        else:
            kn_out = P.dout("kn", [4, 128, TPC], BF16)
        v_out = P.dout("v", [TPC // 128, 128, 512], BF16)
        sums_out = P.dout("sumS", [2, 4, 128, 128]); sumd_out = P.dout("sumD", [128, 8])
    else:
        xT_out = P.dout("xTo", [D, TPC])
        kn_in = P.din("kn_all", [4, 128, TALL], BF16)
        if even:
            kr_in = P.din("kr_all", [4, 64, TALL], BF16)
            wuq_in = P.din("w_uq", [384, 768])
        v_in = P.din("v_all", [4, 128, NKB, 128], BF16)
        sums_in = P.din("sumS_all", [NCORES, 2, 4, 128, 128]); sumd_in = P.din("sumD_all", [128, NCORES * 8])
        wo_in = P.din("w_o", [D, D]); w1_in = P.din("w1", [D, 4 * D]); w2_in = P.din("w2", [4 * D, D])
        wo_bf = P.dscr("wo_bf", [D, D]); w1_bf = P.dscr("w1_bf", [D, 4 * D]); w2_bf = P.dscr("w2_bf", [4 * D, D])
        if even:
            wuq_bf = P.dscr("wuq_bf", [384, 768])
    win_bf = P.dscr("win_bf", [D, WIN])
    if even:
        wukv_bf = P.dscr("wukv_bf", [256, 1024])

    st = P.st
    with st:
        sems = [st.enter_context(nc.semaphore(f"s{i}")) for i in range(4 + 10)]
        k = K(nc, sems)
        op, dma = k.op, k.dma
        P.vec = None
        vec = P.sb("vec", [128, P.nv]); P.vec = vec.t
        cosT = P.sb("cosT", [128, 512]); sinT = P.sb("sinT", [128, 512])
        cmf = P.sb("cmf", [128, 6, 128]); cmb = P.sb("cmb", [128, 6, 128], BF16)
        dma(vec[:], vec_in, writes=[vec.b()])
        dma(cmf[:], cm_in.rearrange("c p d -> p c d"), writes=[cmf.b()])
        op("dve", lambda e: e.tensor_copy(out=cmb[:], in_=cmf[:]), [cmf.b()], [cmb.b()])
        RMf = cmf[:, 0, :]
        MASK = {0: cmf[:, 1, :], 1: cmf[:, 2, :]}
        ONESb = cmb[:, 3, :]; BDb = cmb[:, 4, :]; IDb = cmb[:, 5, :]; ONESf = cmf[:, 3, :]
        CB = [vec.b(), cmf.b(), cmb.b(), cosT.b(), sinT.b()]

        PS = [P.ps(f"ps{i}", [128, 512]) for i in range(7)]
        PSB = P.ps("psb", [128, 1024], BF16)

        stg = [P.sb(f"stg{i}", [128, 1024]) for i in range(2)]
        stgb = [P.sb(f"stgb{i}", [128, 1024], BF16) for i in range(2)]
        prep_n = [0]
        WB = {}

        def prep(src, dst, rows, cols, name, col_ranges=None):
            WB[name] = Buf()
            for r0 in range(0, rows, 128):
                rr = min(128, rows - r0)
                pieces = []
                for (a0, an) in (col_ranges or [(0, cols)]):
                    for c in range(a0, a0 + an, 1024):
                        pieces.append((c, min(1024, a0 + an - c)))
                for (c0, cn) in pieces:
                    i = prep_n[0] % 2
                    prep_n[0] += 1
                    dma(stg[i][0:rr, 0:cn], src[r0:r0 + rr, c0:c0 + cn], writes=[stg[i].b()])
                    op("pool" if i else "dve", lambda e, i=i, rr=rr, cn=cn: e.tensor_copy(out=stgb[i][0:rr, 0:cn], in_=stg[i][0:rr, 0:cn]),
                       [stg[i].b()], [stgb[i].b()])
                    dma(dst[r0:r0 + rr, c0:c0 + cn], stgb[i][0:rr, 0:cn], reads=[stgb[i].b()], writes=[WB[name]])

        if phase == "A":
            if even:
                prep(win_in, win_bf, D, WIN, "win", [(512, 1536), (2944, 320)])
                prep(wukv_in, wukv_bf, 256, 1024, "wukv")
            else:
                prep(win_in, win_bf, D, WIN, "win", [(512, 1024), (1792, 768)])
        else:
            if even:
                prep(win_in, win_bf, D, WIN, "win", [(0, 2048), (2048, 896)])
                prep(wuq_in, wuq_bf, 384, 768, "wuq")
            else:
                prep(win_in, win_bf, D, WIN, "win", [(0, 512), (1536, 1536)])
            prep(wo_in, wo_bf, D, D, "wo"); prep(w1_in, w1_bf, D, 4 * D, "w1"); prep(w2_in, w2_bf, 4 * D, D, "w2")

        scT = P.sb("scT", [128, 8, 2]); MOD = P.sb("MOD", [128, 48, 2])
        for kc in range(8):
            op("act", lambda e, kc=kc: e.activation(out=scT[:, kc, 0:1], in_=P.V("c", kc), func=AF.Silu), [vec.b()], [scT.b()])
            op("act", lambda e, kc=kc: e.activation(out=scT[:, kc, 1:2], in_=P.V("cctx", kc), func=AF.Silu), [vec.b()], [scT.b()])
        ncb = 2 if phase == "A" else 6
        IDf = cmf[:, 5, :]
        mrow = [P.sb(f"mrow{i}", [128, 512]) for i in range(2)]
        an = 0
        for cb in range(ncb):
            for kc in range(8):
                ab = stg[an % 2]
                an += 1
                dma(ab[:, 0:1024], ada_in[kc * 128:(kc + 1) * 128, cb * 1024:(cb + 1) * 1024], writes=[ab.b()])
                for half in range(2):
                    op("pe", lambda e, kc=kc, ab=ab, half=half: e.matmul(PS[half][0:2, 0:512], lhsT=scT[:, kc, :], rhs=ab[:, half * 512:(half + 1) * 512],
                                                                        start=(kc == 0), stop=(kc == 7)), [ab.b(), scT.b()], [PS[half].b()])
            for half in range(2):
                op("act", lambda e, half=half: e.copy(out=mrow[half][0:2, 0:512], in_=PS[half][0:2, 0:512]), [PS[half].b()], [mrow[half].b()])
            for j in range(8):
                ch = cb * 8 + j
                half, jj = j // 4, j % 4
                op("pe", lambda e, half=half, jj=jj: e.matmul(PS[6][:, 0:2], lhsT=mrow[half][0:2, jj * 128:(jj + 1) * 128], rhs=IDf[0:2, 0:2], start=True, stop=True),
                   [mrow[half].b(), cmf.b()], [PS[6].b()])
                op("dve", lambda e, ch=ch: e.tensor_scalar(out=MOD[:, ch, :], in0=PS[6][:, 0:2], scalar1=P.V("adab", ch), scalar2=None, op0=ALU.add),
                   [PS[6].b(), vec.b()], [MOD.b()])
        A1 = P.sb("A1", [128, 8, 2]); A2 = P.sb("A2", [128, 8, 2])
        for kc in range(8):
            op("dve", lambda e, kc=kc: e.tensor_scalar(out=A1[:, kc, :], in0=MOD[:, 8 + kc, :], scalar1=1.0, scalar2=P.V("nw1", kc), op0=ALU.add, op1=ALU.mult),
               [MOD.b(), vec.b()], [A1.b()])
            if phase == "B":
                op("dve", lambda e, kc=kc: e.tensor_scalar(out=A2[:, kc, :], in0=MOD[:, 32 + kc, :], scalar1=1.0, scalar2=P.V("nw2", kc), op0=ALU.add, op1=ALU.mult),
                   [MOD.b(), vec.b()], [A2.b()])
        CB += [MOD.b(), A1.b(), A2.b()]

        XT = [P.sb("XT0", [128, 8, 512])] * 2
        HT = P.sb("HT", [128, 8, 512], BF16)
        SQ = P.sb("SQ", [128, 8, 512], BF16)
        RSTD = P.sb("RSTD", [128, 512]); TMP = [P.sb(f"TMP{i}", [128, 512]) for i in range(4)]
        WBUF = [P.sb(f"WBUF{i}", [128, 4096], BF16) for i in range(3)]
        wn = [0]

        def load_w(wbf, name, r0, rows, c0, cols):
            w = WBUF[wn[0] % 3]
            wn[0] += 1
            kcn = max(1, rows // 128)
            pr = min(rows, 128)
            view = w.t[0:pr, 0:kcn * cols].rearrange("p (kc n) -> p kc n", kc=kcn)
            if rows >= 128:
                src = wbf[r0:r0 + rows, c0:c0 + cols].rearrange("(kc p) n -> p kc n", p=128)
            else:
                src = wbf[r0:r0 + rows, c0:c0 + cols].rearrange("(kc p) n -> p kc n", kc=1)
            dma(view, src, reads=[WB[name]], writes=[w.b()])
            return w, view

        def load_x(ti, xt):
            c0, n, _ = TILES[ti]
            dma(xt[:, :, 0:n], xT_in[:, c0:c0 + n].rearrange("(kc p) t -> p kc t", p=128), writes=[xt.b()])
            dma(cosT[:, 0:n], cos_in[:, c0:c0 + n], writes=[cosT.b()])
            dma(sinT[:, 0:n], sin_in[:, c0:c0 + n], writes=[sinT.b()])

        def rstd_from(ps_t, n, scale, out_t, rows=128):
            op("act", lambda e: e.activation(out=out_t[0:rows, 0:n], in_=ps_t[0:rows, 0:n], func=AF.Sqrt, bias=P.V("eps", rows=rows), scale=scale),
               [ps_t.b(), vec.b()], [out_t.b()])
            op("dve", lambda e: e.reciprocal(out=out_t[0:rows, 0:n], in_=out_t[0:rows, 0:n]), [out_t.b()], [out_t.b()])

        def norm_mod(xt, n, s, A, shc, ht):
            for kc in range(8):
                op("act", lambda e, kc=kc: e.activation(out=SQ[:, kc, 0:n], in_=xt[:, kc, 0:n], func=AF.Square), [xt.b()], [SQ.b()])
            for kc in range(8):
                op("pe", lambda e, kc=kc: e.matmul(PS[6][:, 0:n], lhsT=ONESb, rhs=SQ[:, kc, 0:n], start=(kc == 0), stop=(kc == 7)),
                   [SQ.b(), cmb.b()], [PS[6].b()])
            rstd_from(PS[6], n, 1.0 / D, RSTD)
            for kc in range(8):
                tm = TMP[kc % 2]
                op("dve", lambda e, kc=kc, tm=tm: e.scalar_tensor_tensor(out=tm[:, 0:n], in0=xt[:, kc, 0:n], scalar=A[:, kc, s:s + 1], in1=RSTD[:, 0:n],
                                                                       op0=ALU.mult, op1=ALU.mult), [xt.b(), RSTD.b(), A.b()], [tm.b()])
                op("act", lambda e, kc=kc, tm=tm: e.activation(out=ht[:, kc, 0:n], in_=tm[:, 0:n], func=AF.Identity, bias=MOD[:, shc + kc, s:s + 1]),
                   [tm.b(), MOD.b()], [ht.b()])

        def lin_fm(pst, n, wview, c0, m, rhs_t, rhs_view, kcn, m0=0, extra_reads=()):
            for kc in range(kcn):
                op("pe", lambda e, kc=kc: e.matmul(pst[m0:m0 + m, 0:n], lhsT=wview[:, kc, c0:c0 + m], rhs=rhs_view[:, kc, 0:n],
                                                  start=(kc == 0), stop=(kc == kcn - 1)), [rhs_t.b()] + list(extra_reads), [pst.b()])

        def rope(x_t, x_view, rows, c0, n, out_view, out_t, pst):
            op("pe", lambda e: e.matmul(pst[0:rows, 0:n], lhsT=RMf[0:rows, 0:rows], rhs=x_view, start=True, stop=True), [x_t.b(), cmf.b()], [pst.b()])
            t1, t2 = TMP[2], TMP[3]
            op("pool", lambda e: e.tensor_tensor(out=t1[0:rows, 0:n], in0=x_view, in1=cosT[0:rows, 0:n], op=ALU.mult), [x_t.b(), cosT.b()], [t1.b()])
            op("dve", lambda e: e.tensor_tensor(out=t2[0:rows, 0:n], in0=pst[0:rows, 0:n], in1=sinT[0:rows, 0:n], op=ALU.mult), [pst.b(), sinT.b()], [t2.b()])
            op("dve", lambda e: e.tensor_tensor(out=out_view, in0=t1[0:rows, 0:n], in1=t2[0:rows, 0:n], op=ALU.add), [t1.b(), t2.b()], [out_t.b()])

        NH = 4
        KD = 128 if even else 64
        S = [[P.sb(f"S{h}_{i}", [128, 128]) for i in range(2)] for h in range(NH)]
        scur = [0] * NH
        QF = P.sb("QF", [128, 512]); KF = P.sb("KF", [128, 512]); LF = P.sb("LF", [128, 512])
        Bc = P.sb("Bc", [128, 512]); Pc = P.sb("Pc", [128, 512]); Dk = P.sb("Dk", [128, 512]); Db = P.sb("Db", [128, 512])
        Ek = P.sb("Ek", [128, 512]); Eq = P.sb("Eq", [128, 512]); Eb = P.sb("Eb", [128, 512])
        KH = P.sb("KH", [128, 512], BF16); QH = P.sb("QH", [128, 512], BF16); QTL = P.sb("QTL", [128, 512])
        KHT = P.sb("KHT", [128, 4, 128], BF16); ATM = P.sb("ATM", [128, 128], BF16)
        VT = P.sb("VT", [128, 4, 512], BF16)
        OACC = P.sb("OACC", [128, 4, TPC], BF16) if phase == "B" else None
        LTOT = P.sb("LTOT", [128, 8])

        def decay_factors(n, d):
            nch = n // 32
            op("dve", lambda e: e.tensor_tensor_scan(out=Bc[:, 0:n], data0=LF[:, 0:n], data1=LF[:, 0:n], initial=0.0, op0=ALU.add, op1=ALU.bypass),
               [LF.b()], [Bc.b()])
            op("pool", lambda e: e.tensor_tensor(out=Pc[:, 0:n], in0=Bc[:, 0:n], in1=LF[:, 0:n], op=ALU.subtract), [Bc.b(), LF.b()], [Pc.b()])
            v3 = lambda t: t[:, 0:n].rearrange("p (c t) -> p c t", t=32)
            bend = v3(Bc)[:, :, 31:32].to_broadcast([128, nch, 32])
            pst = v3(Pc)[:, :, 0:1].to_broadcast([128, nch, 32])
            if d == 0:
                op("dve", lambda e: e.tensor_tensor(out=v3(Dk), in0=bend, in1=v3(Bc), op=ALU.subtract), [Bc.b()], [Dk.b()])
                op("dve", lambda e: e.tensor_tensor(out=v3(Db), in0=v3(Bc), in1=pst, op=ALU.subtract), [Bc.b(), Pc.b()], [Db.b()])
            else:
                op("dve", lambda e: e.tensor_tensor(out=v3(Dk), in0=v3(Pc), in1=pst, op=ALU.subtract), [Pc.b()], [Dk.b()])
                op("dve", lambda e: e.tensor_tensor(out=v3(Db), in0=bend, in1=v3(Pc), op=ALU.subtract), [Bc.b(), Pc.b()], [Db.b()])
            op("pool", lambda e: e.tensor_scalar(out=Dk[:, 0:n], in0=Dk[:, 0:n], scalar1=-80.0, scalar2=None, op0=ALU.max), [Dk.b()], [Dk.b()])
            op("act", lambda e: e.activation(out=Ek[:, 0:n], in_=Dk[:, 0:n], func=AF.Exp), [Dk.b()], [Ek.b()])
            op("act", lambda e: e.activation(out=Eq[:, 0:n], in_=Dk[:, 0:n], func=AF.Exp, scale=-1.0), [Dk.b()], [Eq.b()])
            op("act", lambda e: e.activation(out=Eb[:, 0:n], in_=Db[:, 0:n], func=AF.Exp), [Db.b()], [Eb.b()])

        def rec_chunk_tile(heads, c0, n, d, want_out):
            op("pool", lambda e: e.tensor_tensor(out=KH[:, 0:n], in0=KF[:, 0:n], in1=Ek[:, 0:n], op=ALU.mult), [KF.b(), Ek.b()], [KH.b()])
            if want_out:
                op("pool", lambda e: e.tensor_tensor(out=QH[:, 0:n], in0=QF[:, 0:n], in1=Eq[:, 0:n], op=ALU.mult), [QF.b(), Eq.b()], [QH.b()])
                op("dve", lambda e: e.tensor_tensor(out=QTL[:, 0:n], in0=QF[:, 0:n], in1=Eb[:, 0:n], op=ALU.mult), [QF.b(), Eb.b()], [QTL.b()])
            nst = n // 128
            order = range(nst) if d == 0 else range(nst - 1, -1, -1)
            for sti in order:
                t0 = sti * 128
                op("pe", lambda e, t0=t0: e.transpose(PSB[:, 0:128], KH[:, t0:t0 + 128], IDb), [KH.b(), cmb.b()], [PSB.b()])
                for cm in range(4):
                    op("act", lambda e, cm=cm: e.activation(out=KHT[:, cm, :], in_=PSB[:, 0:128], func=AF.Copy, scale=P.V("rmask", cm)), [PSB.b(), vec.b()], [KHT.b()])
                for (h, pb) in heads:
                    if DBG == 95: break
                    rows = slice(pb, pb + KD)
                    if want_out:
                        op("pe", lambda e, t0=t0, rows=rows: e.matmul(PS[4][:, 0:128], lhsT=KH[rows, t0:t0 + 128], rhs=QH[rows, t0:t0 + 128], start=True, stop=True),
                           [KH.b(), QH.b()], [PS[4].b()])
                        op("dve", lambda e: e.tensor_tensor(out=ATM[:], in0=PS[4][:, 0:128], in1=MASK[d], op=ALU.mult), [PS[4].b(), cmf.b()], [ATM.b()])
                        op("pe", lambda e, h=h, sti=sti: e.matmul(PS[5][:, 0:128], lhsT=VT[:, sti, h * 128:(h + 1) * 128], rhs=ATM[:], start=True, stop=False),
                           [VT.b(), ATM.b()], [PS[5].b()])
                    corder = range(4) if d == 0 else range(3, -1, -1)
                    for ci, c in enumerate(corder):
                        sp = S[h][scur[h]]; sn = S[h][1 - scur[h]]
                        cc = t0 + c * 32
                        if want_out:
                            op("pe", lambda e, sp=sp, rows=rows, cc=cc, c=c, ci=ci: e.matmul(PS[5][:, c * 32:(c + 1) * 32], lhsT=sp[rows, :], rhs=QTL[rows, cc:cc + 32],
                                                                                    start=False, stop=(ci == 3)), [sp.b(), QTL.b()], [PS[5].b()])
                        op("pe", lambda e, c=c, h=h, sti=sti: e.matmul(PS[3][:, 0:128], lhsT=KHT[:, c, :], rhs=VT[:, sti, h * 128:(h + 1) * 128],
                                                                       start=True, stop=True), [KHT.b(), VT.b()], [PS[3].b()])
                        ecol = cc + 31 if d == 0 else cc
                        if DBG == 96: continue
                        op("dve", lambda e, sp=sp, sn=sn, rows=rows, ecol=ecol: e.scalar_tensor_tensor(out=sn[rows, :], in0=sp[rows, :], scalar=Eb[rows, ecol:ecol + 1],
                                                                                                     in1=PS[3][rows, 0:128], op0=ALU.mult, op1=ALU.add),
                           [sp.b(), Eb.b(), PS[3].b()], [sn.b()])
                        scur[h] = 1 - scur[h]
                    if want_out:
                        oc = OACC[:, h, c0 + t0:c0 + t0 + 128]
                        if d == 0:
                            op("act", lambda e, oc=oc: e.copy(out=oc, in_=PS[5][:, 0:128]), [PS[5].b()], [OACC.b((h, c0))])
                        else:
                            op("dve", lambda e, oc=oc: e.tensor_tensor(out=oc, in0=PS[5][:, 0:128], in1=oc, op=ALU.add), [PS[5].b()], [OACC.b((h, c0))])

        def zero_states():
            for h in range(NH):
                op("pool", lambda e, h=h: e.memset(S[h][scur[h]][:], 0.0), [], [S[h][scur[h]].b()])

        def v_token_major(ht, n, wname_cols):
            c0w, = wname_cols
            w, wv = load_w(win_bf, "win", 0, D, c0w, 512)
            for sti in range(n // 128):
                for kc in range(8):
                    op("pe", lambda e, kc=kc, sti=sti: e.matmul(PS[2][:, 0:512], lhsT=ht[:, kc, sti * 128:(sti + 1) * 128], rhs=wv[:, kc, :],
                                                               start=(kc == 0), stop=(kc == 7)), [ht.b(), w.b()], [PS[2].b()])
                op("act", lambda e, sti=sti: e.copy(out=VT[:, sti, :], in_=PS[2][:, 0:512]), [PS[2].b()], [VT.b()])

        LB = P.sb("LB", [128, 8]); OMLB = P.sb("OMLB", [128, 8]); LG = P.sb("LG", [128, 4])
        if even:
            op("dve", lambda e: e.tensor_tensor(out=LB[:], in0=P.V("lb1", 0, 8), in1=P.V("lb0", 0, 8), op=ALU.subtract), [vec.b()], [LB.b()])
            op("act", lambda e: e.activation(out=LB[:], in_=LB[:], func=AF.Sigmoid), [LB.b()], [LB.b()])
            op("dve", lambda e: e.tensor_scalar(out=LB[:], in0=LB[:], scalar1=P.V("lbsel"), scalar2=None, op0=ALU.mult), [LB.b(), vec.b()], [LB.b()])
            op("dve", lambda e: e.tensor_scalar(out=OMLB[:], in0=LB[:], scalar1=-1.0, scalar2=1.0, op0=ALU.mult, op1=ALU.add), [LB.b()], [OMLB.b()])
        else:
            op("act", lambda e: e.activation(out=LG[:], in_=P.V("rtd", 0, 4), func=AF.Sigmoid), [vec.b()], [LG.b()])
            op("act", lambda e: e.activation(out=LG[:], in_=LG[:], func=AF.Ln), [LG.b()], [LG.b()])
        CB += [LB.b(), OMLB.b(), LG.b()]

        def rec_feature_chunk(ht, n, c0, fc, d, want_q, wk, wq):
            if even:
                w, wv = wk
                lin_fm(PS[0], n, wv, fc * 128, 128, ht, ht.t, 8, extra_reads=[w.b()])
                sg = TMP[0]
                op("act", lambda e: e.activation(out=sg[:, 0:n], in_=PS[0][:, 0:n], func=AF.Sigmoid), [PS[0].b()], [sg.b()])
                col = d * 4 + fc
                op("dve", lambda e: e.tensor_scalar(out=sg[:, 0:n], in0=sg[:, 0:n], scalar1=OMLB[:, col:col + 1], scalar2=LB[:, col:col + 1], op0=ALU.mult, op1=ALU.add),
                   [sg.b(), LB.b(), OMLB.b()], [sg.b()])
                op("pool", lambda e: e.tensor_scalar(out=KF[:, 0:n], in0=sg[:, 0:n], scalar1=-1.0, scalar2=1.0, op0=ALU.mult, op1=ALU.add), [sg.b()], [KF.b()])
                op("act", lambda e: e.activation(out=LF[:, 0:n], in_=sg[:, 0:n], func=AF.Ln), [sg.b()], [LF.b()])
                if want_q:
                    w, wv = wq
                    lin_fm(PS[1], n, wv, fc * 128, 128, ht, ht.t, 8, extra_reads=[w.b()])
                    op("act", lambda e: e.activation(out=QF[:, 0:n], in_=PS[1][:, 0:n], func=AF.Silu), [PS[1].b()], [QF.b()])
            else:
                w, wv = wk
                lin_fm(PS[0], n, wv, fc * 128, 128, ht, ht.t, 8, extra_reads=[w.b()])
                kx = TMP[0]
                op("act", lambda e: e.activation(out=kx[:, 0:n], in_=PS[0][:, 0:n], func=AF.Copy, scale=0.125), [PS[0].b()], [kx.b()])
                rope(kx, kx[:, 0:n], 128, c0, n, KF[:, 0:n], KF, PS[1])
                op("pool", lambda e: e.memset(LF[:, 0:n], 1.0), [], [LF.b()])
                col = d * 2 + fc
                op("dve", lambda e: e.tensor_scalar(out=LF[:, 0:n], in0=LF[:, 0:n], scalar1=LG[:, col:col + 1], scalar2=None, op0=ALU.mult), [LF.b(), LG.b()], [LF.b()])
                if want_q:
                    w, wv = wq
                    lin_fm(PS[0], n, wv, fc * 128, 128, ht, ht.t, 8, extra_reads=[w.b()])
                    qx = TMP[1]
                    op("act", lambda e: e.copy(out=qx[:, 0:n], in_=PS[0][:, 0:n]), [PS[0].b()], [qx.b()])
                    rope(qx, qx[:, 0:n], 128, c0, n, QF[:, 0:n], QF, PS[1])

        NFC = 4 if even else 2
        VCOL = 1536 if even else 2048

        def heads_of(fc):
            return [(fc, 0)] if even else [(2 * fc, 0), (2 * fc + 1, 64)]

        def rec_tile(ht, ti, d, want_out):
            c0, n, _ = TILES[ti]
            v_token_major(ht, n, (VCOL,))
            if even:
                wk = load_w(win_bf, "win", 0, D, 512 + d * 512, 512)
                wq = load_w(win_bf, "win", 0, D, 0, 512) if want_out else None
            else:
                wk = load_w(win_bf, "win", 0, D, 1792, 256)
                wq = load_w(win_bf, "win", 0, D, 1536, 256) if want_out else None
            for fc in range(NFC):
                if DBG == 91: break
                rec_feature_chunk(ht, n, c0, fc, d, want_out, wk, wq)
                if DBG == 94: continue
                decay_factors(n, d)
                if DBG == 92: continue
                if phase == "A":
                    col = d * 4 + fc
                    tot = Bc[:, n - 1:n]
                    op("dve", lambda e, col=col, tot=tot: e.tensor_tensor(out=LTOT[:, col:col + 1], in0=LTOT[:, col:col + 1], in1=tot, op=ALU.add),
                       [Bc.b(), LTOT.b()], [LTOT.b()])
                rec_chunk_tile(heads_of(fc), c0, n, d, want_out)

        if phase == "A":
            KNs = P.sb("KNs", [128, 4, 512], BF16); KRs = P.sb("KRs", [128, 4, 512], BF16)
            CKV = P.sb("CKV", [128, 2, 512]); CKVN = P.sb("CKVN", [128, 2, 512], BF16)
            KRP = P.sb("KRP", [128, 512]); KRR = P.sb("KRR", [128, 512]); KNF = P.sb("KNF", [128, 512])
            VS = P.sb("VS", [128, 4, 512], BF16)
            op("pool", lambda e: e.memset(LTOT[:], 0.0), [], [LTOT.b()])

            def kv_even(ht, ti):
                c0, n, _ = TILES[ti]
                w, wv = load_w(win_bf, "win", 0, D, 2944, 320)
                for cc in range(2):
                    lin_fm(PS[0], n, wv, cc * 128, 128, ht, ht.t, 8, extra_reads=[w.b()])
                    op("act", lambda e, cc=cc: e.copy(out=CKV[:, cc, 0:n], in_=PS[0][:, 0:n]), [PS[0].b()], [CKV.b()])
                    op("pool", lambda e, cc=cc: e.tensor_tensor(out=SQ[:, cc, 0:n], in0=CKV[:, cc, 0:n], in1=CKV[:, cc, 0:n], op=ALU.mult), [CKV.b()], [SQ.b()])
                if DBG == 31: return
                lin_fm(PS[1], n, wv, 256, 64, ht, ht.t, 8, extra_reads=[w.b()])
                if DBG == 311: return
                raw = TMP[1]
                op("act", lambda e: e.copy(out=raw[0:64, 0:n], in_=PS[1][0:64, 0:n]), [PS[1].b()], [raw.b()])
                op("dve", lambda e: e.tensor_scalar(out=KRP[0:64, 0:n], in0=raw[0:64, 0:n], scalar1=P.V("qkk_r", rows=64), scalar2=None, op0=ALU.mult),
                   [raw.b(), vec.b()], [KRP.b()])
                if DBG == 312: return
                op("pool", lambda e: e.tensor_tensor(out=SQ[0:64, 2, 0:n], in0=raw[0:64, 0:n], in1=raw[0:64, 0:n], op=ALU.mult), [raw.b()], [SQ.b()])
                if DBG == 32: return
                for cc in range(2):
                    op("pe", lambda e, cc=cc: e.matmul(PS[6][:, 0:n], lhsT=ONESb, rhs=SQ[:, cc, 0:n], start=(cc == 0), stop=(cc == 1)), [SQ.b(), cmb.b()], [PS[6].b()])
                rstd_from(PS[6], n, 1.0 / 256, RSTD)
                for cc in range(2):
                    op("dve", lambda e, cc=cc: e.scalar_tensor_tensor(out=CKVN[:, cc, 0:n], in0=CKV[:, cc, 0:n], scalar=P.V("kvn", cc), in1=RSTD[:, 0:n], op0=ALU.mult, op1=ALU.mult),
                       [CKV.b(), RSTD.b(), vec.b()], [CKVN.b()])
                if DBG == 33: return
                rope(KRP, KRP[0:64, 0:n], 64, c0, n, KRR[0:64, 0:n], KRR, PS[1])
                if DBG == 34: return
                w2, w2v = load_w(wukv_bf, "wukv", 0, 256, 0, 1024)
                for sti in range(n // 128):
                    for hh in range(4):
                        for kc in range(2):
                            op("pe", lambda e, kc=kc, sti=sti, hh=hh: e.matmul(PS[2][:, hh * 128:(hh + 1) * 128], lhsT=CKVN[:, kc, sti * 128:(sti + 1) * 128],
                                                                       rhs=w2v[:, kc, hh * 256 + 128:hh * 256 + 256],
                                                                       start=(kc == 0), stop=(kc == 1)), [CKVN.b(), w2.b()], [PS[2].b()])
                    op("act", lambda e, sti=sti: e.copy(out=VS[:, sti, :], in_=PS[2][:, 0:512]), [PS[2].b()], [VS.b()])
                    tb = (c0 + sti * 128) // 128
                    dma(v_out[tb], VS[:, sti, :], reads=[VS.b()], writes=[v_outb])
                if DBG == 35: return
                for h in range(4):
                    lin_fm(PS[0], n, w2v, h * 256, 128, CKVN, CKVN.t, 2, extra_reads=[w2.b()])
                    op("act", lambda e: e.copy(out=KNF[:, 0:n], in_=PS[0][:, 0:n]), [PS[0].b()], [KNF.b()])
                    op("pool", lambda e: e.tensor_tensor(out=SQ[:, 3, 0:n], in0=KNF[:, 0:n], in1=KNF[:, 0:n], op=ALU.mult), [KNF.b()], [SQ.b()])
                    op("pe", lambda e: e.matmul(PS[6][:, 0:n], lhsT=ONESb, rhs=SQ[:, 3, 0:n], start=True, stop=False), [SQ.b(), cmb.b()], [PS[6].b()])
                    op("pe", lambda e: e.matmul(PS[6][:, 0:n], lhsT=ONESb[0:64, :], rhs=SQ[0:64, 2, 0:n], start=False, stop=True), [SQ.b(), cmb.b()], [PS[6].b()])
                    rstd_from(PS[6], n, 1.0 / 192, RSTD)
                    op("dve", lambda e, h=h: e.scalar_tensor_tensor(out=KNs[:, h, 0:n], in0=KNF[:, 0:n], scalar=P.V("qkk_n"), in1=RSTD[:, 0:n], op0=ALU.mult, op1=ALU.mult),
                       [KNF.b(), RSTD.b(), vec.b()], [KNs.b()])
                    op("dve", lambda e, h=h: e.tensor_tensor(out=KRs[0:64, h, 0:n], in0=KRR[0:64, 0:n], in1=RSTD[0:64, 0:n], op=ALU.mult), [KRR.b(), RSTD.b()], [KRs.b()])
                for h in range(4):
                    dma(kn_out[h, :, c0:c0 + n], KNs[:, h, 0:n], reads=[KNs.b()], writes=[v_outb])
                    dma(kr_out[h, :, c0:c0 + n], KRs[0:64, h, 0:n], reads=[KRs.b()], writes=[v_outb])

            def kv_odd(ht, ti):
                c0, n, _ = TILES[ti]
                w, wv = load_w(win_bf, "win", 0, D, 512, 512)
                for h in range(4):
                    lin_fm(PS[0], n, wv, h * 128, 128, ht, ht.t, 8, extra_reads=[w.b()])
                    op("act", lambda e: e.copy(out=KNF[:, 0:n], in_=PS[0][:, 0:n]), [PS[0].b()], [KNF.b()])
                    op("pool", lambda e: e.tensor_tensor(out=SQ[:, 3, 0:n], in0=KNF[:, 0:n], in1=KNF[:, 0:n], op=ALU.mult), [KNF.b()], [SQ.b()])
                    op("pe", lambda e: e.matmul(PS[6][:, 0:n], lhsT=BDb, rhs=SQ[:, 3, 0:n], start=True, stop=True), [SQ.b(), cmb.b()], [PS[6].b()])
                    rstd_from(PS[6], n, 1.0 / 64, RSTD)
                    op("dve", lambda e: e.scalar_tensor_tensor(out=KRP[:, 0:n], in0=KNF[:, 0:n], scalar=P.V("dak"), in1=RSTD[:, 0:n], op0=ALU.mult, op1=ALU.mult),
                       [KNF.b(), RSTD.b(), vec.b()], [KRP.b()])
                    rope(KRP, KRP[:, 0:n], 128, c0, n, KNs[:, h, 0:n], KNs, PS[1])
                for h in range(4):
                    dma(kn_out[h, :, c0:c0 + n], KNs[:, h, 0:n], reads=[KNs.b()], writes=[v_outb])
                w, wv = load_w(win_bf, "win", 0, D, 1024, 512)
                for sti in range(n // 128):
                    for kc in range(8):
                        op("pe", lambda e, kc=kc, sti=sti: e.matmul(PS[2][:, 0:512], lhsT=ht[:, kc, sti * 128:(sti + 1) * 128], rhs=wv[:, kc, :],
                                                                   start=(kc == 0), stop=(kc == 7)), [ht.b(), w.b()], [PS[2].b()])
                    op("act", lambda e, sti=sti: e.copy(out=VS[:, sti, :], in_=PS[2][:, 0:512]), [PS[2].b()], [VS.b()])
                    tb = (c0 + sti * 128) // 128
                    dma(v_out[tb], VS[:, sti, :], reads=[VS.b()], writes=[v_outb])

            v_outb = Buf()
            zero_states()
            for ti in range(5 if DBG != 1 else 0):
                xt = XT[ti % 2]
                load_x(ti, xt)
                c0, n, isc = TILES[ti]
                norm_mod(xt, n, isc, A1, 0, HT)
                if DBG != 2:
                    (kv_even if even else kv_odd)(HT, ti)
                if not isc and DBG not in (2, 3, 31, 32, 33, 34, 35, 311, 312):
                    rec_tile(HT, ti, 0, False)
            for h in range(4):
                sp = S[h][scur[h]]
                dma(sums_out[0, h], sp[:], reads=[sp.b()], writes=[v_outb])
            zero_states()
            for ti in ((4, 3, 2, 1) if DBG in (0, 9, 91, 92, 93, 94, 95, 96) else ()):
                xt = XT[ti % 2]
                load_x(ti, xt)
                c0, n, isc = TILES[ti]
                norm_mod(xt, n, isc, A1, 0, HT)
                rec_tile(HT, ti, 1, False)
            for h in range(4):
                sp = S[h][scur[h]]
                dma(sums_out[1, h], sp[:], reads=[sp.b()], writes=[v_outb])
            op("act", lambda e: e.activation(out=LTOT[:], in_=LTOT[:], func=AF.Exp), [LTOT.b()], [LTOT.b()])
            dma(sumd_out, LTOT[:], reads=[LTOT.b()], writes=[v_outb])
            k.finish([v_outb])
            P.n_instr = k.n
            return P

        xob = Buf()
        SJ = [P.sb(f"SJ{i}", [128, 128]) for i in range(2)]
        SD = P.sb("SD", [128, NCORES * 8]); MJ = P.sb("MJ", [128, 1])
        dma(SD[:], sumd_in, writes=[SD.b()])
        GT = P.sb("GT", [128, 4, 512], BF16)
        YT = P.sb("YT", [128, 8, 512], BF16)
        QN = P.sb("QN", [128, 4, 512], BF16); QR = P.sb("QR", [128, 4, 512], BF16)
        KNb = [P.sb(f"KNb{i}", [128, 1280], BF16) for i in range(2)]
        KRb = [P.sb(f"KRb{i}", [128, 1280], BF16) for i in range(2)]
        Vb = [P.sb(f"Vb{i}", [128, 10, 128], BF16) for i in range(2)]
        PT = [P.sb(f"PT{i}", [128, 512], BF16) for i in range(2)]
        AO = [P.sb(f"AO{i}", [128, 512]) for i in range(2)]
        HID = [P.sb("HID0", [128, 4, 512], BF16)] * 2
        CQ = P.sb("CQ", [128, 3, 512]); CQN = P.sb("CQN", [128, 3, 512], BF16)
        QX = P.sb("QX", [128, 512]); QY = P.sb("QY", [128, 512])
        NLAM = P.sb("NLAM", [128, 1])
        sbn = [0]

        def fold(d):
            js = range(NCORES) if d == 0 else range(NCORES - 1, -1, -1)
            fl, nfl = ("flf", "nflf") if d == 0 else ("flb", "nflb")
            for j in js:
                for h in range(4):
                    sj = SJ[sbn[0] % 2]
                    sbn[0] += 1
                    dma(sj[:], sums_in[j, d, h], writes=[sj.b()])
                    fc, pb = (h, 0) if even else (h // 2, 64 * (h % 2))
                    rows = slice(pb, pb + KD)
                    col = j * 8 + d * 4 + fc
                    op("dve", lambda e, col=col, j=j: e.tensor_scalar(out=MJ[:], in0=SD[:, col:col + 1], scalar1=P.V(fl, j), scalar2=P.V(nfl, j), op0=ALU.mult, op1=ALU.add),
                       [SD.b(), vec.b()], [MJ.b()])
                    op("pool", lambda e, sj=sj, j=j: e.tensor_scalar(out=sj[:], in0=sj[:], scalar1=P.V(fl, j), scalar2=None, op0=ALU.mult), [sj.b(), vec.b()], [sj.b()])
                    sp = S[h][scur[h]]; sn = S[h][1 - scur[h]]
                    op("dve", lambda e, sp=sp, sn=sn, sj=sj, rows=rows: e.scalar_tensor_tensor(out=sn[rows, :], in0=sp[rows, :], scalar=MJ[rows, 0:1], in1=sj[rows, :],
                                                                                          op0=ALU.mult, op1=ALU.add), [sp.b(), sj.b(), MJ.b()], [sn.b()])
                    scur[h] = 1 - scur[h]

        def attention(parts, h, n, out_t, kbs, scale):
            nsb = (kbs + 9) // 10
            sbinfo = []

            def load_sb(sbi):
                kb0 = sbi * 10
                nk = min(10, kbs - kb0)
                bi = sbn[0] % 2
                sbn[0] += 1
                for (qv, ksrc, r0, nr, kbuf) in parts:
                    dma(kbuf[bi][r0:r0 + nr, 0:nk * 128], ksrc[:, kb0 * 128:(kb0 + nk) * 128], writes=[kbuf[bi].b()])
                dma(Vb[bi][:, 0:nk, :], v_in[h, :, kb0:kb0 + nk, :], writes=[Vb[bi].b()])
                sbinfo.append(bi)

            blocks = [(sbi, kb) for sbi in range(nsb) for kb in range(min(10, kbs - sbi * 10))]
            nb = len(blocks)

            def qk(i):
                sbi, kb = blocks[i]
                bi = sbinfo[sbi]
                pss = PS[i % 2]
                for pi, (qv, ksrc, r0, nr, kbuf) in enumerate(parts):
                    op("pe", lambda e, qv=qv, r0=r0, nr=nr, kbuf=kbuf, kb=kb, pi=pi, pss=pss, bi=bi: e.matmul(
                        pss[:, 0:n], lhsT=kbuf[bi][r0:r0 + nr, kb * 128:(kb + 1) * 128], rhs=qv, start=(pi == 0), stop=(pi == len(parts) - 1)),
                       [kbuf[bi].b(), QN.b(), QR.b()], [pss.b()])

            load_sb(0)
            if nsb > 1:
                load_sb(1)
            qk(0)
            for i, (sbi, kb) in enumerate(blocks):
                if kb == 0 and sbi >= 1 and sbi + 1 < nsb:
                    load_sb(sbi + 1)
                if i + 1 < nb:
                    qk(i + 1)
                bi = sbinfo[sbi]
                pss = PS[i % 2]
                pt = PT[i % 2]
                op("act", lambda e, pss=pss, pt=pt: e.activation(out=pt[:, 0:n], in_=pss[:, 0:n], func=AF.Exp, scale=scale), [pss.b()], [pt.b()])
                first = (i == 0)
                last = (i == nb - 1)
                op("pe", lambda e, kb=kb, pt=pt, last=last, first=first, bi=bi: e.matmul(PS[2][:, 0:n], lhsT=Vb[bi][:, kb, :], rhs=pt[:, 0:n], start=first, stop=last),
                   [Vb[bi].b(), pt.b()], [PS[2].b()])
                acc = mrow[i % 2]
                eng = "dve" if i % 2 == 0 else "pool"
                if i < 2:
                    op(eng, lambda e, acc=acc, pt=pt: e.tensor_copy(out=acc[:, 0:n], in_=pt[:, 0:n]), [pt.b()], [acc.b()])
                else:
                    op(eng, lambda e, acc=acc, pt=pt: e.tensor_tensor(out=acc[:, 0:n], in0=acc[:, 0:n], in1=pt[:, 0:n], op=ALU.add), [pt.b(), acc.b()], [acc.b()])
            nacc = min(2, nb)
            for ai in range(nacc):
                op("pe", lambda e, ai=ai: e.matmul(PS[3][:, 0:n], lhsT=ONESf, rhs=mrow[ai][:, 0:n], start=(ai == 0), stop=(ai == nacc - 1)),
                   [mrow[ai].b(), cmf.b()], [PS[3].b()])
            rc = TMP[0]
            op("dve", lambda e: e.reciprocal(out=rc[:, 0:n], in_=PS[3][:, 0:n]), [PS[3].b()], [rc.b()])
            op("dve", lambda e: e.tensor_tensor(out=out_t[:, 0:n], in0=PS[2][:, 0:n], in1=rc[:, 0:n], op=ALU.mult), [PS[2].b(), rc.b()], [out_t.b()])

        def head_rms_gate(src_view, n, gain_ap, gate_view, out_view, src_b, out_b, lhs_ones, inv_dim, extra_scale=None):
            sqv = SQ[:, 4, 0:n]
            op("pool", lambda e: e.tensor_tensor(out=sqv, in0=src_view, in1=src_view, op=ALU.mult), [src_b], [SQ.b()])
            op("pe", lambda e: e.matmul(PS[6][:, 0:n], lhsT=lhs_ones, rhs=sqv, start=True, stop=True), [SQ.b(), cmb.b()], [PS[6].b()])
            rstd_from(PS[6], n, inv_dim, RSTD)
            t = TMP[1]
            op("dve", lambda e: e.scalar_tensor_tensor(out=t[:, 0:n], in0=src_view, scalar=gain_ap, in1=RSTD[:, 0:n], op0=ALU.mult, op1=ALU.mult),
               [src_b, RSTD.b(), vec.b()], [t.b()])
            if gate_view is not None:
                op("dve", lambda e: e.tensor_tensor(out=out_view, in0=t[:, 0:n], in1=gate_view, op=ALU.mult), [t.b(), GT.b()], [out_b])
            else:
                op("dve", lambda e: e.tensor_scalar(out=out_view, in0=t[:, 0:n], scalar1=extra_scale, scalar2=None, op0=ALU.mult), [t.b(), vec.b()], [out_b])

        if not even:
            lp = P.sb("lp", [128, 2])
            op("dve", lambda e: e.tensor_tensor(out=lp[:, 0:1], in0=P.V("lam", 0), in1=P.V("lam", 1), op=ALU.mult), [vec.b()], [lp.b()])
            op("dve", lambda e: e.tensor_tensor(out=lp[:, 1:2], in0=P.V("lam", 2), in1=P.V("lam", 3), op=ALU.mult), [vec.b()], [lp.b()])
            op("pe", lambda e: e.matmul(PS[6][:, 0:2], lhsT=ONESf, rhs=lp[:], start=True, stop=True), [lp.b(), cmf.b()], [PS[6].b()])
            op("act", lambda e: e.activation(out=lp[:], in_=PS[6][:, 0:2], func=AF.Exp), [PS[6].b()], [lp.b()])
            op("dve", lambda e: e.tensor_tensor(out=NLAM[:], in0=lp[:, 1:2], in1=lp[:, 0:1], op=ALU.subtract), [lp.b()], [NLAM.b()])
            op("dve", lambda e: e.tensor_tensor(out=NLAM[:], in0=NLAM[:], in1=P.V("laminit"), op=ALU.subtract), [NLAM.b(), vec.b()], [NLAM.b()])

        def mixer_attn_even(ht, ti):
            c0, n, isc = TILES[ti]
            kbs = 2 if isc else NKB
            w, wv = load_w(win_bf, "win", 0, D, 2560, 384)
            for cc in range(3):
                lin_fm(PS[0], n, wv, cc * 128, 128, ht, ht.t, 8, extra_reads=[w.b()])
                op("act", lambda e, cc=cc: e.copy(out=CQ[:, cc, 0:n], in_=PS[0][:, 0:n]), [PS[0].b()], [CQ.b()])
                op("pool", lambda e, cc=cc: e.tensor_tensor(out=SQ[:, cc, 0:n], in0=CQ[:, cc, 0:n], in1=CQ[:, cc, 0:n], op=ALU.mult), [CQ.b()], [SQ.b()])
            for cc in range(3):
                op("pe", lambda e, cc=cc: e.matmul(PS[6][:, 0:n], lhsT=ONESb, rhs=SQ[:, cc, 0:n], start=(cc == 0), stop=(cc == 2)), [SQ.b(), cmb.b()], [PS[6].b()])
            rstd_from(PS[6], n, 1.0 / 384, RSTD)
            for cc in range(3):
                op("dve", lambda e, cc=cc: e.scalar_tensor_tensor(out=CQN[:, cc, 0:n], in0=CQ[:, cc, 0:n], scalar=P.V("qn", cc), in1=RSTD[:, 0:n], op0=ALU.mult, op1=ALU.mult),
                   [CQ.b(), RSTD.b(), vec.b()], [CQN.b()])
            w2, w2v = load_w(wuq_bf, "wuq", 0, 384, 0, 768)
            for h in range(4):
                lin_fm(PS[0], n, w2v, h * 192, 128, CQN, CQN.t, 3, extra_reads=[w2.b()])
                lin_fm(PS[1], n, w2v, h * 192 + 128, 64, CQN, CQN.t, 3, extra_reads=[w2.b()])
                op("act", lambda e: e.copy(out=QX[:, 0:n], in_=PS[0][:, 0:n]), [PS[0].b()], [QX.b()])
                op("act", lambda e: e.copy(out=QY[0:64, 0:n], in_=PS[1][0:64, 0:n]), [PS[1].b()], [QY.b()])
                op("pool", lambda e: e.tensor_tensor(out=SQ[:, 3, 0:n], in0=QX[:, 0:n], in1=QX[:, 0:n], op=ALU.mult), [QX.b()], [SQ.b()])
                op("pool", lambda e: e.tensor_tensor(out=SQ[0:64, 2, 0:n], in0=QY[0:64, 0:n], in1=QY[0:64, 0:n], op=ALU.mult), [QY.b()], [SQ.b()])
                op("pe", lambda e: e.matmul(PS[6][:, 0:n], lhsT=ONESb, rhs=SQ[:, 3, 0:n], start=True, stop=False), [SQ.b(), cmb.b()], [PS[6].b()])
                op("pe", lambda e: e.matmul(PS[6][:, 0:n], lhsT=ONESb[0:64, :], rhs=SQ[0:64, 2, 0:n], start=False, stop=True), [SQ.b(), cmb.b()], [PS[6].b()])
                rstd_from(PS[6], n, 1.0 / 192, RSTD)
                op("dve", lambda e, h=h: e.scalar_tensor_tensor(out=QN[:, h, 0:n], in0=QX[:, 0:n], scalar=P.V("qkq_n"), in1=RSTD[:, 0:n], op0=ALU.mult, op1=ALU.mult),
                   [QX.b(), RSTD.b(), vec.b()], [QN.b()])
                op("dve", lambda e: e.scalar_tensor_tensor(out=QY[0:64, 0:n], in0=QY[0:64, 0:n], scalar=P.V("qkq_r", rows=64), in1=RSTD[0:64, 0:n], op0=ALU.mult, op1=ALU.mult),
                   [QY.b(), RSTD.b(), vec.b()], [QY.b()])
                rope(QY, QY[0:64, 0:n], 64, c0, n, QR[0:64, h, 0:n], QR, PS[1])
            for h in range(4):
                ao = AO[h % 2]
                attention([(QN[:, h, 0:n], kn_in[h], 0, 128, KNb), (QR[0:64, h, 0:n], kr_in[h], 0, 64, KRb)], h, n, ao, kbs, 192 ** -0.5)
                op("act", lambda e, h=h, ao=ao: e.copy(out=YT[:, 4 + h, 0:n], in_=ao[:, 0:n]), [ao.b()], [YT.b()])

        def mixer_attn_odd(ht, ti):
            c0, n, isc = TILES[ti]
            kbs = 2 if isc else NKB
            w, wv = load_w(win_bf, "win", 0, D, 0, 512)
            for h in range(4):
                lin_fm(PS[0], n, wv, h * 128, 128, ht, ht.t, 8, extra_reads=[w.b()])
                op("act", lambda e: e.copy(out=QX[:, 0:n], in_=PS[0][:, 0:n]), [PS[0].b()], [QX.b()])
                op("pool", lambda e: e.tensor_tensor(out=SQ[:, 3, 0:n], in0=QX[:, 0:n], in1=QX[:, 0:n], op=ALU.mult), [QX.b()], [SQ.b()])
                op("pe", lambda e: e.matmul(PS[6][:, 0:n], lhsT=BDb, rhs=SQ[:, 3, 0:n], start=True, stop=True), [SQ.b(), cmb.b()], [PS[6].b()])
                rstd_from(PS[6], n, 1.0 / 64, RSTD)
                op("dve", lambda e: e.scalar_tensor_tensor(out=QY[:, 0:n], in0=QX[:, 0:n], scalar=P.V("daq"), in1=RSTD[:, 0:n], op0=ALU.mult, op1=ALU.mult),
                   [QX.b(), RSTD.b(), vec.b()], [QY.b()])
                rope(QY, QY[:, 0:n], 128, c0, n, QN[:, h, 0:n], QN, PS[1])
            for h in range(4):
                attention([(QN[0:64, h, 0:n], kn_in[h, 0:64], 0, 64, KNb)], h, n, AO[0], kbs, 0.125)
                attention([(QN[64:128, h, 0:n], kn_in[h, 64:128], 64, 64, KNb)], h, n, AO[1], kbs, 0.125)
                op("dve", lambda e: e.scalar_tensor_tensor(out=AO[0][:, 0:n], in0=AO[1][:, 0:n], scalar=NLAM[:, 0:1], in1=AO[0][:, 0:n], op0=ALU.mult, op1=ALU.add),
                   [AO[0].b(), AO[1].b(), NLAM.b()], [AO[0].b()])
                head_rms_gate(AO[0][:, 0:n], n, P.V("subln"), None, YT[:, h, 0:n], AO[0].b(), YT.b(), ONESb, 1.0 / 128, extra_scale=P.V("omlam"))

        def gate(ht, n):
            gc0 = 2048 if even else 2560
            w, wv = load_w(win_bf, "win", 0, D, gc0, 512)
            for cc in range(4):
                lin_fm(PS[0], n, wv, cc * 128, 128, ht, ht.t, 8, extra_reads=[w.b()])
                op("act", lambda e, cc=cc: e.activation(out=GT[:, cc, 0:n], in_=PS[0][:, 0:n], func=AF.Silu), [PS[0].b()], [GT.b()])

        def finish_tile(xt, ti):
            c0, n, isc = TILES[ti]
            yoff = 0 if even else 4
            for h in range(4):
                src = OACC[:, h, c0:c0 + n]
                op("act", lambda e, src=src: e.copy(out=QX[:, 0:n], in_=src), [OACC.b((h, c0))], [QX.b()])
                head_rms_gate(QX[:, 0:n], n, P.V("hgn" if even else "rtn"), GT[:, h, 0:n], YT[:, yoff + h, 0:n], QX.b(), YT.b(), ONESb, 1.0 / 128)
            for oc in range(8):
                if oc % 4 == 0:
                    w, wv = load_w(wo_bf, "wo", 0, D, oc * 128, 512)
                lin_fm(PS[oc % 2], n, wv, (oc % 4) * 128, 128, YT, YT.t, 8, extra_reads=[w.b()])
                op("dve", lambda e, oc=oc: e.scalar_tensor_tensor(out=xt[:, oc, 0:n], in0=PS[oc % 2][:, 0:n], scalar=MOD[:, 16 + oc, isc:isc + 1], in1=xt[:, oc, 0:n],
                                                                 op0=ALU.mult, op1=ALU.add), [PS[oc % 2].b(), MOD.b(), xt.b()], [xt.b()])
            norm_mod(xt, n, isc, A2, 24, HT)
            for hb in range(8):
                w1, w1v = load_w(w1_bf, "w1", 0, D, hb * 512, 512)
                w2, w2v = load_w(w2_bf, "w2", hb * 512, 512, 0, 1024)
                hid = HID[hb % 2]
                for hc in range(4):
                    lin_fm(PS[hc % 2], n, w1v, hc * 128, 128, HT, HT.t, 8, extra_reads=[w1.b()])
                    r = TMP[hc % 2]
                    op("act", lambda e, hc=hc, r=r: e.activation(out=r[:, 0:n], in_=PS[hc % 2][:, 0:n], func=AF.Relu), [PS[hc % 2].b()], [r.b()])
                    op("dve", lambda e, hc=hc, r=r: e.tensor_tensor(out=hid[:, hc, 0:n], in0=PS[hc % 2][:, 0:n], in1=r[:, 0:n], op=ALU.mult), [PS[hc % 2].b(), r.b()], [hid.b()])
                for oc in range(8):
                    pst = PS[2 + oc % 2]
                    lin_fm(pst, n, w2v, oc * 128, 128, hid, hid.t, 4, extra_reads=[w2.b()])
                    op("dve", lambda e, oc=oc, pst=pst: e.scalar_tensor_tensor(out=xt[:, oc, 0:n], in0=pst[:, 0:n], scalar=MOD[:, 40 + oc, isc:isc + 1], in1=xt[:, oc, 0:n],
                                                                             op0=ALU.mult, op1=ALU.add), [pst.b(), MOD.b(), xt.b()], [xt.b()])
            dma(xT_out[:, c0:c0 + n].rearrange("(kc p) t -> p kc t", p=128), xt[:, :, 0:n], reads=[xt.b()], writes=[xob])

        zero_states()
        for ti in range(5):
            xt = XT[ti % 2]
            load_x(ti, xt)
            c0, n, isc = TILES[ti]
            norm_mod(xt, n, isc, A1, 0, HT)
            if ti == 1:
                fold(0)
            rec_tile(HT, ti, 0, True)
        zero_states()
        for ti in (0, 4, 3, 2, 1):
            xt = XT[ti % 2]
            load_x(ti, xt)
            c0, n, isc = TILES[ti]
            norm_mod(xt, n, isc, A1, 0, HT)
            if ti == 4:
                fold(1)
            rec_tile(HT, ti, 1, True)
            gate(HT, n)
            (mixer_attn_even if even else mixer_attn_odd)(HT, ti)
            finish_tile(xt, ti)
        k.finish([xob])
        P.n_instr = k.n
    return P


_PROGS = {}


def get_prog(even, phase):
    key = (even, phase)
    if key not in _PROGS:
        _PROGS[key] = build(even, phase)
    return _PROGS[key]


def run_layer(inp, l, xT_sh, consts):
    even = l % 2 == 0
    j = l // 2
    f32 = lambda a: np.ascontiguousarray(np.asarray(a, np.float32))
    w_in = f32(inp["a_w_in"][j] if even else inp["c_w_in"][j])
    ada = f32(inp["ada_w"][l])
    vecs = [build_vecs(inp, l, c) for c in range(NCORES)]
    base = []
    for c in range(NCORES):
        m = {"xT": xT_sh[c], "vecs": vecs[c], "cos": consts[c]["cos"], "sin": consts[c]["sin"], "cmat": consts[c]["cmat"],
             "ada_w": ada, "w_in": w_in}
        if even:
            m["w_ukv"] = f32(inp["mla_w_ukv"][j])
        base.append(m)
    pa = get_prog(even, "A")
    ra = run_bass_kernel_spmd(pa.nc, base, core_ids=list(range(NCORES))).results
    cat = lambda name, ax, sl_ctx, sl_lat: np.ascontiguousarray(np.concatenate([sl_ctx(ra[0][name])] + [sl_lat(ra[c][name]) for c in range(NCORES)], axis=ax))
    kn_all = cat("kn", 2, lambda a: a[:, :, :NCTX], lambda a: a[:, :, NCTX:])
    vtm = np.concatenate([ra[0]["v"][:NCTX // 128]] + [ra[c]["v"][NCTX // 128:] for c in range(NCORES)], axis=0)
    v_all = np.ascontiguousarray(vtm.reshape(NKB, 128, 4, 128).transpose(2, 1, 0, 3))
    sumS = np.ascontiguousarray(np.stack([ra[c]["sumS"] for c in range(NCORES)], 0))
    sumD = np.ascontiguousarray(np.concatenate([ra[c]["sumD"] for c in range(NCORES)], 1))
    pb = get_prog(even, "B")
    maps = []
    for c in range(NCORES):
        m = dict(base[c])
        m.update({"kn_all": kn_all, "v_all": v_all, "sumS_all": sumS, "sumD_all": sumD,
                  "w_o": f32(inp["w_o"][l]), "w1": f32(inp["mlp_w1"][l]), "w2": f32(inp["mlp_w2"][l])})
        if even:
            m["kr_all"] = cat("kr", 2, lambda a: a[:, :, :NCTX], lambda a: a[:, :, NCTX:])
            m["w_uq"] = f32(inp["mla_w_uq"][j])
        maps.append(m)
    rb = run_bass_kernel_spmd(pb.nc, maps, core_ids=list(range(NCORES))).results
    return [np.ascontiguousarray(rb[c]["xTo"]) for c in range(NCORES)]


def kernel(**inp):
    x = np.asarray(inp["x"], np.float32)[0]
    ctx = np.asarray(inp["ctx"], np.float32)[0]
    consts = [const_tables(c) for c in range(NCORES)]
    xT_sh = [np.ascontiguousarray(np.concatenate([ctx.T, x[c * LPC:(c + 1) * LPC].T], axis=1)) for c in range(NCORES)]
    for l in range(4):
        xT_sh = run_layer(inp, l, xT_sh, consts)
    out = np.concatenate([xT_sh[c][:, NCTX:].T for c in range(NCORES)], axis=0)
    return np.ascontiguousarray(out[None].astype(np.float32))
```

```python
import contextlib
import math
import numpy as np
import concourse.bass as bass
import concourse.mybir as mybir
from concourse.bass_utils import run_bass_kernel_spmd

F32 = mybir.dt.float32
BF16 = mybir.dt.bfloat16
AF = mybir.ActivationFunctionType
ALU = mybir.AluOpType

NCORES = 8
DBG = 0
D = 1024
NLAT = 16384
NCTX = 256
LPC = NLAT // NCORES
TPC = NCTX + LPC
TALL = NCTX + NLAT
NKB = TALL // 128
EPS = 1e-6
TILES = [(0, 256, 1), (256, 512, 0), (768, 512, 0), (1280, 512, 0), (1792, 512, 0)]


class Buf:
    __slots__ = ("lw", "rd")

    def __init__(self):
        self.lw = None
        self.rd = []


class Eng:
    def __init__(self, eng, sem):
        self.eng, self.sem = eng, sem
        self.count = 0
        self.seen = {}


class K:
    def __init__(self, nc, sems):
        self.nc = nc
        self.E = {"pe": Eng(nc.tensor, sems[0]), "dve": Eng(nc.vector, sems[1]),
                  "act": Eng(nc.scalar, sems[2]), "pool": Eng(nc.gpsimd, sems[3])}
        self.dmaq = Eng(nc.sync, None)
        self.dsems = sems[4:]
        self.dcnt = [0] * len(self.dsems)
        self.dnext = 0
        self.n = 0

    def _deps(self, e, reads, writes):
        deps = {}
        for b in reads:
            if b.lw is not None:
                k, c = b.lw
                if deps.get(k, 0) < c:
                    deps[k] = c
        for b in writes:
            if b.lw is not None:
                k, c = b.lw
                if deps.get(k, 0) < c:
                    deps[k] = c
            for k, c in b.rd:
                if deps.get(k, 0) < c:
                    deps[k] = c
        for k, c in deps.items():
            if e.seen.get(k, 0) >= c:
                continue
            if k is e.sem and e is self.E["pe"]:
                continue
            e.seen[k] = c
            e.eng.wait_ge(k, c)

    def _mark(self, key, reads, writes):
        for b in reads:
            b.rd.append(key)
            if len(b.rd) > 24:
                m = {}
                for k, c in b.rd:
                    if m.get(k, 0) < c:
                        m[k] = c
                b.rd = list(m.items())
        for b in writes:
            b.lw = key
            b.rd = []

    def op(self, en, fn, reads=(), writes=()):
        e = self.E[en]
        self._deps(e, reads, writes)
        ins = fn(e.eng)
        e.count += 1
        ins.then_inc(e.sem, 1)
        self._mark((e.sem, e.count), reads, writes)
        self.n += 1

    def dma(self, out, in_, reads=(), writes=()):
        e = self.dmaq
        i = self.dnext
        self.dnext = (i + 1) % len(self.dsems)
        s = self.dsems[i]
        if self.dcnt[i] > 0 and e.seen.get(s, 0) < self.dcnt[i]:
            e.eng.wait_ge(s, self.dcnt[i])
            e.seen[s] = self.dcnt[i]
        self._deps(e, reads, writes)
        ins = e.eng.dma_start(out=out, in_=in_)
        self.dcnt[i] += 16
        ins.then_inc(s, 16)
        self._mark((s, self.dcnt[i]), reads, writes)
        self.n += 1

    def finish(self, bufs):
        self._deps(self.dmaq, bufs, bufs)


class T:
    def __init__(self, t):
        self.t = t
        self._b = {}

    def b(self, key=0):
        if key not in self._b:
            self._b[key] = Buf()
        return self._b[key]

    def __getitem__(self, idx):
        return self.t[idx]


def fm(v):
    v = np.asarray(v, np.float32)
    return np.ascontiguousarray(v.reshape(-1, 128).T)


def pad128(v):
    o = np.zeros(128, np.float32)
    o[: v.shape[0]] = v
    return o[:, None]


class VecPack:
    def __init__(self):
        self.cols = []
        self.off = {}
        self.n = 0

    def add(self, name, arr):
        arr = np.asarray(arr, np.float32)
        assert arr.shape[0] == 128
        self.off[name] = self.n
        self.cols.append(arr)
        self.n += arr.shape[1]

    def pack(self):
        return np.ascontiguousarray(np.concatenate(self.cols, axis=1))


def vec_layout(even):
    off = {}
    n = 0

    def a(name, w):
        nonlocal n
        off[name] = n
        n += w
    a("nw1", 8); a("nw2", 8); a("adab", 48); a("c", 8); a("cctx", 8)
    a("flf", 8); a("nflf", 8); a("flb", 8); a("nflb", 8); a("eps", 1); a("one", 1); a("rmask", 4)
    if even:
        a("lb0", 8); a("lb1", 8); a("hgn", 1); a("qn", 3); a("kvn", 2)
        a("qkq_n", 1); a("qkq_r", 1); a("qkk_n", 1); a("qkk_r", 1); a("lbsel", 1)
    else:
        a("daq", 1); a("dak", 1); a("subln", 1); a("rtn", 1); a("rtd", 4); a("lam", 4)
        a("laminit", 1); a("omlam", 1)
    return off, n


def build_vecs(inp, l, core):
    even = l % 2 == 0
    j = l // 2
    vp = VecPack()
    vp.add("nw1", fm(inp["norm_w"][l, 0])); vp.add("nw2", fm(inp["norm_w"][l, 1]))
    vp.add("adab", fm(inp["ada_b"][l])); vp.add("c", fm(inp["c"][0])); vp.add("cctx", fm(inp["c_ctx"]))
    flf = np.zeros((128, 8), np.float32); flb = np.zeros((128, 8), np.float32)
    flf[:, :core] = 1.0
    flb[:, core + 1:] = 1.0
    vp.add("flf", flf); vp.add("nflf", 1.0 - flf); vp.add("flb", flb); vp.add("nflb", 1.0 - flb)
    vp.add("eps", np.full((128, 1), EPS, np.float32)); vp.add("one", np.ones((128, 1), np.float32))
    rm = np.zeros((128, 4), np.float32)
    for cc in range(4):
        rm[cc * 32:(cc + 1) * 32, cc] = 1.0
    vp.add("rmask", rm)
    if even:
        vp.add("lb0", fm(inp["hg_lb"][0].reshape(-1))); vp.add("lb1", fm(inp["hg_lb"][1].reshape(-1)))
        vp.add("hgn", fm(inp["hg_norm"][j])); vp.add("qn", fm(inp["mla_q_norm"][j])); vp.add("kvn", fm(inp["mla_kv_norm"][j]))
        vp.add("qkq_n", fm(inp["mla_qk_q"][j][:128])); vp.add("qkq_r", pad128(inp["mla_qk_q"][j][128:]))
        vp.add("qkk_n", fm(inp["mla_qk_k"][j][:128])); vp.add("qkk_r", pad128(inp["mla_qk_k"][j][128:]))
        vp.add("lbsel", np.full((128, 1), float(j), np.float32))
    else:
        vp.add("daq", fm(np.tile(inp["da_qk_q"][j], 2))); vp.add("dak", fm(np.tile(inp["da_qk_k"][j], 2)))
        vp.add("subln", fm(inp["da_subln"][j])); vp.add("rtn", fm(inp["rt_norm"][j]))
        rtd = np.zeros((128, 4), np.float32)
        for d in range(2):
            for c in range(2):
                rtd[:64, d * 2 + c] = inp["rt_decay"][j, d, 2 * c]
                rtd[64:, d * 2 + c] = inp["rt_decay"][j, d, 2 * c + 1]
        vp.add("rtd", rtd)
        lam = np.zeros((128, 4), np.float32)
        lam[:64, :] = inp["da_lambda"][j].T
        vp.add("lam", lam)
        li = 0.8 - 0.6 * math.exp(-0.3 * l)
        vp.add("laminit", np.full((128, 1), li, np.float32)); vp.add("omlam", np.full((128, 1), 1.0 - li, np.float32))
    off, n = vec_layout(even)
    assert off == vp.off and n == vp.n
    return vp.pack()


def const_tables(core):
    pos = np.arange(LPC) + core * LPC
    row = (pos // 64).astype(np.float32)
    col = (pos % 64).astype(np.float32)
    inv = (10000.0 ** (-np.arange(16, dtype=np.float32) / 16)).astype(np.float32)
    ang = np.concatenate([row[:, None] * inv, row[:, None] * inv, col[:, None] * inv, col[:, None] * inv], axis=1)
    cos = np.ones((128, TPC), np.float32); sin = np.zeros((128, TPC), np.float32)
    cos[:64, NCTX:] = np.cos(ang).T.astype(np.float32); cos[64:, NCTX:] = cos[:64, NCTX:]
    sin[:64, NCTX:] = np.sin(ang).T.astype(np.float32); sin[64:, NCTX:] = sin[:64, NCTX:]
    R = np.zeros((128, 128), np.float32)
    for base in (0, 32, 64, 96):
        for i in range(16):
            R[base + i + 16, base + i] = -1.0
            R[base + i, base + i + 16] = 1.0
    s = np.arange(128)[:, None]; t = np.arange(128)[None, :]
    same = (s // 32) == (t // 32)
    mf = (same & (s <= t)).astype(np.float32)
    mb = (same & (s >= t)).astype(np.float32)
    ones = np.ones((128, 128), np.float32)
    bd = ((s // 64) == (t // 64)).astype(np.float32)
    ident = np.eye(128, dtype=np.float32)
    return {"cos": cos, "sin": sin, "cmat": np.ascontiguousarray(np.stack([R, mf, mb, ones, bd, ident], 0))}


class Prog:
    def __init__(self, even, phase):
        self.even, self.phase = even, phase
        self.nc = bass.Bass("TRN2", target_bir_lowering=False)
        self.st = contextlib.ExitStack()
        self.off, self.nv = vec_layout(even)

    def din(self, name, shape, dt=F32):
        return self.nc.dram_tensor(name, list(shape), dt, kind="ExternalInput").ap()

    def dout(self, name, shape, dt=F32):
        return self.nc.dram_tensor(name, list(shape), dt, kind="ExternalOutput").ap()

    def dscr(self, name, shape, dt=BF16):
        return self.nc.dram_tensor(name, list(shape), dt, kind="Internal").ap()

    def sb(self, name, shape, dt=F32):
        return T(self.st.enter_context(self.nc.sbuf_tensor(name, list(shape), dt)))

    def ps(self, name, shape, dt=F32):
        return T(self.st.enter_context(self.nc.psum_tensor(name, list(shape), dt)))

    def V(self, name, i=0, n=1, rows=128):
        o = self.off[name] + i
        return self.vec[0:rows, o:o + n]


def build(even, phase):
    P = Prog(even, phase)
    nc = P.nc
    k = None
    WIN = 3264 if even else 3072
    xT_in = P.din("xT", [D, TPC])
    vec_in = P.din("vecs", [128, P.nv])
    cos_in = P.din("cos", [128, TPC]); sin_in = P.din("sin", [128, TPC]); cm_in = P.din("cmat", [6, 128, 128])
    ada_in = P.din("ada_w", [D, 6 * D])
    win_in = P.din("w_in", [D, WIN])
    if even:
        wukv_in = P.din("w_ukv", [256, 1024])
    if phase == "A":
        if even:
            kn_out = P.dout("kn", [4, 128, TPC], BF16); kr_out = P.dout("kr", [4, 64, TPC], BF16)
        else:
            kn_out = P.dout("kn", [4, 128, TPC], BF16)
        v_out = P.dout("v", [TPC // 128, 128, 512], BF16)
        sums_out = P.dout("sumS", [2, 4, 128, 128]); sumd_out = P.dout("sumD", [128, 8])
    else:
        xT_out = P.dout("xTo", [D, TPC])
        kn_in = P.din("kn_all", [4, 128, TALL], BF16)
        if even:
            kr_in = P.din("kr_all", [4, 64, TALL], BF16)
            wuq_in = P.din("w_uq", [384, 768])
        v_in = P.din("v_all", [4, 128, NKB, 128], BF16)
        sums_in = P.din("sumS_all", [NCORES, 2, 4, 128, 128]); sumd_in = P.din("sumD_all", [128, NCORES * 8])
        wo_in = P.din("w_o", [D, D]); w1_in = P.din("w1", [D, 4 * D]); w2_in = P.din("w2", [4 * D, D])
        wo_bf = P.dscr("wo_bf", [D, D]); w1_bf = P.dscr("w1_bf", [D, 4 * D]); w2_bf = P.dscr("w2_bf", [4 * D, D])
        if even:
            wuq_bf = P.dscr("wuq_bf", [384, 768])
    win_bf = P.dscr("win_bf", [D, WIN])
    if even:
        wukv_bf = P.dscr("wukv_bf", [256, 1024])

    st = P.st
    with st:
        sems = [st.enter_context(nc.semaphore(f"s{i}")) for i in range(4 + 10)]
        k = K(nc, sems)
        op, dma = k.op, k.dma
        P.vec = None
        vec = P.sb("vec", [128, P.nv]); P.vec = vec.t
        cosT = P.sb("cosT", [128, 512]); sinT = P.sb("sinT", [128, 512])
        cmf = P.sb("cmf", [128, 6, 128]); cmb = P.sb("cmb", [128, 6, 128], BF16)
        dma(vec[:], vec_in, writes=[vec.b()])
        dma(cmf[:], cm_in.rearrange("c p d -> p c d"), writes=[cmf.b()])
        op("dve", lambda e: e.tensor_copy(out=cmb[:], in_=cmf[:]), [cmf.b()], [cmb.b()])
        RMf = cmf[:, 0, :]
        MASK = {0: cmf[:, 1, :], 1: cmf[:, 2, :]}
        ONESb = cmb[:, 3, :]; BDb = cmb[:, 4, :]; IDb = cmb[:, 5, :]; ONESf = cmf[:, 3, :]
        CB = [vec.b(), cmf.b(), cmb.b(), cosT.b(), sinT.b()]

        PS = [P.ps(f"ps{i}", [128, 512]) for i in range(7)]
        PSB = P.ps("psb", [128, 1024], BF16)

        stg = [P.sb(f"stg{i}", [128, 1024]) for i in range(2)]
        stgb = [P.sb(f"stgb{i}", [128, 1024], BF16) for i in range(2)]
        prep_n = [0]
        WB = {}

        def prep(src, dst, rows, cols, name, col_ranges=None):
            WB[name] = Buf()
            pieces = []
            for r0 in range(0, rows, 128):
                rr = min(128, rows - r0)
                for (a0, an) in (col_ranges or [(0, cols)]):
                    for c in range(a0, a0 + an, 1024):
                        pieces.append((r0, rr, c, min(1024, a0 + an - c)))

            def load(pi):
                r0, rr, c0, cn = pieces[pi]
                i = (prep_n[0] + pi) % 2
                dma(stg[i][0:rr, 0:cn], src[r0:r0 + rr, c0:c0 + cn], writes=[stg[i].b()])

            load(0)
            for pi, (r0, rr, c0, cn) in enumerate(pieces):
                i = (prep_n[0] + pi) % 2
                if pi + 1 < len(pieces):
                    load(pi + 1)
                op("pool" if i else "dve", lambda e, i=i, rr=rr, cn=cn: e.tensor_copy(out=stgb[i][0:rr, 0:cn], in_=stg[i][0:rr, 0:cn]),
                   [stg[i].b()], [stgb[i].b()])
                dma(dst[r0:r0 + rr, c0:c0 + cn], stgb[i][0:rr, 0:cn], reads=[stgb[i].b()], writes=[WB[name]])
            prep_n[0] += len(pieces)

        if phase == "A":
            if even:
                prep(win_in, win_bf, D, WIN, "win", [(512, 1536), (2944, 320)])
                prep(wukv_in, wukv_bf, 256, 1024, "wukv")
            else:
                prep(win_in, win_bf, D, WIN, "win", [(512, 1024), (1792, 768)])
        else:
            if even:
                prep(win_in, win_bf, D, WIN, "win", [(0, 2048), (2048, 896)])
                prep(wuq_in, wuq_bf, 384, 768, "wuq")
            else:
                prep(win_in, win_bf, D, WIN, "win", [(0, 512), (1536, 1536)])
            prep(wo_in, wo_bf, D, D, "wo"); prep(w1_in, w1_bf, D, 4 * D, "w1"); prep(w2_in, w2_bf, 4 * D, D, "w2")

        scT = P.sb("scT", [128, 8, 2]); MOD = P.sb("MOD", [128, 48, 2])
        for kc in range(8):
            op("act", lambda e, kc=kc: e.activation(out=scT[:, kc, 0:1], in_=P.V("c", kc), func=AF.Silu), [vec.b()], [scT.b()])
            op("act", lambda e, kc=kc: e.activation(out=scT[:, kc, 1:2], in_=P.V("cctx", kc), func=AF.Silu), [vec.b()], [scT.b()])
        ncb = 2 if phase == "A" else 6
        IDf = cmf[:, 5, :]
        mrow = [P.sb(f"mrow{i}", [128, 512]) for i in range(2)]
        an = 0
        for cb in range(ncb):
            for kc in range(8):
                ab = stg[an % 2]
                an += 1
                dma(ab[:, 0:1024], ada_in[kc * 128:(kc + 1) * 128, cb * 1024:(cb + 1) * 1024], writes=[ab.b()])
                for half in range(2):
                    op("pe", lambda e, kc=kc, ab=ab, half=half: e.matmul(PS[half][0:2, 0:512], lhsT=scT[:, kc, :], rhs=ab[:, half * 512:(half + 1) * 512],
                                                                        start=(kc == 0), stop=(kc == 7)), [ab.b(), scT.b()], [PS[half].b()])
            for half in range(2):
                op("act", lambda e, half=half: e.copy(out=mrow[half][0:2, 0:512], in_=PS[half][0:2, 0:512]), [PS[half].b()], [mrow[half].b()])
            for j in range(8):
                ch = cb * 8 + j
                half, jj = j // 4, j % 4
                op("pe", lambda e, half=half, jj=jj: e.matmul(PS[6][:, 0:2], lhsT=mrow[half][0:2, jj * 128:(jj + 1) * 128], rhs=IDf[0:2, 0:2], start=True, stop=True),
                   [mrow[half].b(), cmf.b()], [PS[6].b()])
                op("dve", lambda e, ch=ch: e.tensor_scalar(out=MOD[:, ch, :], in0=PS[6][:, 0:2], scalar1=P.V("adab", ch), scalar2=None, op0=ALU.add),
                   [PS[6].b(), vec.b()], [MOD.b()])
        A1 = P.sb("A1", [128, 8, 2]); A2 = P.sb("A2", [128, 8, 2])
        for kc in range(8):
            op("dve", lambda e, kc=kc: e.tensor_scalar(out=A1[:, kc, :], in0=MOD[:, 8 + kc, :], scalar1=1.0, scalar2=P.V("nw1", kc), op0=ALU.add, op1=ALU.mult),
               [MOD.b(), vec.b()], [A1.b()])
            if phase == "B":
                op("dve", lambda e, kc=kc: e.tensor_scalar(out=A2[:, kc, :], in0=MOD[:, 32 + kc, :], scalar1=1.0, scalar2=P.V("nw2", kc), op0=ALU.add, op1=ALU.mult),
                   [MOD.b(), vec.b()], [A2.b()])
        CB += [MOD.b(), A1.b(), A2.b()]

        XT = [P.sb("XT0", [128, 8, 512])] * 2
        HT = P.sb("HT", [128, 8, 512], BF16)
        SQ = P.sb("SQ", [128, 8, 512], BF16)
        RSTD = P.sb("RSTD", [128, 512]); TMP = [P.sb(f"TMP{i}", [128, 512]) for i in range(4)]
        WBUF = [P.sb(f"WBUF{i}", [128, 4096], BF16) for i in range(3)]
        wn = [0]

        def load_w(wbf, name, r0, rows, c0, cols):
            w = WBUF[wn[0] % 3]
            wn[0] += 1
            kcn = max(1, rows // 128)
            pr = min(rows, 128)
            view = w.t[0:pr, 0:kcn * cols].rearrange("p (kc n) -> p kc n", kc=kcn)
            if rows >= 128:
                src = wbf[r0:r0 + rows, c0:c0 + cols].rearrange("(kc p) n -> p kc n", p=128)
            else:
                src = wbf[r0:r0 + rows, c0:c0 + cols].rearrange("(kc p) n -> p kc n", kc=1)
            dma(view, src, reads=[WB[name]], writes=[w.b()])
            return w, view

        def load_x(ti, xt):
            c0, n, _ = TILES[ti]
            dma(xt[:, :, 0:n], xT_in[:, c0:c0 + n].rearrange("(kc p) t -> p kc t", p=128), writes=[xt.b()])
            dma(cosT[:, 0:n], cos_in[:, c0:c0 + n], writes=[cosT.b()])
            dma(sinT[:, 0:n], sin_in[:, c0:c0 + n], writes=[sinT.b()])

        def rstd_from(ps_t, n, scale, out_t, rows=128):
            op("act", lambda e: e.activation(out=out_t[0:rows, 0:n], in_=ps_t[0:rows, 0:n], func=AF.Sqrt, bias=P.V("eps", rows=rows), scale=scale),
               [ps_t.b(), vec.b()], [out_t.b()])
            op("dve", lambda e: e.reciprocal(out=out_t[0:rows, 0:n], in_=out_t[0:rows, 0:n]), [out_t.b()], [out_t.b()])

        def norm_mod(xt, n, s, A, shc, ht):
            for kc in range(8):
                op("act", lambda e, kc=kc: e.activation(out=SQ[:, kc, 0:n], in_=xt[:, kc, 0:n], func=AF.Square), [xt.b()], [SQ.b()])
            for kc in range(8):
                op("pe", lambda e, kc=kc: e.matmul(PS[6][:, 0:n], lhsT=ONESb, rhs=SQ[:, kc, 0:n], start=(kc == 0), stop=(kc == 7)),
                   [SQ.b(), cmb.b()], [PS[6].b()])
            rstd_from(PS[6], n, 1.0 / D, RSTD)
            for kc in range(8):
                tm = TMP[kc % 2]
                op("dve", lambda e, kc=kc, tm=tm: e.scalar_tensor_tensor(out=tm[:, 0:n], in0=xt[:, kc, 0:n], scalar=A[:, kc, s:s + 1], in1=RSTD[:, 0:n],
                                                                       op0=ALU.mult, op1=ALU.mult), [xt.b(), RSTD.b(), A.b()], [tm.b()])
                op("act", lambda e, kc=kc, tm=tm: e.activation(out=ht[:, kc, 0:n], in_=tm[:, 0:n], func=AF.Identity, bias=MOD[:, shc + kc, s:s + 1]),
                   [tm.b(), MOD.b()], [ht.b()])

        def lin_fm(pst, n, wview, c0, m, rhs_t, rhs_view, kcn, m0=0, extra_reads=()):
            for kc in range(kcn):
                op("pe", lambda e, kc=kc: e.matmul(pst[m0:m0 + m, 0:n], lhsT=wview[:, kc, c0:c0 + m], rhs=rhs_view[:, kc, 0:n],
                                                  start=(kc == 0), stop=(kc == kcn - 1)), [rhs_t.b()] + list(extra_reads), [pst.b()])

        def rope(x_t, x_view, rows, c0, n, out_view, out_t, pst):
            op("pe", lambda e: e.matmul(pst[0:rows, 0:n], lhsT=RMf[0:rows, 0:rows], rhs=x_view, start=True, stop=True), [x_t.b(), cmf.b()], [pst.b()])
            t1, t2 = TMP[2], TMP[3]
            op("pool", lambda e: e.tensor_tensor(out=t1[0:rows, 0:n], in0=x_view, in1=cosT[0:rows, 0:n], op=ALU.mult), [x_t.b(), cosT.b()], [t1.b()])
            op("dve", lambda e: e.tensor_tensor(out=t2[0:rows, 0:n], in0=pst[0:rows, 0:n], in1=sinT[0:rows, 0:n], op=ALU.mult), [pst.b(), sinT.b()], [t2.b()])
            op("dve", lambda e: e.tensor_tensor(out=out_view, in0=t1[0:rows, 0:n], in1=t2[0:rows, 0:n], op=ALU.add), [t1.b(), t2.b()], [out_t.b()])

        NH = 4
        KD = 128 if even else 64
        S = [[P.sb(f"S{h}_{i}", [128, 128]) for i in range(2)] for h in range(NH)]
        scur = [0] * NH
        QF = P.sb("QF", [128, 512]); KF = P.sb("KF", [128, 512]); LF = P.sb("LF", [128, 512])
        Bc = P.sb("Bc", [128, 512]); Pc = P.sb("Pc", [128, 512]); Dk = P.sb("Dk", [128, 512]); Db = P.sb("Db", [128, 512])
        Ek = P.sb("Ek", [128, 512]); Eq = P.sb("Eq", [128, 512]); Eb = P.sb("Eb", [128, 512])
        KH = P.sb("KH", [128, 512], BF16); QH = P.sb("QH", [128, 512], BF16); QTL = P.sb("QTL", [128, 512])
        KHT = P.sb("KHT", [128, 4, 128], BF16); ATM = P.sb("ATM", [128, 128], BF16)
        VT = P.sb("VT", [128, 4, 512], BF16)
        OACC = P.sb("OACC", [128, 4, TPC], BF16) if phase == "B" else None
        LTOT = P.sb("LTOT", [128, 8])

        def decay_factors(n, d):
            nch = n // 32
            op("dve", lambda e: e.tensor_tensor_scan(out=Bc[:, 0:n], data0=LF[:, 0:n], data1=LF[:, 0:n], initial=0.0, op0=ALU.add, op1=ALU.bypass),
               [LF.b()], [Bc.b()])
            op("pool", lambda e: e.tensor_tensor(out=Pc[:, 0:n], in0=Bc[:, 0:n], in1=LF[:, 0:n], op=ALU.subtract), [Bc.b(), LF.b()], [Pc.b()])
            v3 = lambda t: t[:, 0:n].rearrange("p (c t) -> p c t", t=32)
            bend = v3(Bc)[:, :, 31:32].to_broadcast([128, nch, 32])
            pst = v3(Pc)[:, :, 0:1].to_broadcast([128, nch, 32])
            if d == 0:
                op("dve", lambda e: e.tensor_tensor(out=v3(Dk), in0=bend, in1=v3(Bc), op=ALU.subtract), [Bc.b()], [Dk.b()])
                op("dve", lambda e: e.tensor_tensor(out=v3(Db), in0=v3(Bc), in1=pst, op=ALU.subtract), [Bc.b(), Pc.b()], [Db.b()])
            else:
                op("dve", lambda e: e.tensor_tensor(out=v3(Dk), in0=v3(Pc), in1=pst, op=ALU.subtract), [Pc.b()], [Dk.b()])
                op("dve", lambda e: e.tensor_tensor(out=v3(Db), in0=bend, in1=v3(Pc), op=ALU.subtract), [Bc.b(), Pc.b()], [Db.b()])
            op("pool", lambda e: e.tensor_scalar(out=Dk[:, 0:n], in0=Dk[:, 0:n], scalar1=-80.0, scalar2=None, op0=ALU.max), [Dk.b()], [Dk.b()])
            op("act", lambda e: e.activation(out=Ek[:, 0:n], in_=Dk[:, 0:n], func=AF.Exp), [Dk.b()], [Ek.b()])
            op("act", lambda e: e.activation(out=Eq[:, 0:n], in_=Dk[:, 0:n], func=AF.Exp, scale=-1.0), [Dk.b()], [Eq.b()])
            op("act", lambda e: e.activation(out=Eb[:, 0:n], in_=Db[:, 0:n], func=AF.Exp), [Db.b()], [Eb.b()])

        def rec_chunk_tile(heads, c0, n, d, want_out):
            op("pool", lambda e: e.tensor_tensor(out=KH[:, 0:n], in0=KF[:, 0:n], in1=Ek[:, 0:n], op=ALU.mult), [KF.b(), Ek.b()], [KH.b()])
            if want_out:
                op("pool", lambda e: e.tensor_tensor(out=QH[:, 0:n], in0=QF[:, 0:n], in1=Eq[:, 0:n], op=ALU.mult), [QF.b(), Eq.b()], [QH.b()])
                op("dve", lambda e: e.tensor_tensor(out=QTL[:, 0:n], in0=QF[:, 0:n], in1=Eb[:, 0:n], op=ALU.mult), [QF.b(), Eb.b()], [QTL.b()])
            nst = n // 128
            order = range(nst) if d == 0 else range(nst - 1, -1, -1)
            for sti in order:
                t0 = sti * 128
                op("pe", lambda e, t0=t0: e.transpose(PSB[:, 0:128], KH[:, t0:t0 + 128], IDb), [KH.b(), cmb.b()], [PSB.b()])
                for cm in range(4):
                    op("act", lambda e, cm=cm: e.activation(out=KHT[:, cm, :], in_=PSB[:, 0:128], func=AF.Copy, scale=P.V("rmask", cm)), [PSB.b(), vec.b()], [KHT.b()])
                for (h, pb) in heads:
                    if DBG == 95: break
                    rows = slice(pb, pb + KD)
                    if want_out:
                        op("pe", lambda e, t0=t0, rows=rows: e.matmul(PS[4][:, 0:128], lhsT=KH[rows, t0:t0 + 128], rhs=QH[rows, t0:t0 + 128], start=True, stop=True),
                           [KH.b(), QH.b()], [PS[4].b()])
                        op("dve", lambda e: e.tensor_tensor(out=ATM[:], in0=PS[4][:, 0:128], in1=MASK[d], op=ALU.mult), [PS[4].b(), cmf.b()], [ATM.b()])
                        op("pe", lambda e, h=h, sti=sti: e.matmul(PS[5][:, 0:128], lhsT=VT[:, sti, h * 128:(h + 1) * 128], rhs=ATM[:], start=True, stop=False),
                           [VT.b(), ATM.b()], [PS[5].b()])
                    corder = range(4) if d == 0 else range(3, -1, -1)
                    for ci, c in enumerate(corder):
                        sp = S[h][scur[h]]; sn = S[h][1 - scur[h]]
                        cc = t0 + c * 32
                        if want_out:
                            op("pe", lambda e, sp=sp, rows=rows, cc=cc, c=c, ci=ci: e.matmul(PS[5][:, c * 32:(c + 1) * 32], lhsT=sp[rows, :], rhs=QTL[rows, cc:cc + 32],
                                                                                    start=False, stop=(ci == 3)), [sp.b(), QTL.b()], [PS[5].b()])
                        op("pe", lambda e, c=c, h=h, sti=sti: e.matmul(PS[3][:, 0:128], lhsT=KHT[:, c, :], rhs=VT[:, sti, h * 128:(h + 1) * 128],
                                                                       start=True, stop=True), [KHT.b(), VT.b()], [PS[3].b()])
                        ecol = cc + 31 if d == 0 else cc
                        if DBG == 96: continue
                        op("dve", lambda e, sp=sp, sn=sn, rows=rows, ecol=ecol: e.scalar_tensor_tensor(out=sn[rows, :], in0=sp[rows, :], scalar=Eb[rows, ecol:ecol + 1],
                                                                                                     in1=PS[3][rows, 0:128], op0=ALU.mult, op1=ALU.add),
                           [sp.b(), Eb.b(), PS[3].b()], [sn.b()])
                        scur[h] = 1 - scur[h]
                    if want_out:
                        oc = OACC[:, h, c0 + t0:c0 + t0 + 128]
                        if d == 0:
                            op("act", lambda e, oc=oc: e.copy(out=oc, in_=PS[5][:, 0:128]), [PS[5].b()], [OACC.b((h, c0))])
                        else:
                            op("dve", lambda e, oc=oc: e.tensor_tensor(out=oc, in0=PS[5][:, 0:128], in1=oc, op=ALU.add), [PS[5].b()], [OACC.b((h, c0))])

        def zero_states():
            for h in range(NH):
                op("pool", lambda e, h=h: e.memset(S[h][scur[h]][:], 0.0), [], [S[h][scur[h]].b()])

        def v_token_major(ht, n, wname_cols):
            c0w, = wname_cols
            w, wv = load_w(win_bf, "win", 0, D, c0w, 512)
            for sti in range(n // 128):
                for kc in range(8):
                    op("pe", lambda e, kc=kc, sti=sti: e.matmul(PS[2][:, 0:512], lhsT=ht[:, kc, sti * 128:(sti + 1) * 128], rhs=wv[:, kc, :],
                                                               start=(kc == 0), stop=(kc == 7)), [ht.b(), w.b()], [PS[2].b()])
                op("act", lambda e, sti=sti: e.copy(out=VT[:, sti, :], in_=PS[2][:, 0:512]), [PS[2].b()], [VT.b()])

        LB = P.sb("LB", [128, 8]); OMLB = P.sb("OMLB", [128, 8]); LG = P.sb("LG", [128, 4])
        if even:
            op("dve", lambda e: e.tensor_tensor(out=LB[:], in0=P.V("lb1", 0, 8), in1=P.V("lb0", 0, 8), op=ALU.subtract), [vec.b()], [LB.b()])
            op("act", lambda e: e.activation(out=LB[:], in_=LB[:], func=AF.Sigmoid), [LB.b()], [LB.b()])
            op("dve", lambda e: e.tensor_scalar(out=LB[:], in0=LB[:], scalar1=P.V("lbsel"), scalar2=None, op0=ALU.mult), [LB.b(), vec.b()], [LB.b()])
            op("dve", lambda e: e.tensor_scalar(out=OMLB[:], in0=LB[:], scalar1=-1.0, scalar2=1.0, op0=ALU.mult, op1=ALU.add), [LB.b()], [OMLB.b()])
        else:
            op("act", lambda e: e.activation(out=LG[:], in_=P.V("rtd", 0, 4), func=AF.Sigmoid), [vec.b()], [LG.b()])
            op("act", lambda e: e.activation(out=LG[:], in_=LG[:], func=AF.Ln), [LG.b()], [LG.b()])
        CB += [LB.b(), OMLB.b(), LG.b()]

        def rec_feature_chunk(ht, n, c0, fc, d, want_q, wk, wq):
            if even:
                w, wv = wk
                lin_fm(PS[0], n, wv, fc * 128, 128, ht, ht.t, 8, extra_reads=[w.b()])
                sg = TMP[0]
                op("act", lambda e: e.activation(out=sg[:, 0:n], in_=PS[0][:, 0:n], func=AF.Sigmoid), [PS[0].b()], [sg.b()])
                col = d * 4 + fc
                op("dve", lambda e: e.tensor_scalar(out=sg[:, 0:n], in0=sg[:, 0:n], scalar1=OMLB[:, col:col + 1], scalar2=LB[:, col:col + 1], op0=ALU.mult, op1=ALU.add),
                   [sg.b(), LB.b(), OMLB.b()], [sg.b()])
                op("pool", lambda e: e.tensor_scalar(out=KF[:, 0:n], in0=sg[:, 0:n], scalar1=-1.0, scalar2=1.0, op0=ALU.mult, op1=ALU.add), [sg.b()], [KF.b()])
                op("act", lambda e: e.activation(out=LF[:, 0:n], in_=sg[:, 0:n], func=AF.Ln), [sg.b()], [LF.b()])
                if want_q:
                    w, wv = wq
                    lin_fm(PS[1], n, wv, fc * 128, 128, ht, ht.t, 8, extra_reads=[w.b()])
                    op("act", lambda e: e.activation(out=QF[:, 0:n], in_=PS[1][:, 0:n], func=AF.Silu), [PS[1].b()], [QF.b()])
            else:
                w, wv = wk
                lin_fm(PS[0], n, wv, fc * 128, 128, ht, ht.t, 8, extra_reads=[w.b()])
                kx = TMP[0]
                op("act", lambda e: e.activation(out=kx[:, 0:n], in_=PS[0][:, 0:n], func=AF.Copy, scale=0.125), [PS[0].b()], [kx.b()])
                rope(kx, kx[:, 0:n], 128, c0, n, KF[:, 0:n], KF, PS[1])
                op("pool", lambda e: e.memset(LF[:, 0:n], 1.0), [], [LF.b()])
                col = d * 2 + fc
                op("dve", lambda e: e.tensor_scalar(out=LF[:, 0:n], in0=LF[:, 0:n], scalar1=LG[:, col:col + 1], scalar2=None, op0=ALU.mult), [LF.b(), LG.b()], [LF.b()])
                if want_q:
                    w, wv = wq
                    lin_fm(PS[0], n, wv, fc * 128, 128, ht, ht.t, 8, extra_reads=[w.b()])
                    qx = TMP[1]
                    op("act", lambda e: e.copy(out=qx[:, 0:n], in_=PS[0][:, 0:n]), [PS[0].b()], [qx.b()])
                    rope(qx, qx[:, 0:n], 128, c0, n, QF[:, 0:n], QF, PS[1])

        NFC = 4 if even else 2
        VCOL = 1536 if even else 2048

        def heads_of(fc):
            return [(fc, 0)] if even else [(2 * fc, 0), (2 * fc + 1, 64)]

        def rec_tile(ht, ti, d, want_out):
            c0, n, _ = TILES[ti]
            v_token_major(ht, n, (VCOL,))
            if even:
                wk = load_w(win_bf, "win", 0, D, 512 + d * 512, 512)
                wq = load_w(win_bf, "win", 0, D, 0, 512) if want_out else None
            else:
                wk = load_w(win_bf, "win", 0, D, 1792, 256)
                wq = load_w(win_bf, "win", 0, D, 1536, 256) if want_out else None
            for fc in range(NFC):
                if DBG == 91: break
                rec_feature_chunk(ht, n, c0, fc, d, want_out, wk, wq)
                if DBG == 94: continue
                decay_factors(n, d)
                if DBG == 92: continue
                if phase == "A":
                    col = d * 4 + fc
                    tot = Bc[:, n - 1:n]
                    op("dve", lambda e, col=col, tot=tot: e.tensor_tensor(out=LTOT[:, col:col + 1], in0=LTOT[:, col:col + 1], in1=tot, op=ALU.add),
                       [Bc.b(), LTOT.b()], [LTOT.b()])
                rec_chunk_tile(heads_of(fc), c0, n, d, want_out)

        if phase == "A":
            KNs = P.sb("KNs", [128, 4, 512], BF16); KRs = P.sb("KRs", [128, 4, 512], BF16)
            CKV = P.sb("CKV", [128, 2, 512]); CKVN = P.sb("CKVN", [128, 2, 512], BF16)
            KRP = P.sb("KRP", [128, 512]); KRR = P.sb("KRR", [128, 512]); KNF = P.sb("KNF", [128, 512])
            VS = P.sb("VS", [128, 4, 512], BF16)
            op("pool", lambda e: e.memset(LTOT[:], 0.0), [], [LTOT.b()])

            def kv_even(ht, ti):
                c0, n, _ = TILES[ti]
                w, wv = load_w(win_bf, "win", 0, D, 2944, 320)
                for cc in range(2):
                    lin_fm(PS[0], n, wv, cc * 128, 128, ht, ht.t, 8, extra_reads=[w.b()])
                    op("act", lambda e, cc=cc: e.copy(out=CKV[:, cc, 0:n], in_=PS[0][:, 0:n]), [PS[0].b()], [CKV.b()])
                    op("pool", lambda e, cc=cc: e.tensor_tensor(out=SQ[:, cc, 0:n], in0=CKV[:, cc, 0:n], in1=CKV[:, cc, 0:n], op=ALU.mult), [CKV.b()], [SQ.b()])
                if DBG == 31: return
                lin_fm(PS[1], n, wv, 256, 64, ht, ht.t, 8, extra_reads=[w.b()])
                if DBG == 311: return
                raw = TMP[1]
                op("act", lambda e: e.copy(out=raw[0:64, 0:n], in_=PS[1][0:64, 0:n]), [PS[1].b()], [raw.b()])
                op("dve", lambda e: e.tensor_scalar(out=KRP[0:64, 0:n], in0=raw[0:64, 0:n], scalar1=P.V("qkk_r", rows=64), scalar2=None, op0=ALU.mult),
                   [raw.b(), vec.b()], [KRP.b()])
                if DBG == 312: return
                op("pool", lambda e: e.tensor_tensor(out=SQ[0:64, 2, 0:n], in0=raw[0:64, 0:n], in1=raw[0:64, 0:n], op=ALU.mult), [raw.b()], [SQ.b()])
                if DBG == 32: return
                for cc in range(2):
                    op("pe", lambda e, cc=cc: e.matmul(PS[6][:, 0:n], lhsT=ONESb, rhs=SQ[:, cc, 0:n], start=(cc == 0), stop=(cc == 1)), [SQ.b(), cmb.b()], [PS[6].b()])
                rstd_from(PS[6], n, 1.0 / 256, RSTD)
                for cc in range(2):
                    op("dve", lambda e, cc=cc: e.scalar_tensor_tensor(out=CKVN[:, cc, 0:n], in0=CKV[:, cc, 0:n], scalar=P.V("kvn", cc), in1=RSTD[:, 0:n], op0=ALU.mult, op1=ALU.mult),
                       [CKV.b(), RSTD.b(), vec.b()], [CKVN.b()])
                if DBG == 33: return
                rope(KRP, KRP[0:64, 0:n], 64, c0, n, KRR[0:64, 0:n], KRR, PS[1])
                if DBG == 34: return
                w2, w2v = load_w(wukv_bf, "wukv", 0, 256, 0, 1024)
                for sti in range(n // 128):
                    for hh in range(4):
                        for kc in range(2):
                            op("pe", lambda e, kc=kc, sti=sti, hh=hh: e.matmul(PS[2][:, hh * 128:(hh + 1) * 128], lhsT=CKVN[:, kc, sti * 128:(sti + 1) * 128],
                                                                       rhs=w2v[:, kc, hh * 256 + 128:hh * 256 + 256],
                                                                       start=(kc == 0), stop=(kc == 1)), [CKVN.b(), w2.b()], [PS[2].b()])
                    op("act", lambda e, sti=sti: e.copy(out=VS[:, sti, :], in_=PS[2][:, 0:512]), [PS[2].b()], [VS.b()])
                    tb = (c0 + sti * 128) // 128
                    dma(v_out[tb], VS[:, sti, :], reads=[VS.b()], writes=[v_outb])
                if DBG == 35: return
                for h in range(4):
                    lin_fm(PS[0], n, w2v, h * 256, 128, CKVN, CKVN.t, 2, extra_reads=[w2.b()])
                    op("act", lambda e: e.copy(out=KNF[:, 0:n], in_=PS[0][:, 0:n]), [PS[0].b()], [KNF.b()])
                    op("pool", lambda e: e.tensor_tensor(out=SQ[:, 3, 0:n], in0=KNF[:, 0:n], in1=KNF[:, 0:n], op=ALU.mult), [KNF.b()], [SQ.b()])
                    op("pe", lambda e: e.matmul(PS[6][:, 0:n], lhsT=ONESb, rhs=SQ[:, 3, 0:n], start=True, stop=False), [SQ.b(), cmb.b()], [PS[6].b()])
                    op("pe", lambda e: e.matmul(PS[6][:, 0:n], lhsT=ONESb[0:64, :], rhs=SQ[0:64, 2, 0:n], start=False, stop=True), [SQ.b(), cmb.b()], [PS[6].b()])
                    rstd_from(PS[6], n, 1.0 / 192, RSTD)
                    op("dve", lambda e, h=h: e.scalar_tensor_tensor(out=KNs[:, h, 0:n], in0=KNF[:, 0:n], scalar=P.V("qkk_n"), in1=RSTD[:, 0:n], op0=ALU.mult, op1=ALU.mult),
                       [KNF.b(), RSTD.b(), vec.b()], [KNs.b()])
                    op("dve", lambda e, h=h: e.tensor_tensor(out=KRs[0:64, h, 0:n], in0=KRR[0:64, 0:n], in1=RSTD[0:64, 0:n], op=ALU.mult), [KRR.b(), RSTD.b()], [KRs.b()])
                for h in range(4):
                    dma(kn_out[h, :, c0:c0 + n], KNs[:, h, 0:n], reads=[KNs.b()], writes=[v_outb])
                    dma(kr_out[h, :, c0:c0 + n], KRs[0:64, h, 0:n], reads=[KRs.b()], writes=[v_outb])

            def kv_odd(ht, ti):
                c0, n, _ = TILES[ti]
                w, wv = load_w(win_bf, "win", 0, D, 512, 512)
                for h in range(4):
                    lin_fm(PS[0], n, wv, h * 128, 128, ht, ht.t, 8, extra_reads=[w.b()])
                    op("act", lambda e: e.copy(out=KNF[:, 0:n], in_=PS[0][:, 0:n]), [PS[0].b()], [KNF.b()])
                    op("pool", lambda e: e.tensor_tensor(out=SQ[:, 3, 0:n], in0=KNF[:, 0:n], in1=KNF[:, 0:n], op=ALU.mult), [KNF.b()], [SQ.b()])
                    op("pe", lambda e: e.matmul(PS[6][:, 0:n], lhsT=BDb, rhs=SQ[:, 3, 0:n], start=True, stop=True), [SQ.b(), cmb.b()], [PS[6].b()])
                    rstd_from(PS[6], n, 1.0 / 64, RSTD)
                    op("dve", lambda e: e.scalar_tensor_tensor(out=KRP[:, 0:n], in0=KNF[:, 0:n], scalar=P.V("dak"), in1=RSTD[:, 0:n], op0=ALU.mult, op1=ALU.mult),
                       [KNF.b(), RSTD.b(), vec.b()], [KRP.b()])
                    rope(KRP, KRP[:, 0:n], 128, c0, n, KNs[:, h, 0:n], KNs, PS[1])
                for h in range(4):
                    dma(kn_out[h, :, c0:c0 + n], KNs[:, h, 0:n], reads=[KNs.b()], writes=[v_outb])
                w, wv = load_w(win_bf, "win", 0, D, 1024, 512)
                for sti in range(n // 128):
                    for kc in range(8):
                        op("pe", lambda e, kc=kc, sti=sti: e.matmul(PS[2][:, 0:512], lhsT=ht[:, kc, sti * 128:(sti + 1) * 128], rhs=wv[:, kc, :],
                                                                   start=(kc == 0), stop=(kc == 7)), [ht.b(), w.b()], [PS[2].b()])
                    op("act", lambda e, sti=sti: e.copy(out=VS[:, sti, :], in_=PS[2][:, 0:512]), [PS[2].b()], [VS.b()])
                    tb = (c0 + sti * 128) // 128
                    dma(v_out[tb], VS[:, sti, :], reads=[VS.b()], writes=[v_outb])

            v_outb = Buf()
            zero_states()
            for ti in range(5 if DBG != 1 else 0):
                xt = XT[ti % 2]
                load_x(ti, xt)
                c0, n, isc = TILES[ti]
                norm_mod(xt, n, isc, A1, 0, HT)
                if DBG != 2:
                    (kv_even if even else kv_odd)(HT, ti)
                if not isc and DBG not in (2, 3, 31, 32, 33, 34, 35, 311, 312):
                    rec_tile(HT, ti, 0, False)
            for h in range(4):
                sp = S[h][scur[h]]
                dma(sums_out[0, h], sp[:], reads=[sp.b()], writes=[v_outb])
            zero_states()
            for ti in ((4, 3, 2, 1) if DBG in (0, 9, 91, 92, 93, 94, 95, 96) else ()):
                xt = XT[ti % 2]
                load_x(ti, xt)
                c0, n, isc = TILES[ti]
                norm_mod(xt, n, isc, A1, 0, HT)
                rec_tile(HT, ti, 1, False)
            for h in range(4):
                sp = S[h][scur[h]]
                dma(sums_out[1, h], sp[:], reads=[sp.b()], writes=[v_outb])
            op("act", lambda e: e.activation(out=LTOT[:], in_=LTOT[:], func=AF.Exp), [LTOT.b()], [LTOT.b()])
            dma(sumd_out, LTOT[:], reads=[LTOT.b()], writes=[v_outb])
            k.finish([v_outb])
            P.n_instr = k.n
            return P

        xob = Buf()
        SJ = [P.sb(f"SJ{i}", [128, 128]) for i in range(2)]
        SD = P.sb("SD", [128, NCORES * 8]); MJ = P.sb("MJ", [128, 1])
        dma(SD[:], sumd_in, writes=[SD.b()])
        GT = P.sb("GT", [128, 4, 512], BF16)
        YT = P.sb("YT", [128, 8, 512], BF16)
        QN = P.sb("QN", [128, 4, 512], BF16); QR = P.sb("QR", [128, 4, 512], BF16)
        KNb = [P.sb(f"KNb{i}", [128, 1280], BF16) for i in range(2)]
        KRb = [P.sb(f"KRb{i}", [128, 1280], BF16) for i in range(2)]
        Vb = [P.sb(f"Vb{i}", [128, 10, 128], BF16) for i in range(2)]
        PT = [P.sb(f"PT{i}", [128, 512], BF16) for i in range(2)]
        AO = [P.sb(f"AO{i}", [128, 512]) for i in range(2)]
        HID = [P.sb("HID0", [128, 4, 512], BF16)] * 2
        CQ = P.sb("CQ", [128, 3, 512]); CQN = P.sb("CQN", [128, 3, 512], BF16)
        QX = P.sb("QX", [128, 512]); QY = P.sb("QY", [128, 512])
        NLAM = P.sb("NLAM", [128, 1])
        sbn = [0]

        def fold(d):
            js = range(NCORES) if d == 0 else range(NCORES - 1, -1, -1)
            fl, nfl = ("flf", "nflf") if d == 0 else ("flb", "nflb")
            for j in js:
                for h in range(4):
                    sj = SJ[sbn[0] % 2]
                    sbn[0] += 1
                    dma(sj[:], sums_in[j, d, h], writes=[sj.b()])
                    fc, pb = (h, 0) if even else (h // 2, 64 * (h % 2))
                    rows = slice(pb, pb + KD)
                    col = j * 8 + d * 4 + fc
                    op("dve", lambda e, col=col, j=j: e.tensor_scalar(out=MJ[:], in0=SD[:, col:col + 1], scalar1=P.V(fl, j), scalar2=P.V(nfl, j), op0=ALU.mult, op1=ALU.add),
                       [SD.b(), vec.b()], [MJ.b()])
                    op("pool", lambda e, sj=sj, j=j: e.tensor_scalar(out=sj[:], in0=sj[:], scalar1=P.V(fl, j), scalar2=None, op0=ALU.mult), [sj.b(), vec.b()], [sj.b()])
                    sp = S[h][scur[h]]; sn = S[h][1 - scur[h]]
                    op("dve", lambda e, sp=sp, sn=sn, sj=sj, rows=rows: e.scalar_tensor_tensor(out=sn[rows, :], in0=sp[rows, :], scalar=MJ[rows, 0:1], in1=sj[rows, :],
                                                                                          op0=ALU.mult, op1=ALU.add), [sp.b(), sj.b(), MJ.b()], [sn.b()])
                    scur[h] = 1 - scur[h]

        def attention(parts, h, n, out_t, kbs, scale):
            nsb = (kbs + 9) // 10
            sbinfo = []

            def load_sb(sbi):
                kb0 = sbi * 10
                nk = min(10, kbs - kb0)
                bi = sbn[0] % 2
                sbn[0] += 1
                for (qv, ksrc, r0, nr, kbuf) in parts:
                    dma(kbuf[bi][r0:r0 + nr, 0:nk * 128], ksrc[:, kb0 * 128:(kb0 + nk) * 128], writes=[kbuf[bi].b()])
                dma(Vb[bi][:, 0:nk, :], v_in[h, :, kb0:kb0 + nk, :], writes=[Vb[bi].b()])
                sbinfo.append(bi)

            blocks = [(sbi, kb) for sbi in range(nsb) for kb in range(min(10, kbs - sbi * 10))]
            nb = len(blocks)

            def qk(i):
                sbi, kb = blocks[i]
                bi = sbinfo[sbi]
                pss = PS[i % 2]
                for pi, (qv, ksrc, r0, nr, kbuf) in enumerate(parts):
                    op("pe", lambda e, qv=qv, r0=r0, nr=nr, kbuf=kbuf, kb=kb, pi=pi, pss=pss, bi=bi: e.matmul(
                        pss[:, 0:n], lhsT=kbuf[bi][r0:r0 + nr, kb * 128:(kb + 1) * 128], rhs=qv, start=(pi == 0), stop=(pi == len(parts) - 1)),
                       [kbuf[bi].b(), QN.b(), QR.b()], [pss.b()])

            load_sb(0)
            if nsb > 1:
                load_sb(1)
            qk(0)
            for i, (sbi, kb) in enumerate(blocks):
                if kb == 0 and sbi >= 1 and sbi + 1 < nsb:
                    load_sb(sbi + 1)
                if i + 1 < nb:
                    qk(i + 1)
                bi = sbinfo[sbi]
                pss = PS[i % 2]
                pt = PT[i % 2]
                op("act", lambda e, pss=pss, pt=pt: e.activation(out=pt[:, 0:n], in_=pss[:, 0:n], func=AF.Exp, scale=scale), [pss.b()], [pt.b()])
                first = (i == 0)
                last = (i == nb - 1)
                op("pe", lambda e, kb=kb, pt=pt, last=last, first=first, bi=bi: e.matmul(PS[2][:, 0:n], lhsT=Vb[bi][:, kb, :], rhs=pt[:, 0:n], start=first, stop=last),
                   [Vb[bi].b(), pt.b()], [PS[2].b()])
                acc = mrow[i % 2]
                eng = "dve" if i % 2 == 0 else "pool"
                if i < 2:
                    op(eng, lambda e, acc=acc, pt=pt: e.tensor_copy(out=acc[:, 0:n], in_=pt[:, 0:n]), [pt.b()], [acc.b()])
                else:
                    op(eng, lambda e, acc=acc, pt=pt: e.tensor_tensor(out=acc[:, 0:n], in0=acc[:, 0:n], in1=pt[:, 0:n], op=ALU.add), [pt.b(), acc.b()], [acc.b()])
            nacc = min(2, nb)
            for ai in range(nacc):
                op("pe", lambda e, ai=ai: e.matmul(PS[3][:, 0:n], lhsT=ONESf, rhs=mrow[ai][:, 0:n], start=(ai == 0), stop=(ai == nacc - 1)),
                   [mrow[ai].b(), cmf.b()], [PS[3].b()])
            rc = TMP[0]
            op("dve", lambda e: e.reciprocal(out=rc[:, 0:n], in_=PS[3][:, 0:n]), [PS[3].b()], [rc.b()])
            op("dve", lambda e: e.tensor_tensor(out=out_t[:, 0:n], in0=PS[2][:, 0:n], in1=rc[:, 0:n], op=ALU.mult), [PS[2].b(), rc.b()], [out_t.b()])

        def head_rms_gate(src_view, n, gain_ap, gate_view, out_view, src_b, out_b, lhs_ones, inv_dim, extra_scale=None):
            sqv = SQ[:, 4, 0:n]
            op("pool", lambda e: e.tensor_tensor(out=sqv, in0=src_view, in1=src_view, op=ALU.mult), [src_b], [SQ.b()])
            op("pe", lambda e: e.matmul(PS[6][:, 0:n], lhsT=lhs_ones, rhs=sqv, start=True, stop=True), [SQ.b(), cmb.b()], [PS[6].b()])
            rstd_from(PS[6], n, inv_dim, RSTD)
            t = TMP[1]
            op("dve", lambda e: e.scalar_tensor_tensor(out=t[:, 0:n], in0=src_view, scalar=gain_ap, in1=RSTD[:, 0:n], op0=ALU.mult, op1=ALU.mult),
               [src_b, RSTD.b(), vec.b()], [t.b()])
            if gate_view is not None:
                op("dve", lambda e: e.tensor_tensor(out=out_view, in0=t[:, 0:n], in1=gate_view, op=ALU.mult), [t.b(), GT.b()], [out_b])
            else:
                op("dve", lambda e: e.tensor_scalar(out=out_view, in0=t[:, 0:n], scalar1=extra_scale, scalar2=None, op0=ALU.mult), [t.b(), vec.b()], [out_b])

        if not even:
            lp = P.sb("lp", [128, 2])
            op("dve", lambda e: e.tensor_tensor(out=lp[:, 0:1], in0=P.V("lam", 0), in1=P.V("lam", 1), op=ALU.mult), [vec.b()], [lp.b()])
            op("dve", lambda e: e.tensor_tensor(out=lp[:, 1:2], in0=P.V("lam", 2), in1=P.V("lam", 3), op=ALU.mult), [vec.b()], [lp.b()])
            op("pe", lambda e: e.matmul(PS[6][:, 0:2], lhsT=ONESf, rhs=lp[:], start=True, stop=True), [lp.b(), cmf.b()], [PS[6].b()])
            op("act", lambda e: e.activation(out=lp[:], in_=PS[6][:, 0:2], func=AF.Exp), [PS[6].b()], [lp.b()])
            op("dve", lambda e: e.tensor_tensor(out=NLAM[:], in0=lp[:, 1:2], in1=lp[:, 0:1], op=ALU.subtract), [lp.b()], [NLAM.b()])
            op("dve", lambda e: e.tensor_tensor(out=NLAM[:], in0=NLAM[:], in1=P.V("laminit"), op=ALU.subtract), [NLAM.b(), vec.b()], [NLAM.b()])

        def mixer_attn_even(ht, ti):
            c0, n, isc = TILES[ti]
            kbs = 2 if isc else NKB
            w, wv = load_w(win_bf, "win", 0, D, 2560, 384)
            for cc in range(3):
                lin_fm(PS[0], n, wv, cc * 128, 128, ht, ht.t, 8, extra_reads=[w.b()])
                op("act", lambda e, cc=cc: e.copy(out=CQ[:, cc, 0:n], in_=PS[0][:, 0:n]), [PS[0].b()], [CQ.b()])
                op("pool", lambda e, cc=cc: e.tensor_tensor(out=SQ[:, cc, 0:n], in0=CQ[:, cc, 0:n], in1=CQ[:, cc, 0:n], op=ALU.mult), [CQ.b()], [SQ.b()])
            for cc in range(3):
                op("pe", lambda e, cc=cc: e.matmul(PS[6][:, 0:n], lhsT=ONESb, rhs=SQ[:, cc, 0:n], start=(cc == 0), stop=(cc == 2)), [SQ.b(), cmb.b()], [PS[6].b()])
            rstd_from(PS[6], n, 1.0 / 384, RSTD)
            for cc in range(3):
                op("dve", lambda e, cc=cc: e.scalar_tensor_tensor(out=CQN[:, cc, 0:n], in0=CQ[:, cc, 0:n], scalar=P.V("qn", cc), in1=RSTD[:, 0:n], op0=ALU.mult, op1=ALU.mult),
                   [CQ.b(), RSTD.b(), vec.b()], [CQN.b()])
            w2, w2v = load_w(wuq_bf, "wuq", 0, 384, 0, 768)
            for h in range(4):
                lin_fm(PS[0], n, w2v, h * 192, 128, CQN, CQN.t, 3, extra_reads=[w2.b()])
                lin_fm(PS[1], n, w2v, h * 192 + 128, 64, CQN, CQN.t, 3, extra_reads=[w2.b()])
                op("act", lambda e: e.copy(out=QX[:, 0:n], in_=PS[0][:, 0:n]), [PS[0].b()], [QX.b()])
                op("act", lambda e: e.copy(out=QY[0:64, 0:n], in_=PS[1][0:64, 0:n]), [PS[1].b()], [QY.b()])
                op("pool", lambda e: e.tensor_tensor(out=SQ[:, 3, 0:n], in0=QX[:, 0:n], in1=QX[:, 0:n], op=ALU.mult), [QX.b()], [SQ.b()])
                op("pool", lambda e: e.tensor_tensor(out=SQ[0:64, 2, 0:n], in0=QY[0:64, 0:n], in1=QY[0:64, 0:n], op=ALU.mult), [QY.b()], [SQ.b()])
                op("pe", lambda e: e.matmul(PS[6][:, 0:n], lhsT=ONESb, rhs=SQ[:, 3, 0:n], start=True, stop=False), [SQ.b(), cmb.b()], [PS[6].b()])
                op("pe", lambda e: e.matmul(PS[6][:, 0:n], lhsT=ONESb[0:64, :], rhs=SQ[0:64, 2, 0:n], start=False, stop=True), [SQ.b(), cmb.b()], [PS[6].b()])
                rstd_from(PS[6], n, 1.0 / 192, RSTD)
                op("dve", lambda e, h=h: e.scalar_tensor_tensor(out=QN[:, h, 0:n], in0=QX[:, 0:n], scalar=P.V("qkq_n"), in1=RSTD[:, 0:n], op0=ALU.mult, op1=ALU.mult),
                   [QX.b(), RSTD.b(), vec.b()], [QN.b()])
                op("dve", lambda e: e.scalar_tensor_tensor(out=QY[0:64, 0:n], in0=QY[0:64, 0:n], scalar=P.V("qkq_r", rows=64), in1=RSTD[0:64, 0:n], op0=ALU.mult, op1=ALU.mult),
                   [QY.b(), RSTD.b(), vec.b()], [QY.b()])
                rope(QY, QY[0:64, 0:n], 64, c0, n, QR[0:64, h, 0:n], QR, PS[1])
            for h in range(4):
                ao = AO[h % 2]
                attention([(QN[:, h, 0:n], kn_in[h], 0, 128, KNb), (QR[0:64, h, 0:n], kr_in[h], 0, 64, KRb)], h, n, ao, kbs, 192 ** -0.5)
                op("act", lambda e, h=h, ao=ao: e.copy(out=YT[:, 4 + h, 0:n], in_=ao[:, 0:n]), [ao.b()], [YT.b()])

        def mixer_attn_odd(ht, ti):
            c0, n, isc = TILES[ti]
            kbs = 2 if isc else NKB
            w, wv = load_w(win_bf, "win", 0, D, 0, 512)
            for h in range(4):
                lin_fm(PS[0], n, wv, h * 128, 128, ht, ht.t, 8, extra_reads=[w.b()])
                op("act", lambda e: e.copy(out=QX[:, 0:n], in_=PS[0][:, 0:n]), [PS[0].b()], [QX.b()])
                op("pool", lambda e: e.tensor_tensor(out=SQ[:, 3, 0:n], in0=QX[:, 0:n], in1=QX[:, 0:n], op=ALU.mult), [QX.b()], [SQ.b()])
                op("pe", lambda e: e.matmul(PS[6][:, 0:n], lhsT=BDb, rhs=SQ[:, 3, 0:n], start=True, stop=True), [SQ.b(), cmb.b()], [PS[6].b()])
                rstd_from(PS[6], n, 1.0 / 64, RSTD)
                op("dve", lambda e: e.scalar_tensor_tensor(out=QY[:, 0:n], in0=QX[:, 0:n], scalar=P.V("daq"), in1=RSTD[:, 0:n], op0=ALU.mult, op1=ALU.mult),
                   [QX.b(), RSTD.b(), vec.b()], [QY.b()])
                rope(QY, QY[:, 0:n], 128, c0, n, QN[:, h, 0:n], QN, PS[1])
            for h in range(4):
                attention([(QN[0:64, h, 0:n], kn_in[h, 0:64], 0, 64, KNb)], h, n, AO[0], kbs, 0.125)
                attention([(QN[64:128, h, 0:n], kn_in[h, 64:128], 64, 64, KNb)], h, n, AO[1], kbs, 0.125)
                op("dve", lambda e: e.scalar_tensor_tensor(out=AO[0][:, 0:n], in0=AO[1][:, 0:n], scalar=NLAM[:, 0:1], in1=AO[0][:, 0:n], op0=ALU.mult, op1=ALU.add),
                   [AO[0].b(), AO[1].b(), NLAM.b()], [AO[0].b()])
                head_rms_gate(AO[0][:, 0:n], n, P.V("subln"), None, YT[:, h, 0:n], AO[0].b(), YT.b(), ONESb, 1.0 / 128, extra_scale=P.V("omlam"))

        def gate(ht, n):
            gc0 = 2048 if even else 2560
            w, wv = load_w(win_bf, "win", 0, D, gc0, 512)
            for cc in range(4):
                lin_fm(PS[0], n, wv, cc * 128, 128, ht, ht.t, 8, extra_reads=[w.b()])
                op("act", lambda e, cc=cc: e.activation(out=GT[:, cc, 0:n], in_=PS[0][:, 0:n], func=AF.Silu), [PS[0].b()], [GT.b()])

        def finish_tile(xt, ti):
            c0, n, isc = TILES[ti]
            yoff = 0 if even else 4
            for h in range(4):
                src = OACC[:, h, c0:c0 + n]
                op("act", lambda e, src=src: e.copy(out=QX[:, 0:n], in_=src), [OACC.b((h, c0))], [QX.b()])
                head_rms_gate(QX[:, 0:n], n, P.V("hgn" if even else "rtn"), GT[:, h, 0:n], YT[:, yoff + h, 0:n], QX.b(), YT.b(), ONESb, 1.0 / 128)
            for oc in range(8):
                if oc % 4 == 0:
                    w, wv = load_w(wo_bf, "wo", 0, D, oc * 128, 512)
                lin_fm(PS[oc % 2], n, wv, (oc % 4) * 128, 128, YT, YT.t, 8, extra_reads=[w.b()])
                op("dve", lambda e, oc=oc: e.scalar_tensor_tensor(out=xt[:, oc, 0:n], in0=PS[oc % 2][:, 0:n], scalar=MOD[:, 16 + oc, isc:isc + 1], in1=xt[:, oc, 0:n],
                                                                 op0=ALU.mult, op1=ALU.add), [PS[oc % 2].b(), MOD.b(), xt.b()], [xt.b()])
            norm_mod(xt, n, isc, A2, 24, HT)
            for hb in range(8):
                w1, w1v = load_w(w1_bf, "w1", 0, D, hb * 512, 512)
                w2, w2v = load_w(w2_bf, "w2", hb * 512, 512, 0, 1024)
                hid = HID[hb % 2]
                for hc in range(4):
                    lin_fm(PS[hc % 2], n, w1v, hc * 128, 128, HT, HT.t, 8, extra_reads=[w1.b()])
                    r = TMP[hc % 2]
                    op("act", lambda e, hc=hc, r=r: e.activation(out=r[:, 0:n], in_=PS[hc % 2][:, 0:n], func=AF.Relu), [PS[hc % 2].b()], [r.b()])
                    op("dve", lambda e, hc=hc, r=r: e.tensor_tensor(out=hid[:, hc, 0:n], in0=PS[hc % 2][:, 0:n], in1=r[:, 0:n], op=ALU.mult), [PS[hc % 2].b(), r.b()], [hid.b()])
                for oc in range(8):
                    pst = PS[2 + oc % 2]
                    lin_fm(pst, n, w2v, oc * 128, 128, hid, hid.t, 4, extra_reads=[w2.b()])
                    op("dve", lambda e, oc=oc, pst=pst: e.scalar_tensor_tensor(out=xt[:, oc, 0:n], in0=pst[:, 0:n], scalar=MOD[:, 40 + oc, isc:isc + 1], in1=xt[:, oc, 0:n],
                                                                             op0=ALU.mult, op1=ALU.add), [pst.b(), MOD.b(), xt.b()], [xt.b()])
            dma(xT_out[:, c0:c0 + n].rearrange("(kc p) t -> p kc t", p=128), xt[:, :, 0:n], reads=[xt.b()], writes=[xob])

        zero_states()
        for ti in range(5):
            xt = XT[ti % 2]
            load_x(ti, xt)
            c0, n, isc = TILES[ti]
            norm_mod(xt, n, isc, A1, 0, HT)
            if ti == 1:
                fold(0)
            rec_tile(HT, ti, 0, True)
        zero_states()
        for ti in (0, 4, 3, 2, 1):
            xt = XT[ti % 2]
            load_x(ti, xt)
            c0, n, isc = TILES[ti]
            norm_mod(xt, n, isc, A1, 0, HT)
            if ti == 4:
                fold(1)
            rec_tile(HT, ti, 1, True)
            gate(HT, n)
            (mixer_attn_even if even else mixer_attn_odd)(HT, ti)
            finish_tile(xt, ti)
        k.finish([xob])
        P.n_instr = k.n
    return P


_PROGS = {}


def get_prog(even, phase):
    key = (even, phase)
    if key not in _PROGS:
        _PROGS[key] = build(even, phase)
    return _PROGS[key]


def run_layer(inp, l, xT_sh, consts):
    even = l % 2 == 0
    j = l // 2
    f32 = lambda a: np.ascontiguousarray(np.asarray(a, np.float32))
    w_in = f32(inp["a_w_in"][j] if even else inp["c_w_in"][j])
    ada = f32(inp["ada_w"][l])
    vecs = [build_vecs(inp, l, c) for c in range(NCORES)]
    base = []
    for c in range(NCORES):
        m = {"xT": xT_sh[c], "vecs": vecs[c], "cos": consts[c]["cos"], "sin": consts[c]["sin"], "cmat": consts[c]["cmat"],
             "ada_w": ada, "w_in": w_in}
        if even:
            m["w_ukv"] = f32(inp["mla_w_ukv"][j])
        base.append(m)
    pa = get_prog(even, "A")
    ra = run_bass_kernel_spmd(pa.nc, base, core_ids=list(range(NCORES))).results
    cat = lambda name, ax, sl_ctx, sl_lat: np.ascontiguousarray(np.concatenate([sl_ctx(ra[0][name])] + [sl_lat(ra[c][name]) for c in range(NCORES)], axis=ax))
    kn_all = cat("kn", 2, lambda a: a[:, :, :NCTX], lambda a: a[:, :, NCTX:])
    vtm = np.concatenate([ra[0]["v"][:NCTX // 128]] + [ra[c]["v"][NCTX // 128:] for c in range(NCORES)], axis=0)
    v_all = np.ascontiguousarray(vtm.reshape(NKB, 128, 4, 128).transpose(2, 1, 0, 3))
    sumS = np.ascontiguousarray(np.stack([ra[c]["sumS"] for c in range(NCORES)], 0))
    sumD = np.ascontiguousarray(np.concatenate([ra[c]["sumD"] for c in range(NCORES)], 1))
    pb = get_prog(even, "B")
    maps = []
    for c in range(NCORES):
        m = dict(base[c])
        m.update({"kn_all": kn_all, "v_all": v_all, "sumS_all": sumS, "sumD_all": sumD,
                  "w_o": f32(inp["w_o"][l]), "w1": f32(inp["mlp_w1"][l]), "w2": f32(inp["mlp_w2"][l])})
        if even:
            m["kr_all"] = cat("kr", 2, lambda a: a[:, :, :NCTX], lambda a: a[:, :, NCTX:])
            m["w_uq"] = f32(inp["mla_w_uq"][j])
        maps.append(m)
    rb = run_bass_kernel_spmd(pb.nc, maps, core_ids=list(range(NCORES))).results
    return [np.ascontiguousarray(rb[c]["xTo"]) for c in range(NCORES)]


def kernel(**inp):
    x = np.asarray(inp["x"], np.float32)[0]
    ctx = np.asarray(inp["ctx"], np.float32)[0]
    consts = [const_tables(c) for c in range(NCORES)]
    xT_sh = [np.ascontiguousarray(np.concatenate([ctx.T, x[c * LPC:(c + 1) * LPC].T], axis=1)) for c in range(NCORES)]
    for l in range(4):
        xT_sh = run_layer(inp, l, xT_sh, consts)
    out = np.concatenate([xT_sh[c][:, NCTX:].T for c in range(NCORES)], axis=0)
    return np.ascontiguousarray(out[None].astype(np.float32))
```
